# Optimizing a Trainium2 kernel written in Bass

```python
import jax, jax.numpy as jnp
from jax import lax
import numpy as np

D_MODEL = 2048
BATCH = 8
SEQ = 2048
DEPTH = 1
DEC_BATCH = 128
DEC_SEQ = 1
PAST_LEN = 2048
PAGE_SIZE = 128

HEAD_DIM = 128
HEADS_PER_GROUP = 4
ATTN_GROUPS = ((128, 1), (512, 4), (2048, 16))
N_HEADS = HEADS_PER_GROUP * len(ATTN_GROUPS)
D_ATTN = N_HEADS * HEAD_DIM
D_ATTN_OUT = HEADS_PER_GROUP * HEAD_DIM
BAND = 128
QBLOCK = 128
POOL_WINDOWS = (2, 4, 8, 16)
POOL_GROUP = 128
D_POOL = POOL_GROUP * len(POOL_WINDOWS)
POOL_HIST = max(POOL_WINDOWS) - 1
D_FF = 4 * D_MODEL
D_IN = 3 * D_ATTN + D_POOL + 2 * D_MODEL
SPLITS = (D_ATTN, 2 * D_ATTN, 3 * D_ATTN, 3 * D_ATTN + D_POOL, 3 * D_ATTN + D_POOL + D_MODEL)
ALIBI_MAX_BIAS = 8.0
EPS = 1e-6

kernel_name = 'hybrid_dilated_attn_pool_adaln_step'


def rmsnorm(x, g):
    xf = x.astype(jnp.float32)
    y = xf * lax.rsqrt(jnp.mean(xf * xf, axis=-1, keepdims=True) + EPS)
    return (y * g.astype(jnp.float32)).astype(x.dtype)


def alibi_slopes():
    h = jnp.arange(1, N_HEADS + 1, dtype=jnp.float32)
    return jnp.exp2(-ALIBI_MAX_BIAS * h / N_HEADS)


def masked_softmax_attend(s, valid, v, eq):
    s = jnp.where(valid, s, -jnp.inf)
    m = jnp.max(s, axis=-1, keepdims=True)
    p = jnp.exp(s - m)
    den = jnp.sum(p, axis=-1, keepdims=True)
    out = jnp.einsum(eq, p / den, v.astype(jnp.float32))
    return out, (m + jnp.log(den))[..., 0]


def dilated_attn_prompt(q, k, v, dilation, slopes):
    B, S, H, hd = q.shape
    n = S // dilation
    nb = -(-n // QBLOCK)
    n_pad = nb * QBLOCK
    Z = B * dilation

    def to_sub(t):
        return t.reshape(B, n, dilation, H, hd).transpose(0, 2, 1, 3, 4).reshape(Z, n, H, hd)

    def band_keys(t):
        tp = jnp.pad(t, ((0, 0), (BAND, n_pad - n), (0, 0), (0, 0)))
        prev = tp[:, :n_pad].reshape(Z, nb, QBLOCK, H, hd)
        cur = tp[:, BAND:].reshape(Z, nb, QBLOCK, H, hd)
        return jnp.concatenate([prev, cur], axis=2)

    qb = jnp.pad(to_sub(q), ((0, 0), (0, n_pad - n), (0, 0), (0, 0))).reshape(Z, nb, QBLOCK, H, hd)
    kb = band_keys(to_sub(k))
    vb = band_keys(to_sub(v))
    s = jnp.einsum('znqhd,znkhd->znhqk', qb, kb, preferred_element_type=jnp.float32) * (hd ** -0.5)
    a = jnp.arange(QBLOCK)[:, None]
    b = jnp.arange(2 * QBLOCK)[None, :]
    dist = a - b + BAND
    u = (jnp.arange(nb) * QBLOCK)[:, None, None] + a[None]
    valid = (dist >= 0) & (dist <= BAND) & (u - dist >= 0)
    bias = -slopes[:, None, None] * (dist * dilation).astype(jnp.float32)
    out, lse = masked_softmax_attend(s + bias[None, None], valid[None, :, None], vb, 'znhqk,znkhd->znqhd')
    out = out.reshape(Z, n_pad, H, hd)[:, :n]
    lse = lse.transpose(0, 1, 3, 2).reshape(Z, n_pad, H)[:, :n]
    out = out.reshape(B, dilation, n, H, hd).transpose(0, 2, 1, 3, 4).reshape(B, S, H, hd)
    lse = lse.reshape(B, dilation, n, H).transpose(0, 2, 1, 3).reshape(B, S, H)
    return out, lse


def dilated_attn_sample(q, k_new, v_new, hist_kv, dilation, slopes):
    T, hd = q.shape[1], q.shape[-1]
    w_hist = hist_kv.shape[1]
    kvc = jnp.concatenate([hist_kv, jnp.stack([k_new, v_new], axis=2)], axis=1)
    i = jnp.arange(T)[:, None]
    j = jnp.arange(BAND + 1)[None, :]
    idx = w_hist + i - j * dilation
    valid = idx >= 0
    kvg = kvc[:, jnp.maximum(idx, 0)]
    s = jnp.einsum('bthd,btlhd->bthl', q, kvg[:, :, :, 0], preferred_element_type=jnp.float32) * (hd ** -0.5)
    bias = -slopes[:, None] * (j * dilation).astype(jnp.float32)
    out, lse = masked_softmax_attend(s + bias, valid[None, :, None, :], kvg[:, :, :, 1], 'bthl,btlhd->bthd')
    return out, lse, kvc


def pool_mixer(u_ext, pos0, w_pool, pool_scale):
    B, n_ext, _ = u_ext.shape
    n_new = n_ext - POOL_HIST
    cs = jnp.pad(jnp.cumsum(u_ext.astype(jnp.float32), axis=1), ((0, 0), (1, 0), (0, 0)))
    u_new = u_ext[:, POOL_HIST:].astype(jnp.float32)
    pos = pos0 + jnp.arange(n_new)
    outs = []
    for g, w in enumerate(POOL_WINDOWS):
        sl = slice(g * POOL_GROUP, (g + 1) * POOL_GROUP)
        win_sum = cs[:, POOL_HIST + 1:, sl] - cs[:, POOL_HIST + 1 - w:POOL_HIST + 1 - w + n_new, sl]
        count = jnp.minimum(pos + 1, w).astype(jnp.float32)[None, :, None]
        outs.append(win_sum / count - u_new[..., sl])
    z = jnp.stack(outs, axis=2)
    z = jnp.einsum('btgc,gcd->btgd', z, w_pool.astype(jnp.float32))
    return (z.reshape(B, n_new, D_POOL) * pool_scale.astype(jnp.float32)).astype(u_ext.dtype)


def trunk_layer(x, c, kv_hist, u_hist, pos0, norm_mix_g, w_ada, b_ada, w_in, w_up_attn, w_pool, pool_scale,
                w_up_pool, w_out, norm_mlp_g, w_mlp_up, w_mlp_down):
    B, T, _ = x.shape
    mod = jax.nn.silu(c) @ w_ada + b_ada
    sh1, sc1, gt1, sh2, sc2, gt2 = jnp.split(mod[:, None, :], 6, axis=-1)
    h = rmsnorm(x, norm_mix_g) * (1 + sc1) + sh1
    q, k, v, u, ga, gb = jnp.split(h @ w_in, SPLITS, axis=-1)
    q = q.reshape(B, T, N_HEADS, HEAD_DIM)
    k = k.reshape(B, T, N_HEADS, HEAD_DIM)
    v = v.reshape(B, T, N_HEADS, HEAD_DIM)
    slopes = alibi_slopes()
    outs, lses, new_kv = [], [], []
    for g, (window, dil) in enumerate(ATTN_GROUPS):
        hs = slice(g * HEADS_PER_GROUP, (g + 1) * HEADS_PER_GROUP)
        if kv_hist is None:
            o, l = dilated_attn_prompt(q[:, :, hs], k[:, :, hs], v[:, :, hs], dil, slopes[hs])
            keep = min(window, T)
            new_kv.append(jnp.stack([k[:, T - keep:, hs], v[:, T - keep:, hs]], axis=2))
        else:
            o, l, kvc = dilated_attn_sample(q[:, :, hs], k[:, :, hs], v[:, :, hs], kv_hist[g], dil, slopes[hs])
            keep = min(window, kvc.shape[1])
            new_kv.append(kvc[:, kvc.shape[1] - keep:])
        outs.append(o)
        lses.append(l)
    alpha = jax.nn.softmax(jnp.stack(lses, axis=0), axis=0)
    o_attn = jnp.sum(alpha[..., None] * jnp.stack(outs, axis=0), axis=0).reshape(B, T, D_ATTN_OUT).astype(x.dtype)
    if u_hist is None:
        u_hist = jnp.zeros((B, POOL_HIST, D_POOL), u.dtype)
    u_ext = jnp.concatenate([u_hist, u], axis=1)
    p = pool_mixer(u_ext, pos0, w_pool, pool_scale)
    new_u = u_ext[:, u_ext.shape[1] - POOL_HIST:]
    mix = jax.nn.sigmoid(ga) * (o_attn @ w_up_attn) + jax.nn.sigmoid(gb) * (p @ w_up_pool)
    x = x + gt1 * (mix @ w_out)
    h2 = rmsnorm(x, norm_mlp_g) * (1 + sc2) + sh2
    ff = jnp.square(jax.nn.relu(h2 @ w_mlp_up)) @ w_mlp_down
    x = x + gt2 * ff
    return x, new_kv, new_u


def setup_inputs(seed: int = 0) -> dict:
    key = jax.random.key(seed)
    ks = jax.random.split(key, 24)
    f32 = jnp.float32
    nrm = lambda k, shape, s: jax.random.normal(k, shape, f32) * s
    hist = [min(w, PAST_LEN) for w, _ in ATTN_GROUPS]
    return {
        'x_prompt': nrm(ks[0], (BATCH, SEQ, D_MODEL), 1.0),
        'x_sample': nrm(ks[1], (DEC_BATCH, DEC_SEQ, D_MODEL), 1.0),
        'c_prompt': nrm(ks[2], (BATCH, D_MODEL), 1.0),
        'c_sample': nrm(ks[3], (DEC_BATCH, D_MODEL), 1.0),
        'cache_kv_w128': nrm(ks[4], (DEPTH, DEC_BATCH, hist[0], 2, HEADS_PER_GROUP, HEAD_DIM), 1.0),
        'cache_kv_w512': nrm(ks[5], (DEPTH, DEC_BATCH, hist[1], 2, HEADS_PER_GROUP, HEAD_DIM), 1.0),
        'cache_kv_w2048': nrm(ks[6], (DEPTH, DEC_BATCH, hist[2], 2, HEADS_PER_GROUP, HEAD_DIM), 1.0),
        'state_pool': nrm(ks[7], (DEPTH, DEC_BATCH, POOL_HIST, D_POOL), 1.0),
        'norm_mix_g': 1.0 + nrm(ks[8], (DEPTH, D_MODEL), 0.05),
        'w_ada': nrm(ks[9], (DEPTH, D_MODEL, 6 * D_MODEL), D_MODEL ** -0.5),
        'b_ada': nrm(ks[10], (DEPTH, 6 * D_MODEL), 0.01),
        'w_in': nrm(ks[11], (DEPTH, D_MODEL, D_IN), D_MODEL ** -0.5),
        'w_up_attn': nrm(ks[12], (DEPTH, D_ATTN_OUT, D_MODEL), D_ATTN_OUT ** -0.5),
        'w_pool': nrm(ks[13], (DEPTH, len(POOL_WINDOWS), POOL_GROUP, POOL_GROUP), POOL_GROUP ** -0.5),
        'pool_scale': 1.0 + nrm(ks[14], (DEPTH, D_POOL), 0.05),
        'w_up_pool': nrm(ks[15], (DEPTH, D_POOL, D_MODEL), D_POOL ** -0.5),
        'w_out': nrm(ks[16], (DEPTH, D_MODEL, D_MODEL), D_MODEL ** -0.5),
        'norm_mlp_g': 1.0 + nrm(ks[17], (DEPTH, D_MODEL), 0.05),
        'w_mlp_up': nrm(ks[18], (DEPTH, D_MODEL, D_FF), D_MODEL ** -0.5),
        'w_mlp_down': nrm(ks[19], (DEPTH, D_FF, D_MODEL), D_FF ** -0.5),
        'norm_final_g': 1.0 + nrm(ks[20], (D_MODEL,), 0.05),
    }


def reference(x_prompt, x_sample, c_prompt, c_sample, cache_kv_w128, cache_kv_w512, cache_kv_w2048, state_pool,
              norm_mix_g, w_ada, b_ada, w_in, w_up_attn, w_pool, pool_scale, w_up_pool, w_out, norm_mlp_g,
              w_mlp_up, w_mlp_down, norm_final_g):
    xp, xs = x_prompt, x_sample
    kvp = ([], [], [])
    kvs = ([], [], [])
    pool_p, pool_s = [], []
    for l in range(DEPTH):
        params = (norm_mix_g[l], w_ada[l], b_ada[l], w_in[l], w_up_attn[l], w_pool[l], pool_scale[l],
                  w_up_pool[l], w_out[l], norm_mlp_g[l], w_mlp_up[l], w_mlp_down[l])
        xp, nkv_p, nu_p = trunk_layer(xp, c_prompt, None, None, 0, *params)
        hist = (cache_kv_w128[l], cache_kv_w512[l], cache_kv_w2048[l])
        xs, nkv_s, nu_s = trunk_layer(xs, c_sample, hist, state_pool[l], PAST_LEN, *params)
        for g in range(len(ATTN_GROUPS)):
            kvp[g].append(nkv_p[g])
            kvs[g].append(nkv_s[g])
        pool_p.append(nu_p)
        pool_s.append(nu_s)
    y_prompt = rmsnorm(xp, norm_final_g)
    y_sample = rmsnorm(xs, norm_final_g)
    return (y_prompt, y_sample,
            jnp.stack(kvp[0]), jnp.stack(kvp[1]), jnp.stack(kvp[2]), jnp.stack(pool_p),
            jnp.stack(kvs[0]), jnp.stack(kvs[1]), jnp.stack(kvs[2]), jnp.stack(pool_s))
```

```python
import contextlib
import numpy as np
import concourse.bass as bass
import concourse.mybir as mybir
from concourse.bass_utils import run_bass_kernel_spmd

F32 = mybir.dt.float32
BF16 = mybir.dt.bfloat16
AF = mybir.ActivationFunctionType
ALU = mybir.AluOpType
AX = mybir.AxisListType

NCORES = 8
D = 2048
NT = 2048
NS = 16
KC = 16
DFF = 8192
Q0, K0, V0, U0, GA0, GB0 = 0, 1536, 3072, 4608, 5120, 7168
EPS = 1e-6
SCALE = 128.0 ** -0.5
DIL = (1, 4, 16)
NCF = 588

_slopes = 2.0 ** (-8.0 * np.arange(1, 13) / 12.0)
_dilh = np.array([1, 1, 1, 1, 4, 4, 4, 4, 16, 16, 16, 16], np.float64)
_ah = _slopes * _dilh


def host_consts():
    CF = np.zeros((128, NCF), np.float32)
    CF[:, 0:128] = np.eye(128)
    j = np.arange(128)
    CF[:, 128:140] = (j[:, None] - 64) * _ah[None, :]
    for g, w in enumerate((2, 4, 8, 16)):
        pos = np.arange(16)
        CF[:, 140 + g * 16:140 + (g + 1) * 16] = 1.0 / np.minimum(pos + 1, w)
    for g, w in enumerate((2, 4, 8, 16)):
        for half in range(2):
            for bl in range(8):
                for row in range(15 - (w - 1), 15):
                    CF[bl * 15 + row, 204 + (g * 2 + half) * 16 + half * 8 + bl] = 1.0
    CF[:, 332:588] = np.eye(16).reshape(1, 256)
    MW = np.zeros((128, 12, 256), np.float32)
    i = np.arange(128)
    for h in range(12):
        MW[:, h, 0:128] = (j[:, None] <= i[None, :]) * np.exp(-_ah[h] * (i[None, :] - 64))
        MW[:, h, 128:256] = (j[:, None] >= i[None, :]) * np.exp(-_ah[h] * (i[None, :] - 64) - 128 * _ah[h])
    BS = np.zeros((16, 12, 129), np.float32)
    pos = np.arange(129)
    for h in range(12):
        BS[:, h, :] = -_ah[h] * (128 - pos)
    return CF, MW.reshape(128, 12 * 256), BS.reshape(16, 12 * 129)


class Tok:
    __slots__ = ("name", "w", "r", "sem", "dcnt")

    def __init__(self, name):
        self.name = name
        self.w = None
        self.r = {}
        self.sem = None
        self.dcnt = 0


class Sched:
    ENG = ("pe", "act", "dve", "pool", "sp")

    def __init__(self, nc, es):
        self.nc = nc
        self.es = es
        self.ops = {e: [] for e in self.ENG}
        self.cnt = {e: 0 for e in self.ENG}
        self.waited = {e: {} for e in self.ENG}
        self.esem = {e: es.enter_context(nc.semaphore("sem_" + e)) for e in self.ENG if e != "sp"}
        self.nsem = 0
        self.final = {}
        self.owners = []
        self.pending = {e: [] for e in self.ENG}

    def barrier(self, exclude=()):
        targets = {("e", e): self.cnt[e] for e in self.ENG if e != "sp"}
        for o in self.owners:
            if o not in exclude:
                targets[("d", o)] = o.dcnt
        for en in self.ENG:
            wd = self.waited[en]
            for k, v in targets.items():
                if v > 0 and k != ("e", en) and wd.get(k, 0) < v:
                    wd[k] = v
                    self.pending[en].append((k, v))

    def _collect(self, eng, reads, writes):
        need = {}

        def add(key, val):
            if need.get(key, 0) < val:
                need[key] = val

        for t in reads:
            if t.w is not None:
                add(*t.w)
        for t in writes:
            if t.w is not None:
                add(*t.w)
            for k, v in t.r.items():
                add(k, v)
        waits = []
        wd = self.waited[eng]
        for key, val in need.items():
            if key == ("e", "pe") and eng == "pe":
                continue
            if wd.get(key, 0) >= val:
                continue
            wd[key] = val
            waits.append((key, val))
        return waits

    def _commit(self, ev, reads, writes):
        for t in writes:
            t.w = ev
            t.r = {}
        for t in reads:
            if t not in writes:
                if t.r.get(ev[0], 0) < ev[1]:
                    t.r[ev[0]] = ev[1]

    def op(self, eng, fn, reads=(), writes=()):
        waits = self._collect(eng, reads, writes)
        self.cnt[eng] += 1
        ev = (("e", eng), self.cnt[eng])
        self._commit(ev, reads, writes)
        waits = self.pending[eng] + waits
        self.pending[eng] = []
        self.ops[eng].append((waits, fn, ("e", eng)))

    def dma(self, q, out, in_, reads=(), writes=(), owner=None, final=False):
        if owner is None:
            owner = writes[0] if writes else reads[0]
        if owner.sem is None:
            owner.sem = self.es.enter_context(self.nc.semaphore("dsem%d" % self.nsem))
            self.nsem += 1
            self.owners.append(owner)
        waits = self._collect(q, reads, writes)
        owner.dcnt += 16
        ev = (("d", owner), owner.dcnt)
        self._commit(ev, reads, writes)
        if final:
            self.final[ev[0]] = ev[1]
        waits = self.pending[q] + waits
        self.pending[q] = []
        self.ops[q].append((waits, lambda e, o=out, i=in_: e.dma_start(out=o, in_=i), ("d", owner)))

    def _sem(self, key):
        return self.esem[key[1]] if key[0] == "e" else key[1].sem

    def emit(self, block):
        engmap = {"pe": block.tensor, "act": block.scalar, "dve": block.vector,
                  "pool": block.gpsimd, "sp": block.sync}
        for en in self.ENG:
            ops = self.ops[en]
            fin = self.final if en == "sp" else None
            pend = self.pending[en]

            def body(eng, ops=ops, fin=fin, pend=pend):
                for waits, fn, sig in ops:
                    for key, val in waits:
                        eng.wait_ge(self._sem(key), val)
                    ins = fn(eng)
                    if sig[0] == "e":
                        ins.then_inc(self.esem[sig[1]], 1)
                    else:
                        ins.then_inc(sig[1].sem, 16)
                for key, val in pend:
                    eng.wait_ge(self._sem(key), val)
                if fin is not None:
                    for key, val in fin.items():
                        eng.wait_ge(self._sem(key), val)

            engmap[en](body)


class Arena:
    def __init__(self, tensor, nbytes):
        self.t = tensor
        self.cap = nbytes
        self.top = 0
        self.rtop = nbytes

    def mark(self):
        return self.top

    def release(self, m):
        self.top = m

    def rmark(self):
        return self.rtop

    def rrelease(self, m):
        self.rtop = m

    def alloc(self, shape, dtype, name="", side="L"):
        esz = 4 if dtype == F32 else 2
        n = 1
        for s in shape[1:]:
            n *= s
        nb = (n * esz + 31) // 32 * 32
        if side == "L":
            off = self.top
            self.top += nb
        else:
            self.rtop -= nb
            off = self.rtop
        assert self.top <= self.rtop, "SBUF arena overflow at %s: L=%d R=%d" % (name, self.top, self.rtop)
        ap = self.t[0:shape[0], off // 2: off // 2 + n * esz // 2]
        if dtype == F32:
            ap = ap.bitcast(F32)
        if len(shape) == 3:
            ap = ap.rearrange("p (a b) -> p a b", b=shape[2])
        elif len(shape) == 4:
            ap = ap.rearrange("p (a b c) -> p a b c", b=shape[2], c=shape[3])
        return ap


class _Stop(Exception):
    pass


import os as _os2
SKIP = int(_os2.environ.get('SKIP', '0'))


def build_program(stage=99):
    nc = bass.Bass("TRN2", target_bir_lowering=False)
    es = contextlib.ExitStack()

    def din(name, shape):
        return nc.dram_tensor(name, list(shape), F32, kind="ExternalInput").ap()

    def dout(name, shape):
        return nc.dram_tensor(name, list(shape), F32, kind="ExternalOutput").ap()

    xp = din("xp", (NT, D))
    xs = din("xs", (NS, D))
    call = din("call", (17, D))
    ckv = [din("ckv0", (NS, 128, 1024)), din("ckv1", (NS, 512, 1024)), din("ckv2", (NS, 2048, 1024))]
    spool = din("spool", (NS * 15, 512))
    w_ada = din("w_ada", (D, 6 * D))
    b_ada = din("b_ada", (96, 128))
    w_in = din("w_in", (D, 9216))
    w_ua = din("w_ua", (512, D))
    w_pool = din("w_pool", (4, 128, 128))
    vecs = din("vecs", (52, 128))
    w_up = din("w_up", (512, D))
    w_out = din("w_out", (D, D))
    w_mu = din("w_mu", (D, DFF))
    w_md = din("w_md", (DFF, D))
    cfd = din("cf", (128, NCF))
    mwd = din("mw", (128, 12 * 256))
    bsd = din("bs", (16, 12 * 129))

    yp = dout("yp", (NT, D))
    ys = dout("ys", (NS, D))
    kvp = [dout("kvp0", (128, 1024)), dout("kvp1", (512, 1024)), dout("kvp2", (2048, 1024))]
    poolp = dout("poolp", (15, 512))
    kvs = [dout("kvs0", (NS, 128, 1024)), dout("kvs1", (NS, 512, 1024)), dout("kvs2", (NS, 2048, 1024))]
    pools = dout("pools", (NS, 15, 512))
    qscr = nc.dram_tensor("qscr", [NS, 1536], F32, kind="Internal").ap()

    ARENA_BYTES = 207 * 1024
    arena_t = es.enter_context(nc.sbuf_tensor("arena", [128, ARENA_BYTES // 2], BF16))
    AR = Arena(arena_t, ARENA_BYTES)
    S = Sched(nc, es)
    psb = []
    for i in range(8):
        t = es.enter_context(nc.psum_tensor("ps%d" % i, [128, 512], F32))
        psb.append((t, Tok("ps%d" % i)))
    pstate = [0]

    def psn():
        t = psb[pstate[0] % 6]
        pstate[0] += 1
        return t

    PSD = psb[6]
    PSE = psb[7]

    cf = AR.alloc([128, NCF], F32, "cf")
    T_cf = Tok("cf")
    ident = cf[:, 0:128]
    cbias = cf[:, 128:140]
    invc = cf[:, 140:204]
    selw = cf[:, 204:332]
    i16 = cf[:, 332:588]
    identb = AR.alloc([128, 128], BF16, "identb")
    onesb = AR.alloc([128, 128], BF16, "onesb")
    T_cb = Tok("constb")
    vT1 = AR.alloc([128, 96], F32, "vT1")
    vT2 = AR.alloc([128, 52], F32, "vT2")
    T_vT = Tok("vT")
    modT = AR.alloc([128, 96, 17], F32, "modT")
    T_mod = Tok("modT")
    A1 = AR.alloc([128, 16, 17], F32, "A1")
    A2 = AR.alloc([128, 16, 17], F32, "A2")
    T_A = Tok("A")
    wpool = AR.alloc([128, 4, 128], BF16, "wpool")
    T_wpool = Tok("wpool")
    oaT = AR.alloc([128, 4, NT], BF16, "oaT")
    T_oa = [Tok("oa%d" % s) for s in range(4)]
    oaTs = AR.alloc([128, 4, NS], BF16, "oaTs")
    T_oas = Tok("oaTs")
    pTs = AR.alloc([128, 4, NS], BF16, "pTs")
    T_pTs = Tok("pTs")
    x1Ts = AR.alloc([128, 16, NS], F32, "x1Ts")
    T_x1s = Tok("x1Ts")
    hTs = AR.alloc([128, 16, NS], BF16, "hTs")
    T_hs = Tok("hTs")
    mixTs = AR.alloc([128, 16, NS], BF16, "mixTs")
    T_mixs = Tok("mixTs")
    smallf = AR.alloc([128, 64], F32, "smallf")
    T_small = Tok("smallf")
    WSLOT = 8 * 1024
    wsl = [(AR.alloc([128, WSLOT // 2], BF16, "ws%d" % i), Tok("ws%d" % i)) for i in range(3)]
    wstate = [0]
    tmpf = [(AR.alloc([128, 512], F32, "tmpf%d" % i), Tok("tmpf%d" % i)) for i in range(3)]
    tstate = [0]

    def tmpn():
        t = tmpf[tstate[0] % 3]
        tstate[0] += 1
        return t

    tmpb = [(AR.alloc([128, 512], BF16, "tmpb%d" % i), Tok("tmpb%d" % i)) for i in range(3)]
    bstate = [0]

    def tmpbn():
        t = tmpb[bstate[0] % 3]
        bstate[0] += 1
        return t

    bhsm = [(AR.alloc([128, 160], F32, "bhsm%d" % i), Tok("bhsm%d" % i)) for i in range(2)]
    bhstate = [0]
    qkvs = AR.alloc([128, 40, NS], F32, "qkvs")
    T_qkvs = Tok("qkvs")

    T_cp = Tok("cpy")
    cplist = []
    for g, W in enumerate((128, 512, 2048)):
        for b in range(NS):
            nrow = W - 1
            r = 0
            while r < nrow:
                n = min(512, nrow - r)
                cplist.append((kvs[g][b, r:r + n, :], ckv[g][b, r + 1:r + 1 + n, :]))
                r += n
    spool3 = spool.rearrange("(b r) c -> b r c", r=15)
    for b in range(NS):
        cplist.append((pools[b, 0:14, :], spool3[b, 1:15, :]))
    cpstate = [0]

    def drip(n=1):
        for _ in range(n):
            if cpstate[0] < len(cplist):
                o, i_ = cplist[cpstate[0]]
                cpstate[0] += 1
                S.dma("act", o, i_, owner=T_cp, final=True)

    def run_jobs(jobs):
        n = len(jobs)
        base = wstate[0]

        def issue(k):
            sl, tok = wsl[(base + k) % 3]
            for o, i_ in jobs[k][0](sl):
                S.dma("pool", o, i_, writes=[tok])

        for k in range(min(2, n)):
            issue(k)
        for k in range(n):
            if k + 2 < n:
                issue(k + 2)
            sl, tok = wsl[(base + k) % 3]
            drip(1)
            jobs[k][1](sl, tok)
        wstate[0] = (base + n) % 3

    def wview(sl, kk, cols):
        return sl[:, 0:kk * cols].rearrange("p (k c) -> p k c", c=cols)

    def rows_view(w, r0, nk, c0, cols):
        return w[r0:r0 + nk * 128, c0:c0 + cols].rearrange("(k p) c -> p k c", p=128)

    try:
        S.dma("sp", cf, cfd, writes=[T_cf])
        S.dma("pool", wpool, w_pool.rearrange("g c d -> c g d"), writes=[T_wpool])
        S.op("dve", lambda e: e.tensor_copy(out=identb, in_=ident), reads=[T_cf], writes=[T_cb])
        S.op("dve", lambda e: e.memset(onesb, 1.0), writes=[T_cb])

        m0 = AR.mark()
        call_sb = AR.alloc([17, D], F32, "call_sb")
        T_call = Tok("call")
        sT = AR.alloc([128, 16, 17], BF16, "sT")
        T_sT = Tok("sT")
        v1 = AR.alloc([96, 128], F32, "v1")
        v2 = AR.alloc([52, 128], F32, "v2")
        T_v = Tok("v12")
        S.dma("sp", call_sb, call, writes=[T_call])
        S.dma("sp", v1, b_ada, writes=[T_v])
        S.dma("sp", v2, vecs, writes=[T_v])
        S.op("act", lambda e: e.activation(out=call_sb, in_=call_sb, func=AF.Silu), writes=[T_call])
        pt, ptk = psn()

        def f(e):
            ins = None
            for kc in range(16):
                ins = e.matmul(pt[:, kc * 17:(kc + 1) * 17], call_sb[0:17, kc * 128:(kc + 1) * 128],
                               ident[0:17, 0:17], start=True, stop=True)
            return ins
        S.op("pe", f, reads=[T_call, T_cf], writes=[ptk])
        S.op("dve", lambda e: e.tensor_copy(out=sT, in_=pt[:, 0:272].rearrange("p (k c) -> p k c", c=17)),
             reads=[ptk], writes=[T_sT])
        pt2, ptk2 = psn()

        def f(e):
            e.matmul(pt2[:, 0:96], v1[0:96, :], ident[0:96, 0:96], start=True, stop=True)
            return e.matmul(pt2[:, 96:148], v2[0:52, :], ident[0:52, 0:52], start=True, stop=True)
        S.op("pe", f, reads=[T_v, T_cf], writes=[ptk2])

        def f(e):
            e.tensor_copy(out=vT1, in_=pt2[:, 0:96])
            return e.tensor_copy(out=vT2, in_=pt2[:, 96:148])
        S.op("dve", f, reads=[ptk2], writes=[T_vT])
        g1T = vT2[:, 0:16]
        g2T = vT2[:, 16:32]
        gfT = vT2[:, 32:48]
        pscT = vT2[:, 48:52]

        jobs = []
        for j in range(48):
            def loads(sl, j=j):
                return [(wview(sl, 16, 256), rows_view(w_ada, 0, 16, j * 256, 256))]

            def comp(sl, tok, j=j):
                wv = wview(sl, 16, 256)
                p, pk = psn()

                def f(e):
                    ins = None
                    for cc in range(2):
                        for kc in range(16):
                            ins = e.matmul(p[:, cc * 17:(cc + 1) * 17], wv[:, kc, cc * 128:(cc + 1) * 128],
                                           sT[:, kc, :], start=(kc == 0), stop=(kc == 15))
                    return ins
                S.op("pe", f, reads=[tok, T_sT], writes=[pk])

                def f2(e):
                    ins = None
                    for cc in range(2):
                        c = j * 2 + cc
                        ins = e.tensor_scalar(out=modT[:, c, :], in0=p[:, cc * 17:(cc + 1) * 17],
                                              scalar1=vT1[:, c:c + 1], scalar2=None, op0=ALU.add)
                    return ins
                S.op("dve", f2, reads=[pk, T_vT], writes=[T_mod])
            jobs.append((loads, comp))
        run_jobs(jobs)

        def f(e):
            ins = None
            for kc in range(16):
                e.tensor_scalar(out=A1[:, kc, :], in0=modT[:, 16 + kc, :], scalar1=1.0, scalar2=g1T[:, kc:kc + 1],
                                op0=ALU.add, op1=ALU.mult)
                ins = e.tensor_scalar(out=A2[:, kc, :], in0=modT[:, 64 + kc, :], scalar1=1.0, scalar2=g2T[:, kc:kc + 1],
                                      op0=ALU.add, op1=ALU.mult)
            return ins
        S.op("dve", f, reads=[T_mod, T_vT], writes=[T_A])
        S.barrier(exclude=(T_cp,))
        AR.release(m0)
        if stage <= 0:
            raise _Stop()
        SH1, GT1, SH2, GT2 = 0, 32, 48, 80

        def rstd_from_sum(out_ap, in_ap, toks_r, toks_w):
            S.op("dve", lambda e: e.tensor_scalar(out=out_ap, in0=in_ap, scalar1=1.0 / D, scalar2=EPS,
                                                  op0=ALU.mult, op1=ALU.add), reads=toks_r, writes=toks_w)
            S.op("act", lambda e: e.activation(out=out_ap, in_=out_ap, func=AF.Sqrt), writes=toks_w)
            S.op("dve", lambda e: e.reciprocal(out=out_ap, in_=out_ap), writes=toks_w)

        def build_h(xst, xrows, ntok, h_dst=None, T_h=None, x_dst=None, T_x=None, sample=False):
            xt, xtk = xst[0][xst[1][0] % len(xst[0])]
            xst[1][0] += 1
            xv = xt[0:ntok, :]
            S.dma("sp", xv, xrows, writes=[xtk])
            sm, smk = bhsm[bhstate[0] % 2]
            bhstate[0] += 1
            if h_dst is not None:
                jb, jbk = tmpbn()
                ss4 = sm[0:ntok, 0:4]
                ss = sm[0:ntok, 8:9]
                dm = sm[0:ntok, 16:16 + ntok]

                S.op("dve", lambda e: e.memzero(ss4), writes=[smk])
                jb2, jbk2 = tmpbn()
                jbs = [jb[0:ntok, :], jb2[0:ntok, :]]

                def f(e):
                    ins = None
                    for q in range(4):
                        ins = e.activation(out=jbs[q % 2], in_=xv[:, q * 512:(q + 1) * 512],
                                           func=AF.Square, accum_out=ss4[:, q:q + 1])
                    return ins
                S.op("act", f, reads=[xtk], writes=[jbk, jbk2, smk])
                S.op("dve", lambda e: e.tensor_reduce(out=ss, in_=ss4, axis=AX.X, op=ALU.add), writes=[smk])
                rstd_from_sum(ss, ss, [smk], [smk])
                S.op("dve", lambda e: e.tensor_scalar(out=dm, in0=ident[0:ntok, 0:ntok], scalar1=ss, scalar2=None,
                                                      op0=ALU.mult), reads=[T_cf], writes=[smk])
            for kq in range(4):
                if x_dst is not None:
                    p2, pk2 = psn()

                    def f(e, kq=kq, p2=p2):
                        ins = None
                        for jj in range(4):
                            kc = kq * 4 + jj
                            ins = e.matmul(p2[:, jj * ntok:(jj + 1) * ntok], xv[:, kc * 128:(kc + 1) * 128],
                                           ident[0:ntok, 0:ntok], start=True, stop=True)
                        return ins
                    S.op("pe", f, reads=[xtk, T_cf], writes=[pk2])
                    S.op("act", lambda e, kq=kq, p2=p2: e.activation(
                        out=x_dst(kq), in_=p2[:, 0:4 * ntok].rearrange("p (a b) -> p a b", b=ntok), func=AF.Copy),
                        reads=[pk2], writes=[T_x])
                if h_dst is None:
                    continue
                p, pk = psn()

                def f(e, kq=kq, p=p):
                    ins = None
                    for jj in range(4):
                        kc = kq * 4 + jj
                        ins = e.matmul(p[:, jj * ntok:(jj + 1) * ntok], xv[:, kc * 128:(kc + 1) * 128], dm,
                                       start=True, stop=True)
                    return ins
                S.op("pe", f, reads=[xtk, smk], writes=[pk])
                if not sample:
                    if kq % 2 == 0:
                        def f(e, kq=kq, p=p):
                            ins = None
                            for jj in range(4):
                                kc = kq * 4 + jj
                                ins = e.tensor_scalar(out=h_dst(kc), in0=p[:, jj * ntok:(jj + 1) * ntok],
                                                      scalar1=A1[:, kc, 16:17], scalar2=modT[:, SH1 + kc, 16:17],
                                                      op0=ALU.mult, op1=ALU.add)
                            return ins
                        S.op("dve", f, reads=[pk, T_A, T_mod], writes=[T_h])
                    else:
                        def f(e, kq=kq, p=p):
                            ins = None
                            for jj in range(4):
                                kc = kq * 4 + jj
                                ins = e.activation(out=h_dst(kc), in_=p[:, jj * ntok:(jj + 1) * ntok],
                                                   func=AF.Identity, bias=modT[:, SH1 + kc, 16:17],
                                                   scale=A1[:, kc, 16:17])
                            return ins
                        S.op("act", f, reads=[pk, T_A, T_mod], writes=[T_h])
                else:
                    t2, t2k = tmpn()

                    def f(e, kq=kq, p=p, t2=t2):
                        ins = None
                        for jj in range(4):
                            kc = kq * 4 + jj
                            ins = e.tensor_tensor(out=t2[:, jj * 16:(jj + 1) * 16], in0=p[:, jj * ntok:(jj + 1) * ntok],
                                                  in1=A1[:, kc, 0:16], op=ALU.mult)
                        return ins
                    S.op("dve", f, reads=[pk, T_A], writes=[t2k])

                    def f(e, kq=kq, t2=t2):
                        ins = None
                        for jj in range(4):
                            kc = kq * 4 + jj
                            ins = e.tensor_tensor(out=h_dst(kc), in0=t2[:, jj * 16:(jj + 1) * 16],
                                                  in1=modT[:, SH1 + kc, 0:16], op=ALU.add)
                        return ins
                    S.op("dve", f, reads=[t2k, T_mod], writes=[T_h])

        T_xst = [Tok("xst0"), Tok("xst1")]
        T_yst = [Tok("yst0"), Tok("yst1"), Tok("yst2")]

        def make_xst(n):
            return ([(AR.alloc([128, D], F32, "xst%d" % i), T_xst[i]) for i in range(n)], [0])

        mr1 = AR.rmark()
        hT = AR.alloc([128, 16, NT], BF16, "hT", side="R")
        T_h = [Tok("hT%d" % c) for c in range(4)]
        m1a = AR.mark()
        xst = make_xst(2)
        build_h(xst, xs, NS, h_dst=lambda kc: hTs[:, kc, :], T_h=T_hs,
                x_dst=lambda kq: x1Ts[:, kq * 4:(kq + 1) * 4, :], T_x=T_x1s, sample=True)
        for sub in range(16):
            build_h(xst, xp[sub * 128:(sub + 1) * 128, :], 128,
                    h_dst=lambda kc, sub=sub: hT[:, kc, sub * 128:(sub + 1) * 128], T_h=T_h[sub // 4])
        S.barrier(exclude=(T_cp,))
        AR.release(m1a)
        if stage <= 1:
            raise _Stop()

        m1d = AR.mark()
        mw = AR.alloc([128, 12, 256], BF16, "mw")
        T_mw = Tok("mw")
        S.dma("pool", mw, mwd.rearrange("p (h c) -> p h c", c=256), writes=[T_mw])
        qT = AR.alloc([128, 3, NT], BF16, "qT")
        kT = AR.alloc([128, 3, NT], BF16, "kT")
        T_q = [Tok("qT%d" % g) for g in range(3)]
        T_k = [Tok("kT%d" % g) for g in range(3)]
        T_vst = Tok("vst")
        Vt = AR.alloc([128, 3, 16, 128], BF16, "Vt")
        T_V = [Tok("V%d" % g) for g in range(3)]
        acc = AR.alloc([128, 2, NT], F32, "acc")
        T_acc = Tok("acc")
        nmx = AR.alloc([128, 2, 3, 4], F32, "nmx")
        T_nmx = Tok("nmx")
        bcol = AR.alloc([128, 8], F32, "bcol")
        T_bcol = Tok("bcol")
        ptr = [(AR.alloc([128, 256], BF16, "ptr%d" % i), Tok("ptr%d" % i)) for i in range(2)]
        ptm = [(AR.alloc([128, 256], BF16, "ptm%d" % i), Tok("ptm%d" % i)) for i in range(2)]
        pti = [0]

        def deint(ap2, d, t0, n):
            if d == 1:
                return ap2[:, t0:t0 + n]
            return ap2.rearrange("p (r u) -> p u r", r=d)[:, t0 // d:(t0 + n) // d, :]

        def nat(ap2, d):
            if d == 1:
                return ap2
            return ap2.rearrange("p (u r) -> p u r", r=d)

        for s in range(4):
            vst = oaT[:, s, :]
            ajobs = []
            for ty in (1, 2, 0):
                for g in range(3):
                    def loads(sl, ty=ty, s=s, g=g):
                        c0 = (Q0, K0, V0)[ty] + g * 512 + s * 128
                        return [(wview(sl, 16, 128), rows_view(w_in, 0, 16, c0, 128))]

                    def comp(sl, tok, ty=ty, s=s, g=g, vst=vst):
                        wv = wview(sl, 16, 128)
                        d = DIL[g]
                        for c in range(4):
                            p, pk = psn()

                            def f(e, p=p, c=c):
                                ins = None
                                for kc in range(16):
                                    ins = e.matmul(p[:, :], wv[:, kc, :], hT[:, kc, c * 512:(c + 1) * 512],
                                                   start=(kc == 0), stop=(kc == 15))
                                return ins
                            S.op("pe", f, reads=[tok, T_h[c]], writes=[pk])
                            if ty == 2:
                                dk = T_vst
                                dv = deint(vst, d, c * 512, 512)
                            else:
                                buf = qT if ty == 0 else kT
                                dk = (T_q if ty == 0 else T_k)[g]
                                dv = deint(buf[:, g, :], d, c * 512, 512)
                            if not (SKIP & 1):
                                S.op("dve", lambda e, p=p, dv=dv, d=d: e.tensor_copy(out=dv, in_=nat(p[:, :], d)),
                                     reads=[pk], writes=[dk])
                            if ty != 2 and not (SKIP & 2):
                                sq, sqk = tmpbn()
                                S.op("act", lambda e, p=p, sq=sq: e.activation(out=sq[:, :], in_=p[:, :], func=AF.Square),
                                     reads=[pk, dk], writes=[sqk])
                                p2, pk2 = psn()
                                S.op("pe", lambda e, p2=p2, sq=sq: e.matmul(p2[:, :], onesb, sq[:, :], start=True, stop=True),
                                     reads=[sqk, T_cb], writes=[pk2])
                                S.op("dve", lambda e, p2=p2, c=c: e.tensor_reduce(
                                    out=nmx[:, ty, g, c:c + 1], in_=p2[:, :], axis=AX.X, op=ALU.max),
                                    reads=[pk2], writes=[T_nmx])
                        if ty == 2:
                            for bq in range(4):
                                p, pk = psn()
                                pb = p[:, :].bitcast(BF16)

                                def f(e, pb=pb, bq=bq):
                                    ins = None
                                    for jj in range(4):
                                        blk = bq * 4 + jj
                                        ins = e.transpose(pb[:, jj * 128:(jj + 1) * 128], vst[:, blk * 128:(blk + 1) * 128], identb)
                                    return ins
                                S.op("pe", f, reads=[T_vst, T_cb], writes=[pk])
                                S.op("act", lambda e, pb=pb, bq=bq: e.activation(
                                    out=Vt[:, g, bq * 4:(bq + 1) * 4, :],
                                    in_=pb[:, 0:512].rearrange("p (a b) -> p a b", b=128), func=AF.Copy),
                                    reads=[pk], writes=[T_V[g]])
                        if SKIP & 4:
                            return
                        p, pk = psn()

                        def f(e, p=p):
                            ins = None
                            for kc in range(16):
                                ins = e.matmul(p[:, 0:16], wv[:, kc, :], hTs[:, kc, :], start=(kc == 0), stop=(kc == 15))
                            return ins
                        S.op("pe", f, reads=[tok, T_hs], writes=[pk])
                        S.op("act", lambda e, p=p: e.activation(out=qkvs[:, ty * 12 + g * 4 + s, :], in_=p[:, 0:16], func=AF.Copy),
                             reads=[pk], writes=[T_qkvs])
                    ajobs.append((loads, comp))
            import os as _os
            if stage <= 1.2:
                ajobs = ajobs[:int(_os.environ.get('NJOBS', '9'))]
            run_jobs(ajobs)
            if stage <= 1.2:
                raise _Stop()

            S.op("dve", lambda e: e.tensor_reduce(out=smallf[:, 8:14].rearrange("p (a b) -> p a b", b=3),
                                                  in_=nmx[:, 0:2, :, :], axis=AX.X, op=ALU.max),
                 reads=[T_nmx], writes=[T_small])
            S.op("dve", lambda e: e.tensor_tensor(out=smallf[:, 16:19], in0=smallf[:, 8:11], in1=smallf[:, 11:14],
                                                  op=ALU.mult), writes=[T_small])
            S.op("act", lambda e: e.activation(out=smallf[:, 16:19], in_=smallf[:, 16:19], func=AF.Sqrt), writes=[T_small])

            S.op("dve", lambda e: e.tensor_reduce(out=smallf[:, 20:21], in_=smallf[:, 16:19], axis=AX.X, op=ALU.max),
                 writes=[T_small])

            def f(e, s=s):
                ins = None
                for g in range(3):
                    h = g * 4 + s
                    ins = e.scalar_tensor_tensor(out=bcol[:, g:g + 1], in0=smallf[:, 20:21], scalar=-1.02 * SCALE,
                                                 in1=cbias[:, h:h + 1], op0=ALU.mult, op1=ALU.add)
                return ins
            S.op("dve", f, reads=[T_cf], writes=[T_small, T_bcol])
            if stage <= 1.4:
                raise _Stop()

            for g in range(3):
                d = DIL[g]
                h = g * 4 + s
                nb = 16 // d
                for r in range(d):
                    for j in range(nb):
                        blk = r * nb + j
                        wd = 256 if j > 0 else 128
                        pS, pSk = psn()

                        def f(e, pS=pS, g=g, blk=blk, j=j):
                            ins = e.matmul(pS[:, 0:128], kT[:, g, blk * 128:(blk + 1) * 128], qT[:, g, blk * 128:(blk + 1) * 128],
                                           start=True, stop=True)
                            if j > 0:
                                ins = e.matmul(pS[:, 128:256], kT[:, g, (blk - 1) * 128:blk * 128],
                                               qT[:, g, blk * 128:(blk + 1) * 128], start=True, stop=True)
                            return ins
                        S.op("pe", f, reads=[T_k[g], T_q[g]], writes=[pSk])
                        pr, prk = ptr[pti[0] % 2]
                        pm, pmk = ptm[pti[0] % 2]
                        pti[0] += 1
                        S.op("act", lambda e, pS=pS, pr=pr, wd=wd, g=g: e.activation(
                            out=pr[:, 0:wd], in_=pS[:, 0:wd], func=AF.Exp, bias=bcol[:, g:g + 1], scale=SCALE),
                            reads=[pSk, T_bcol], writes=[prk])
                        S.op("dve", lambda e, pr=pr, pm=pm, wd=wd, h=h: e.tensor_tensor(
                            out=pm[:, 0:wd], in0=pr[:, 0:wd], in1=mw[:, h, 0:wd], op=ALU.mult),
                            reads=[prk, T_mw], writes=[pmk])
                        pN, pNk = psn()

                        def f(e, pN=pN, pm=pm, g=g, blk=blk, j=j):
                            e.matmul(pN[:, 0:128], Vt[:, g, blk, :], pm[:, 0:128], start=True, stop=(j == 0))
                            if j > 0:
                                e.matmul(pN[:, 0:128], Vt[:, g, blk - 1, :], pm[:, 128:256], start=False, stop=True)
                            ins = e.matmul(pN[:, 128:256], onesb, pm[:, 0:128], start=True, stop=(j == 0))
                            if j > 0:
                                ins = e.matmul(pN[:, 128:256], onesb, pm[:, 128:256], start=False, stop=True)
                            return ins
                        S.op("pe", f, reads=[T_V[g], pmk, T_cb], writes=[pNk])
                        if d == 1:
                            av = acc[:, :, j * 128:(j + 1) * 128]
                        else:
                            av = acc.rearrange("p c (u r) -> p c u r", r=d)[:, :, j * 128:(j + 1) * 128, r]
                        pv = pN[:, 0:256].rearrange("p (c q) -> p c q", q=128)
                        if g == 0:
                            S.op("act", lambda e, av=av, pv=pv: e.activation(out=av, in_=pv, func=AF.Copy),
                                 reads=[pNk], writes=[T_acc])
                        else:
                            S.op("dve", lambda e, av=av, pv=pv: e.tensor_tensor(out=av, in0=av, in1=pv, op=ALU.add),
                                 reads=[pNk], writes=[T_acc])
            if stage <= 1.6:
                raise _Stop()
            for c in range(4):
                sl_ = slice(c * 512, (c + 1) * 512)

                S.op("dve", lambda e, sl_=sl_: e.reciprocal(out=acc[:, 1, sl_], in_=acc[:, 1, sl_]),
                     reads=[T_vst], writes=[T_acc, T_vst])
                S.op("dve", lambda e, sl_=sl_, s=s: e.tensor_tensor(out=oaT[:, s, sl_], in0=acc[:, 0, sl_], in1=acc[:, 1, sl_],
                                                                    op=ALU.mult),
                     writes=[T_acc, T_oa[s], T_vst])
        S.barrier(exclude=(T_cp,))
        AR.release(m1d)
        if stage <= 2:
            raise _Stop()

        m1c = AR.mark()
        kvst = [(AR.alloc([128, 256], F32, "kvst%d" % i), Tok("kvst%d" % i)) for i in range(3)]
        kvi = [0]
        kjobs = []
        for g, W in enumerate((128, 512, 2048)):
            for kv in range(2):
                for half in range(2):
                    def loads(sl, g=g, kv=kv, half=half):
                        c0 = (K0 if kv == 0 else V0) + g * 512 + half * 256
                        return [(wview(sl, 16, 256), rows_view(w_in, 0, 16, c0, 256))]

                    def comp(sl, tok, g=g, kv=kv, half=half, W=W):
                        wv = wview(sl, 16, 256)
                        for tt in range(W // 128):
                            t0 = NT - W + tt * 128
                            p, pk = psn()

                            def f(e, p=p, t0=t0):
                                ins = None
                                for kc in range(16):
                                    ins = e.matmul(p[:, 0:256], hT[:, kc, t0:t0 + 128], wv[:, kc, :],
                                                   start=(kc == 0), stop=(kc == 15))
                                return ins
                            S.op("pe", f, reads=[tok, T_h[t0 // 512]], writes=[pk])
                            st, stk = kvst[kvi[0] % 3]
                            kvi[0] += 1
                            if kvi[0] % 2 == 0:
                                S.op("act", lambda e, p=p, st=st: e.activation(out=st[:, :], in_=p[:, 0:256], func=AF.Copy),
                                     reads=[pk], writes=[stk])
                            else:
                                S.op("dve", lambda e, p=p, st=st: e.tensor_copy(out=st[:, :], in_=p[:, 0:256]),
                                     reads=[pk], writes=[stk])
                            S.dma("sp", kvp[g][tt * 128:(tt + 1) * 128, kv * 512 + half * 256:kv * 512 + (half + 1) * 256],
                                  st[:, :], reads=[stk], owner=stk, final=True)
                    kjobs.append((loads, comp))
        run_jobs(kjobs)
        S.barrier(exclude=(T_cp,))
        AR.release(m1c)
        if stage <= 3:
            raise _Stop()

        pT = AR.alloc([128, 4, NT], BF16, "pT")
        T_pT = [Tok("pT%d" % g) for g in range(4)]
        m1b = AR.mark()
        ubuf = AR.alloc([128, 16 + NT], F32, "ubuf")
        T_u = Tok("ubuf")
        sa = AR.alloc([128, 16 + NT], F32, "sa")
        sbb = AR.alloc([128, 16 + NT], F32, "sbb")
        T_sa = Tok("sa")
        T_sb = Tok("sb")
        zb = AR.alloc([128, NT], BF16, "zb")
        T_z = Tok("zb")
        zs = AR.alloc([128, 4, NS], BF16, "zs")
        T_zs = Tok("zs")
        shist = AR.alloc([128, 4, NS], F32, "shist")
        T_sh = Tok("shist")
        sp_sb = [AR.alloc([120, 512], F32, "sp_sb%d" % h) for h in range(2)]
        T_sp = Tok("sp_sb")
        utok = AR.alloc([16, 512], F32, "utok")
        T_utok = Tok("utok")
        utoks = AR.alloc([16, 512], F32, "utoks")
        T_utoks = Tok("utoks")
        for h in range(2):
            S.dma("sp", sp_sb[h], spool[h * 120:(h + 1) * 120, :], writes=[T_sp])

        def f(e):
            e.memzero(ubuf[:, 0:16])
            e.memzero(sa[:, 0:16])
            return e.memzero(sbb[:, 0:16])
        S.op("dve", f, writes=[T_u, T_sa, T_sb])
        p, pk = psn()

        def f(e, p=p):
            ins = None
            for g in range(4):
                for h in range(2):
                    ins = e.matmul(p[:, g * 16:(g + 1) * 16], sp_sb[h][0:120, g * 128:(g + 1) * 128],
                                   selw[0:120, (g * 2 + h) * 16:(g * 2 + h + 1) * 16], start=(h == 0), stop=(h == 1))
            return ins
        S.op("pe", f, reads=[T_sp, T_cf], writes=[pk])
        S.op("dve", lambda e, p=p: e.tensor_copy(out=shist, in_=p[:, 0:64].rearrange("p (g b) -> p g b", b=16)),
             reads=[pk], writes=[T_sh])
        pu_tok, pu_tokk = PSD
        pus_tok, pus_tokk = PSE

        ujobs = []
        for jh in range(2):
            def loads(sl, jh=jh):
                return [(wview(sl, 16, 256), rows_view(w_in, 0, 16, U0 + jh * 256, 256))]

            def comp(sl, tok, jh=jh):
                wv = wview(sl, 16, 256)
                for gg in range(2):
                    g = jh * 2 + gg
                    w = 2 << g
                    for c in range(4):
                        p, pk = psn()

                        def f(e, p=p, c=c, gg=gg):
                            ins = None
                            for kc in range(16):
                                ins = e.matmul(p[:, :], wv[:, kc, gg * 128:(gg + 1) * 128], hT[:, kc, c * 512:(c + 1) * 512],
                                               start=(kc == 0), stop=(kc == 15))
                            return ins
                        S.op("pe", f, reads=[tok, T_h[c]], writes=[pk])
                        S.op("act", lambda e, p=p, c=c: e.activation(out=ubuf[:, 16 + c * 512:16 + (c + 1) * 512],
                                                                     in_=p[:, :], func=AF.Copy),
                             reads=[pk], writes=[T_u])
                    p, pk = psn()

                    def f(e, p=p, gg=gg):
                        ins = None
                        for kc in range(16):
                            ins = e.matmul(p[:, 0:16], wv[:, kc, gg * 128:(gg + 1) * 128], hTs[:, kc, :],
                                           start=(kc == 0), stop=(kc == 15))
                        return ins
                    S.op("pe", f, reads=[tok, T_hs], writes=[pk])
                    S.op("act", lambda e, p=p, g=g: e.activation(out=qkvs[:, 36 + g, :], in_=p[:, 0:16], func=AF.Copy),
                         reads=[pk], writes=[T_qkvs])
                    S.op("pe", lambda e, g=g: e.matmul(pu_tok[0:16, g * 128:(g + 1) * 128], ubuf[:, NT:NT + 16], ident,
                                                       start=True, stop=True), reads=[T_u, T_cf], writes=[pu_tokk])
                    S.op("pe", lambda e, g=g: e.matmul(pus_tok[0:16, g * 128:(g + 1) * 128], qkvs[:, 36 + g, :], ident,
                                                       start=True, stop=True), reads=[T_qkvs, T_cf], writes=[pus_tokk])
                    src, srck = ubuf, T_u
                    bufs = [(sa, T_sa), (sbb, T_sb)]
                    sh = 1
                    for step in range(g + 1):
                        dst, dstk = bufs[step % 2]
                        S.op("dve", lambda e, src=src, dst=dst, sh=sh: e.tensor_tensor(
                            out=dst[:, 16:16 + NT], in0=src[:, 16:16 + NT], in1=src[:, 16 - sh:16 - sh + NT], op=ALU.add),
                            reads=[srck], writes=[dstk])
                        src, srck = dst, dstk
                        sh *= 2
                    S.op("dve", lambda e, src=src, w=w: e.scalar_tensor_tensor(
                        out=zb[:, :], in0=src[:, 16:16 + NT], scalar=1.0 / w, in1=ubuf[:, 16:16 + NT],
                        op0=ALU.mult, op1=ALU.subtract), reads=[srck, T_u], writes=[T_z])
                    tt, ttk = tmpn()

                    S.op("dve", lambda e, src=src, g=g, tt=tt: e.tensor_tensor(
                        out=tt[:, 0:16], in0=src[:, 16:32], in1=invc[:, g * 16:(g + 1) * 16], op=ALU.mult),
                        reads=[srck, T_cf], writes=[ttk])
                    S.op("dve", lambda e, tt=tt: e.tensor_tensor(out=zb[:, 0:16], in0=tt[:, 0:16], in1=ubuf[:, 16:32],
                                                                 op=ALU.subtract), reads=[ttk, T_u], writes=[T_z])
                    for c in range(4):
                        p, pk = psn()
                        S.op("pe", lambda e, p=p, c=c, g=g: e.matmul(p[:, :], wpool[:, g, :], zb[:, c * 512:(c + 1) * 512],
                                                                     start=True, stop=True),
                             reads=[T_wpool, T_z], writes=[pk])
                        S.op("act", lambda e, p=p, c=c, g=g: e.activation(
                            out=pT[:, g, c * 512:(c + 1) * 512], in_=p[:, :], func=AF.Identity, scale=pscT[:, g:g + 1]),
                            reads=[pk, T_vT], writes=[T_pT[g]])
                    tt, ttk = tmpn()

                    S.op("dve", lambda e, g=g, tt=tt: e.tensor_tensor(out=tt[:, 0:16], in0=shist[:, g, :],
                                                                      in1=qkvs[:, 36 + g, :], op=ALU.add),
                         reads=[T_sh, T_qkvs], writes=[ttk])
                    S.op("dve", lambda e, g=g, w=w, tt=tt: e.scalar_tensor_tensor(
                        out=zs[:, g, :], in0=tt[:, 0:16], scalar=1.0 / w, in1=qkvs[:, 36 + g, :],
                        op0=ALU.mult, op1=ALU.subtract), reads=[ttk, T_qkvs], writes=[T_zs])
                    p, pk = psn()
                    S.op("pe", lambda e, p=p, g=g: e.matmul(p[:, 0:16], wpool[:, g, :], zs[:, g, :], start=True, stop=True),
                         reads=[T_wpool, T_zs], writes=[pk])
                    S.op("act", lambda e, p=p, g=g: e.activation(out=pTs[:, g, :], in_=p[:, 0:16], func=AF.Identity,
                                                                 scale=pscT[:, g:g + 1]),
                         reads=[pk, T_vT], writes=[T_pTs])
            ujobs.append((loads, comp))
        run_jobs(ujobs)
        S.op("dve", lambda e: e.tensor_copy(out=utok, in_=pu_tok[0:16, :]), reads=[pu_tokk], writes=[T_utok])
        S.op("dve", lambda e: e.tensor_copy(out=utoks, in_=pus_tok[0:16, :]), reads=[pus_tokk], writes=[T_utoks])
        S.dma("sp", poolp, utok[1:16, :], reads=[T_utok], owner=T_utok, final=True)
        S.dma("sp", pools[:, 14, :], utoks, reads=[T_utoks], owner=T_utoks, final=True)
        S.barrier(exclude=(T_cp,))
        AR.release(m1b)
        AR.rrelease(mr1)
        if stage <= 4:
            raise _Stop()

        msa = AR.mark()
        tokq = AR.alloc([16, 3, 1536], F32, "tokq")
        T_tokq = Tok("tokq")
        for ty in range(3):
            for g in range(3):
                p, pk = psn()

                def f(e, p=p, ty=ty, g=g):
                    ins = None
                    for hh in range(4):
                        ins = e.matmul(p[0:16, hh * 128:(hh + 1) * 128], qkvs[:, ty * 12 + g * 4 + hh, :], ident,
                                       start=True, stop=True)
                    return ins
                S.op("pe", f, reads=[T_qkvs, T_cf], writes=[pk])
                S.op("act", lambda e, p=p, ty=ty, g=g: e.activation(out=tokq[:, ty, g * 512:(g + 1) * 512], in_=p[0:16, :],
                                                                    func=AF.Copy), reads=[pk], writes=[T_tokq])
        T_qscr = Tok("qscr")
        S.dma("sp", qscr, tokq[:, 0, :], reads=[T_tokq], writes=[T_qscr])
        for g, W in enumerate((128, 512, 2048)):
            S.dma("sp", kvs[g][:, W - 1, 0:512], tokq[:, 1, g * 512:(g + 1) * 512], reads=[T_tokq], owner=T_tokq, final=True)
            S.dma("sp", kvs[g][:, W - 1, 512:1024], tokq[:, 2, g * 512:(g + 1) * 512], reads=[T_tokq], owner=T_tokq, final=True)
        bs_sb = AR.alloc([16, 12, 129], F32, "bs_sb")
        T_bs = Tok("bs")
        S.dma("sp", bs_sb, bsd.rearrange("p (h c) -> p h c", c=129), writes=[T_bs])
        sT_all = AR.alloc([128, 16, 12], F32, "sT_all")
        T_sTa = Tok("sT_all")
        stok = AR.alloc([16, 12, 129], F32, "stok")
        T_stok = Tok("stok")
        sm16 = AR.alloc([16, 64], F32, "sm16")
        T_sm16 = Tok("sm16")
        prodt = AR.alloc([16, 1536], F32, "prodt")
        T_prodt = Tok("prodt")
        kh = [(AR.alloc([128, 512], F32, "kh%d" % i), Tok("kh%d" % i)) for i in range(2)]
        qb = [(AR.alloc([128, 512], F32, "qb%d" % i), Tok("qb%d" % i)) for i in range(2)]
        vh = [(AR.alloc([128, 512], BF16, "vh%d" % i), Tok("vh%d" % i)) for i in range(3)]
        wT = AR.alloc([128, 12, 16], F32, "wT")
        T_wT = Tok("wT")
        wTm = AR.alloc([128, 12, 16, 16], BF16, "wTm")
        T_wTm = Tok("wTm")
        otok = AR.alloc([16, 512], F32, "otok")
        T_otok = Tok("otok")

        S.op("dve", lambda e: e.tensor_tensor(out=prodt, in0=tokq[:, 0, :], in1=tokq[:, 1, :], op=ALU.mult),
             reads=[T_tokq], writes=[T_prodt])
        S.op("dve", lambda e: e.tensor_reduce(out=stok[:, :, 128:129], in_=prodt.rearrange("p (h x) -> p h x", x=128),
                                              axis=AX.X, op=ALU.add), reads=[T_prodt], writes=[T_stok])
        it = 0
        for b in range(NS):
            for g in range(3):
                d = DIL[g]
                kt, ktk = kh[it % 2]
                qt, qtk = qb[it % 2]
                it += 1
                S.dma("sp", kt, ckv[g][b, :, 0:512].rearrange("(j x) c -> j x c", x=d)[:, 0, :], writes=[ktk])
                S.dma("sp", qt, qscr[b:b + 1, g * 512:(g + 1) * 512].partition_broadcast(128), reads=[T_qscr], writes=[qtk])

                S.op("dve", lambda e, kt=kt, qt=qt: e.tensor_tensor(out=kt, in0=kt, in1=qt, op=ALU.mult),
                     reads=[qtk], writes=[ktk])
                S.op("dve", lambda e, kt=kt, b=b, g=g: e.tensor_reduce(
                    out=sT_all[:, b, g * 4:(g + 1) * 4], in_=kt.rearrange("p (h x) -> p h x", x=128), axis=AX.X, op=ALU.add),
                    reads=[ktk], writes=[T_sTa])
        for g3 in range(3):
            p, pk = psn()

            def f(e, p=p, g3=g3):
                ins = None
                for hh in range(4):
                    ins = e.matmul(p[0:16, hh * 128:(hh + 1) * 128], sT_all[:, :, g3 * 4 + hh], ident, start=True, stop=True)
                return ins
            S.op("pe", f, reads=[T_sTa, T_cf], writes=[pk])
            S.op("dve", lambda e, p=p, g3=g3: e.tensor_copy(
                out=stok[:, g3 * 4:(g3 + 1) * 4, 0:128], in_=p[0:16, :].rearrange("p (h x) -> p h x", x=128)),
                reads=[pk], writes=[T_stok])
        mx = sm16[:, 0:12]
        den = sm16[:, 12:24]
        Mx = sm16[:, 24:28]
        fco = sm16[:, 28:40]
        dtot = sm16[:, 40:44]

        def dv(fn, reads=(), writes=()):
            S.op("dve", fn, reads=list(reads), writes=list(writes))

        g3v = lambda ap: ap.rearrange("p (g s) -> p g s", s=4)
        s3v = lambda ap: ap.rearrange("p (g s) -> p s g", s=4)
        dv(lambda e: e.scalar_tensor_tensor(out=stok, in0=stok, scalar=SCALE, in1=bs_sb, op0=ALU.mult, op1=ALU.add),
           [T_bs], [T_stok])
        dv(lambda e: e.tensor_reduce(out=mx, in_=stok, axis=AX.X, op=ALU.max), [T_stok], [T_sm16])
        dv(lambda e: e.tensor_tensor(out=stok, in0=stok, in1=mx.unsqueeze(2).to_broadcast([16, 12, 129]), op=ALU.subtract),
           [T_sm16], [T_stok])
        S.op("act", lambda e: e.activation(out=stok, in_=stok, func=AF.Exp), writes=[T_stok])
        dv(lambda e: e.tensor_reduce(out=den, in_=stok, axis=AX.X, op=ALU.add), [T_stok], [T_sm16])
        dv(lambda e: e.tensor_reduce(out=Mx, in_=s3v(mx), axis=AX.X, op=ALU.max), [], [T_sm16])
        dv(lambda e: e.tensor_tensor(out=g3v(fco), in0=g3v(mx), in1=Mx.unsqueeze(1).to_broadcast([16, 3, 4]), op=ALU.subtract),
           [], [T_sm16])
        S.op("act", lambda e: e.activation(out=fco, in_=fco, func=AF.Exp), writes=[T_sm16])
        dv(lambda e: e.tensor_tensor(out=den, in0=den, in1=fco, op=ALU.mult), [], [T_sm16])
        dv(lambda e: e.tensor_reduce(out=dtot, in_=s3v(den), axis=AX.X, op=ALU.add), [], [T_sm16])
        dv(lambda e: e.reciprocal(out=dtot, in_=dtot), [], [T_sm16])
        dv(lambda e: e.tensor_tensor(out=g3v(fco), in0=g3v(fco), in1=dtot.unsqueeze(1).to_broadcast([16, 3, 4]), op=ALU.mult),
           [], [T_sm16])
        dv(lambda e: e.tensor_tensor(out=stok, in0=stok, in1=fco.unsqueeze(2).to_broadcast([16, 12, 129]), op=ALU.mult),
           [T_sm16], [T_stok])
        p, pk = psn()

        def f(e, p=p):
            ins = None
            for hh in range(12):
                ins = e.matmul(p[:, hh * 16:(hh + 1) * 16], stok[:, hh, 0:128], ident[0:16, 0:16], start=True, stop=True)
            return ins
        S.op("pe", f, reads=[T_stok, T_cf], writes=[pk])
        S.op("dve", lambda e, p=p: e.tensor_copy(out=wT, in_=p[:, 0:192].rearrange("p (h b) -> p h b", b=16)),
             reads=[pk], writes=[T_wT])

        def f(e):
            ins = None
            for bp in range(16):
                ins = e.tensor_tensor(out=wTm[:, :, bp, :], in0=wT,
                                      in1=i16[:, bp * 16:(bp + 1) * 16].unsqueeze(1).to_broadcast([128, 12, 16]), op=ALU.mult)
            return ins
        S.op("dve", f, reads=[T_wT, T_cf], writes=[T_wTm])
        pO, pOk = PSD
        it = 0
        for b in range(NS):
            for g in range(3):
                d = DIL[g]
                vt, vtk = vh[it % 3]
                it += 1
                S.dma("pool", vt, ckv[g][b, :, 512:1024].rearrange("(j x) c -> j x c", x=d)[:, 0, :], writes=[vtk])

                def f(e, vt=vt, b=b, g=g):
                    ins = None
                    for hh in range(4):
                        ins = e.matmul(pO[0:16, hh * 128:(hh + 1) * 128], wTm[:, g * 4 + hh, b, :], vt[:, hh * 128:(hh + 1) * 128],
                                       start=(b == 0 and g == 0), stop=(b == NS - 1 and g == 2))
                    return ins
                S.op("pe", f, reads=[vtk, T_wTm], writes=[pOk])

        dv(lambda e: e.tensor_tensor(out=prodt.rearrange("p (h x) -> p h x", x=128),
                                     in0=tokq[:, 2, :].rearrange("p (h x) -> p h x", x=128),
                                     in1=stok[:, :, 128:129].to_broadcast([16, 12, 128]), op=ALU.mult),
           [T_stok, T_tokq], [T_prodt])
        dv(lambda e: e.tensor_tensor(out=otok, in0=pO[0:16, :], in1=prodt[:, 0:512], op=ALU.add), [pOk, T_prodt], [T_otok])
        dv(lambda e: e.tensor_tensor(out=otok, in0=otok, in1=prodt[:, 512:1024], op=ALU.add), [T_prodt], [T_otok])
        dv(lambda e: e.tensor_tensor(out=otok, in0=otok, in1=prodt[:, 1024:1536], op=ALU.add), [T_prodt], [T_otok])
        p, pk = psn()

        def f(e, p=p):
            ins = None
            for hh in range(4):
                ins = e.matmul(p[:, hh * 16:(hh + 1) * 16], otok[:, hh * 128:(hh + 1) * 128], ident[0:16, 0:16], start=True, stop=True)
            return ins
        S.op("pe", f, reads=[T_otok, T_cf], writes=[pk])
        S.op("dve", lambda e, p=p: e.tensor_copy(out=oaTs, in_=p[:, 0:64].rearrange("p (h b) -> p h b", b=16)),
             reads=[pk], writes=[T_oas])
        S.barrier(exclude=(T_cp,))
        AR.release(msa)
        if stage <= 5:
            raise _Stop()

        class Chunk:
            pass

        def resid_update(ck, p, pk, fc, GT):
            n = ck.n
            if not ck.sample:
                S.op("dve", lambda e: e.scalar_tensor_tensor(out=ck.x1(fc), in0=p[:, 0:n], scalar=modT[:, GT + fc, 16:17],
                                                             in1=ck.x1(fc), op0=ALU.mult, op1=ALU.add),
                     reads=[pk, T_mod], writes=[ck.T_x1])
            else:
                tt, ttk = tmpn()

                S.op("dve", lambda e: e.tensor_tensor(out=tt[:, 0:n], in0=p[:, 0:n], in1=modT[:, GT + fc, 0:16], op=ALU.mult),
                     reads=[pk, T_mod], writes=[ttk])
                S.op("dve", lambda e: e.tensor_tensor(out=ck.x1(fc), in0=ck.x1(fc), in1=tt[:, 0:n], op=ALU.add),
                     reads=[ttk], writes=[ck.T_x1])

        for tile in range(2):
            t0 = tile * 1024
            mt = AR.mark()
            mrt = AR.rmark()
            mixT = AR.alloc([128, 16, 1024], BF16, "mixT")
            T_mix = [Tok("mixT%d_%d" % (tile, c)) for c in range(2)]
            mh = AR.mark()
            hT2 = AR.alloc([128, 16, 1024], BF16, "hT2")
            T_h2 = [Tok("hT2%d_%d" % (tile, c)) for c in range(2)]
            T_x1 = [Tok("x1T%d_%d" % (tile, c)) for c in range(2)]
            T_aT = [Tok("aT%d_%d" % (tile, i)) for i in range(2)]
            T_aTs = [Tok("aTs%d_%d" % (tile, i)) for i in range(2)]
            def mk_chunks(hbuf, x1buf, aTl, aTsl, t0=t0, tile=tile, mixT=mixT, T_mix=T_mix, T_h2=T_h2, T_x1=T_x1,
                          T_aT=T_aT, T_aTs=T_aTs):
                chunks = []
                for c in range(2):
                    ck = Chunk()
                    ck.sample = False
                    ck.n = 512
                    ck.tok0 = t0 + c * 512
                    ck.x1 = lambda kc, c=c: x1buf[:, kc, c * 512:(c + 1) * 512]
                    ck.T_x1 = T_x1[c]
                    ck.h = lambda kc, c=c: hbuf[:, kc, c * 512:(c + 1) * 512]
                    ck.T_h = T_h2[c]
                    ck.mix = lambda kc, c=c: mixT[:, kc, c * 512:(c + 1) * 512]
                    ck.T_mix = T_mix[c]
                    ck.oa = lambda kc, ck=ck: oaT[:, kc, ck.tok0:ck.tok0 + 512]
                    ck.T_oa = T_oa
                    ck.pp = lambda kc, ck=ck: pT[:, kc, ck.tok0:ck.tok0 + 512]
                    ck.T_pp = T_pT
                    ck.a = lambda i, cc, c=c: aTl[i][:, cc, c * 512:(c + 1) * 512]
                    ck.T_a = T_aT
                    chunks.append(ck)
                if tile == 0:
                    ck = Chunk()
                    ck.sample = True
                    ck.n = NS
                    ck.x1 = lambda kc: x1Ts[:, kc, :]
                    ck.T_x1 = T_x1s
                    ck.h = lambda kc: hTs[:, kc, :]
                    ck.T_h = T_hs
                    ck.mix = lambda kc: mixTs[:, kc, :]
                    ck.T_mix = T_mixs
                    ck.oa = lambda kc: oaTs[:, kc, :]
                    ck.T_oa = [T_oas] * 4
                    ck.pp = lambda kc: pTs[:, kc, :]
                    ck.T_pp = [T_pTs] * 4
                    ck.a = lambda i, cc: aTsl[i][:, cc, :]
                    ck.T_a = T_aTs
                    chunks.append(ck)
                return chunks

            chunks = mk_chunks(hT2, None, None, None)

            mx_ = AR.mark()
            xst = make_xst(2)
            for c in range(2):
                for sub in range(4):
                    tk = t0 + c * 512 + sub * 128
                    build_h(xst, xp[tk:tk + 128, :], 128,
                            h_dst=lambda kc, c=c, sub=sub: hT2[:, kc, c * 512 + sub * 128:c * 512 + (sub + 1) * 128],
                            T_h=T_h2[c])
            S.barrier(exclude=(T_cp,))
            AR.release(mx_)

            jobs = []
            for fc in range(16):
                for br in range(2):
                    def loads(sl, fc=fc, br=br):
                        return [(sl[:, 0:2048].rearrange("p (k c) -> p k c", c=128),
                                 rows_view(w_in, 0, 16, (GA0, GB0)[br] + fc * 128, 128)),
                                (sl[:, 2048:2560].rearrange("p (k c) -> p k c", c=128),
                                 rows_view((w_ua, w_up)[br], 0, 4, fc * 128, 128))]

                    def comp(sl, tok, fc=fc, br=br):
                        wg = sl[:, 0:2048].rearrange("p (k c) -> p k c", c=128)
                        wu = sl[:, 2048:2560].rearrange("p (k c) -> p k c", c=128)
                        for ck in chunks:
                            n = ck.n
                            pG, pGk = psn()
                            pU, pUk = psn()

                            def f(e, pG=pG, ck=ck, n=n):
                                ins = None
                                for kc in range(16):
                                    ins = e.matmul(pG[:, 0:n], wg[:, kc, :], ck.h(kc), start=(kc == 0), stop=(kc == 15))
                                return ins
                            S.op("pe", f, reads=[tok, ck.T_h], writes=[pGk])
                            src = ck.oa if br == 0 else ck.pp
                            srck = ck.T_oa if br == 0 else ck.T_pp

                            def f(e, pU=pU, src=src, n=n):
                                ins = None
                                for kc in range(4):
                                    ins = e.matmul(pU[:, 0:n], wu[:, kc, :], src(kc), start=(kc == 0), stop=(kc == 3))
                                return ins
                            S.op("pe", f, reads=[tok] + list(srck), writes=[pUk])
                            sg, sgk = tmpn()
                            S.op("act", lambda e, pG=pG, sg=sg, n=n: e.activation(out=sg[:, 0:n], in_=pG[:, 0:n], func=AF.Sigmoid),
                                 reads=[pGk], writes=[sgk])
                            if br == 0:
                                S.op("dve", lambda e, pU=pU, sg=sg, n=n, ck=ck: e.tensor_tensor(
                                    out=ck.mix(fc), in0=sg[:, 0:n], in1=pU[:, 0:n], op=ALU.mult),
                                    reads=[pUk, sgk], writes=[ck.T_mix])
                            else:
                                S.op("dve", lambda e, pU=pU, sg=sg, n=n: e.tensor_tensor(
                                    out=sg[:, 0:n], in0=sg[:, 0:n], in1=pU[:, 0:n], op=ALU.mult), reads=[pUk], writes=[sgk])
                                S.op("dve", lambda e, sg=sg, n=n, ck=ck: e.tensor_tensor(
                                    out=ck.mix(fc), in0=ck.mix(fc), in1=sg[:, 0:n], op=ALU.add), reads=[sgk], writes=[ck.T_mix])
                    jobs.append((loads, comp))
            run_jobs(jobs)
            S.barrier(exclude=(T_cp,))
            AR.release(mh)

            x1T = AR.alloc([128, 16, 1024], F32, "x1T", side="R")
            chunks = mk_chunks(None, x1T, None, None)
            mx_ = AR.mark()
            xst = make_xst(2)
            for c in range(2):
                for sub in range(4):
                    tk = t0 + c * 512 + sub * 128
                    build_h(xst, xp[tk:tk + 128, :], 128,
                            x_dst=lambda kq, c=c, sub=sub: x1T[:, kq * 4:(kq + 1) * 4, c * 512 + sub * 128:c * 512 + (sub + 1) * 128],
                            T_x=T_x1[c])
            S.barrier(exclude=(T_cp,))
            AR.release(mx_)

            jobs = []
            for jc in range(8):
                def loads(sl, jc=jc):
                    return [(wview(sl, 16, 256), rows_view(w_out, 0, 16, jc * 256, 256))]

                def comp(sl, tok, jc=jc):
                    wv = wview(sl, 16, 256)
                    for ff in range(2):
                        fc = jc * 2 + ff
                        for ck in chunks:
                            n = ck.n
                            p, pk = psn()

                            def f(e, p=p, ck=ck, n=n, ff=ff):
                                ins = None
                                for kc in range(16):
                                    ins = e.matmul(p[:, 0:n], wv[:, kc, ff * 128:(ff + 1) * 128], ck.mix(kc),
                                                   start=(kc == 0), stop=(kc == 15))
                                return ins
                            S.op("pe", f, reads=[tok, ck.T_mix], writes=[pk])
                            resid_update(ck, p, pk, fc, GT1)
                jobs.append((loads, comp))
            run_jobs(jobs)
            S.barrier(exclude=(T_cp,))
            AR.release(mt)

            h2T = AR.alloc([128, 16, 1024], BF16, "h2T")
            aTl = [AR.alloc([128, 2, 1024], BF16, "aT%d" % i) for i in range(2)]
            aTsl = [AR.alloc([128, 2, NS], BF16, "aTs%d" % i) for i in range(2)]
            chunks = mk_chunks(h2T, x1T, aTl, aTsl)
            rstdb = AR.alloc([128, 512], F32, "rstdb")
            T_rstdb = Tok("rstdb%d" % tile)
            yst = [(AR.alloc([128, 512], F32, "yst%d" % i), T_yst[i]) for i in range(3)]
            ysi = [0]

            for ck in chunks:
                n = ck.n
                pS_, pSk_ = psn()
                for kc in range(16):
                    sq, sqk = tmpbn()
                    S.op("act", lambda e, sq=sq, ck=ck, kc=kc, n=n: e.activation(out=sq[:, 0:n], in_=ck.x1(kc), func=AF.Square),
                         reads=[ck.T_x1], writes=[sqk])
                    S.op("pe", lambda e, sq=sq, kc=kc, n=n, pS_=pS_: e.matmul(pS_[:, 0:n], onesb, sq[:, 0:n], start=(kc == 0),
                                                                               stop=(kc == 15)), reads=[sqk, T_cb], writes=[pSk_])
                rstd_from_sum(rstdb[:, 0:n], pS_[:, 0:n], [pSk_], [T_rstdb])
                for kc in range(16):
                    tt, ttk = tmpn()
                    S.op("dve", lambda e, tt=tt, ck=ck, kc=kc, n=n: e.tensor_tensor(out=tt[:, 0:n], in0=ck.x1(kc), in1=rstdb[:, 0:n],
                                                                                    op=ALU.mult),
                         reads=[ck.T_x1, T_rstdb], writes=[ttk])
                    if not ck.sample:
                        S.op("act", lambda e, tt=tt, ck=ck, kc=kc, n=n: e.activation(
                            out=ck.h(kc), in_=tt[:, 0:n], func=AF.Identity, bias=modT[:, SH2 + kc, 16:17], scale=A2[:, kc, 16:17]),
                            reads=[ttk, T_A, T_mod], writes=[ck.T_h])
                    else:
                        S.op("dve", lambda e, tt=tt, kc=kc, n=n: e.tensor_tensor(out=tt[:, 0:n], in0=tt[:, 0:n],
                                                                                 in1=A2[:, kc, 0:16], op=ALU.mult),
                             reads=[T_A], writes=[ttk])
                        S.op("dve", lambda e, tt=tt, ck=ck, kc=kc, n=n: e.tensor_tensor(
                            out=ck.h(kc), in0=tt[:, 0:n], in1=modT[:, SH2 + kc, 0:16], op=ALU.add),
                            reads=[ttk, T_mod], writes=[ck.T_h])

            NG = DFF // 256
            jobs = []

            def up_job(g):
                def loads(sl):
                    return [(wview(sl, 16, 256), rows_view(w_mu, 0, 16, g * 256, 256))]

                def comp(sl, tok):
                    wv = wview(sl, 16, 256)
                    for cc in range(2):
                        for ck in chunks:
                            n = ck.n
                            p, pk = psn()

                            def f(e, p=p, ck=ck, n=n, cc=cc):
                                ins = None
                                for kc in range(16):
                                    ins = e.matmul(p[:, 0:n], wv[:, kc, cc * 128:(cc + 1) * 128], ck.h(kc),
                                                   start=(kc == 0), stop=(kc == 15))
                                return ins
                            S.op("pe", f, reads=[tok, ck.T_h], writes=[pk])
                            rl, rlk = tmpbn()
                            S.op("act", lambda e, p=p, rl=rl, n=n: e.activation(out=rl[:, 0:n], in_=p[:, 0:n], func=AF.Relu),
                                 reads=[pk], writes=[rlk])
                            S.op("dve", lambda e, rl=rl, ck=ck, n=n, cc=cc: e.tensor_tensor(
                                out=ck.a(g % 2, cc), in0=rl[:, 0:n], in1=rl[:, 0:n], op=ALU.mult),
                                reads=[rlk], writes=[ck.T_a[g % 2]])
                return (loads, comp)

            def down_job(g):
                def loads(sl):
                    return [(wview(sl, 2, 2048), rows_view(w_md, g * 256, 2, 0, 2048))]

                def comp(sl, tok):
                    wv = wview(sl, 2, 2048)
                    for fc in range(16):
                        for ck in chunks:
                            n = ck.n
                            p, pk = psn()

                            def f(e, p=p, ck=ck, n=n, fc=fc):
                                ins = None
                                for cc in range(2):
                                    ins = e.matmul(p[:, 0:n], wv[:, cc, fc * 128:(fc + 1) * 128], ck.a(g % 2, cc),
                                                   start=(cc == 0), stop=(cc == 1))
                                return ins
                            S.op("pe", f, reads=[tok, ck.T_a[g % 2]], writes=[pk])
                            resid_update(ck, p, pk, fc, GT2)
                return (loads, comp)

            for g in range(NG):
                jobs.append(up_job(g))
                if g > 0:
                    jobs.append(down_job(g - 1))
            jobs.append(down_job(NG - 1))
            run_jobs(jobs)

            for ck in chunks:
                n = ck.n
                for kc in range(16):
                    S.op("act", lambda e, ck=ck, kc=kc: e.activation(out=ck.h(kc), in_=ck.x1(kc), func=AF.Square),
                         reads=[ck.T_x1], writes=[ck.T_h])
                    S.op("dve", lambda e, ck=ck, kc=kc: e.tensor_scalar(out=ck.x1(kc), in0=ck.x1(kc), scalar1=gfT[:, kc:kc + 1],
                                                                        scalar2=None, op0=ALU.mult),
                         reads=[ck.T_h, T_vT], writes=[ck.T_x1])
                nsub = 1 if ck.sample else 4
                m = NS if ck.sample else 128
                for sub in range(nsub):
                    pS_, pSk_ = psn()

                    def f(e, pS_=pS_, ck=ck, sub=sub, m=m):
                        ins = None
                        for kc in range(16):
                            ins = e.matmul(pS_[0:m, 0:1], ck.h(kc)[:, sub * m:(sub + 1) * m], onesb[:, 0:1],
                                           start=(kc == 0), stop=(kc == 15))
                        return ins
                    S.op("pe", f, reads=[ck.T_h, T_cb], writes=[pSk_])
                    rs, rsk = bhsm[bhstate[0] % 2]
                    bhstate[0] += 1
                    rstd_from_sum(rs[0:m, 0:1], pS_[0:m, 0:1], [pSk_], [rsk])
                    for nq in range(4):
                        p, pk = psn()

                        def f(e, p=p, ck=ck, sub=sub, m=m, nq=nq):
                            ins = None
                            for jj in range(4):
                                kc = nq * 4 + jj
                                ins = e.matmul(p[0:m, jj * 128:(jj + 1) * 128], ck.x1(kc)[:, sub * m:(sub + 1) * m], ident,
                                               start=True, stop=True)
                            return ins
                        S.op("pe", f, reads=[ck.T_x1, T_cf], writes=[pk])
                        st, stk = yst[ysi[0] % 3]
                        ysi[0] += 1
                        S.op("act", lambda e, p=p, st=st, rs=rs, m=m: e.activation(out=st[0:m, :], in_=p[0:m, :], func=AF.Identity,
                                                                                   scale=rs[0:m, 0:1]),
                             reads=[pk, rsk], writes=[stk])
                        if ck.sample:
                            S.dma("sp", ys[:, nq * 512:(nq + 1) * 512], st[0:m, :], reads=[stk], owner=stk, final=True)
                        else:
                            r0 = ck.tok0 + sub * 128
                            S.dma("sp", yp[r0:r0 + 128, nq * 512:(nq + 1) * 512], st[:, :], reads=[stk], owner=stk, final=True)
            S.barrier(exclude=(T_cp,))
            AR.release(mt)
            AR.rrelease(mrt)

    except _Stop:
        pass
    drip(10000)
    with nc.Block() as block:
        S.emit(block)
    es.close()
    return nc


_CACHE = {}


def kernel(**inp):
    f32 = lambda a: np.ascontiguousarray(np.asarray(a, dtype=np.float32))
    CF, MW, BS = host_consts()
    vecs = np.concatenate([f32(inp["norm_mix_g"]).reshape(16, 128), f32(inp["norm_mlp_g"]).reshape(16, 128),
                           f32(inp["norm_final_g"]).reshape(16, 128), f32(inp["pool_scale"]).reshape(4, 128)], axis=0)
    shared = {
        "w_ada": f32(inp["w_ada"])[0], "b_ada": f32(inp["b_ada"]).reshape(96, 128), "w_in": f32(inp["w_in"])[0],
        "w_ua": f32(inp["w_up_attn"])[0], "w_pool": f32(inp["w_pool"])[0], "vecs": f32(vecs),
        "w_up": f32(inp["w_up_pool"])[0], "w_out": f32(inp["w_out"])[0], "w_mu": f32(inp["w_mlp_up"])[0],
        "w_md": f32(inp["w_mlp_down"])[0], "cf": CF, "mw": MW, "bs": BS,
    }
    xp = f32(inp["x_prompt"])
    xs = f32(inp["x_sample"])[:, 0, :]
    cp = f32(inp["c_prompt"])
    cs = f32(inp["c_sample"])
    c0 = f32(inp["cache_kv_w128"])[0]
    c1 = f32(inp["cache_kv_w512"])[0]
    c2 = f32(inp["cache_kv_w2048"])[0]
    sp = f32(inp["state_pool"])[0]
    in_maps = []
    for c in range(NCORES):
        sl = slice(c * NS, (c + 1) * NS)
        m = dict(shared)
        m["xp"] = xp[c]
        m["xs"] = xs[sl]
        m["call"] = np.concatenate([cs[sl], cp[c:c + 1]], axis=0)
        m["ckv0"] = c0[sl].reshape(NS, 128, 1024)
        m["ckv1"] = c1[sl].reshape(NS, 512, 1024)
        m["ckv2"] = c2[sl].reshape(NS, 2048, 1024)
        m["spool"] = sp[sl].reshape(NS * 15, 512)
        in_maps.append(m)
    if "nc" not in _CACHE:
        _CACHE["nc"] = build_program()
    res = run_bass_kernel_spmd(_CACHE["nc"], in_maps, core_ids=list(range(NCORES)))
    R = res.results
    cat = lambda k: np.stack([np.asarray(R[c][k], dtype=np.float32) for c in range(NCORES)], axis=0)
    y_p = cat("yp")
    y_s = np.concatenate([np.asarray(R[c]["ys"], np.float32) for c in range(NCORES)], axis=0).reshape(128, 1, D)
    kvp0 = cat("kvp0").reshape(1, 8, 128, 2, 4, 128)
    kvp1 = cat("kvp1").reshape(1, 8, 512, 2, 4, 128)
    kvp2 = cat("kvp2").reshape(1, 8, 2048, 2, 4, 128)
    poolp = cat("poolp").reshape(1, 8, 15, 512)
    ccat = lambda k: np.concatenate([np.asarray(R[c][k], np.float32) for c in range(NCORES)], axis=0)
    kvs0 = ccat("kvs0").reshape(1, 128, 128, 2, 4, 128)
    kvs1 = ccat("kvs1").reshape(1, 128, 512, 2, 4, 128)
    kvs2 = ccat("kvs2").reshape(1, 128, 2048, 2, 4, 128)
    pools = ccat("pools").reshape(1, 128, 15, 512)
    return (y_p, y_s, kvp0, kvp1, kvp2, poolp, kvs0, kvs1, kvs2, pools)
```

```python
import contextlib
import numpy as np
import concourse.bass as bass
import concourse.mybir as mybir
from concourse.bass_utils import run_bass_kernel_spmd

F32 = mybir.dt.float32
BF16 = mybir.dt.bfloat16
AF = mybir.ActivationFunctionType
ALU = mybir.AluOpType
AX = mybir.AxisListType

NCORES = 8
D = 2048
NT = 2048
NS = 16
KC = 16
DFF = 8192
Q0, K0, V0, U0, GA0, GB0 = 0, 1536, 3072, 4608, 5120, 7168
EPS = 1e-6
SCALE = 128.0 ** -0.5
DIL = (1, 4, 16)
NCF = 588

_slopes = 2.0 ** (-8.0 * np.arange(1, 13) / 12.0)
_dilh = np.array([1, 1, 1, 1, 4, 4, 4, 4, 16, 16, 16, 16], np.float64)
_ah = _slopes * _dilh


def host_consts():
    CF = np.zeros((128, NCF), np.float32)
    CF[:, 0:128] = np.eye(128)
    j = np.arange(128)
    CF[:, 128:140] = (j[:, None] - 64) * _ah[None, :]
    for g, w in enumerate((2, 4, 8, 16)):
        pos = np.arange(16)
        CF[:, 140 + g * 16:140 + (g + 1) * 16] = 1.0 / np.minimum(pos + 1, w)
    for g, w in enumerate((2, 4, 8, 16)):
        for half in range(2):
            for bl in range(8):
                for row in range(15 - (w - 1), 15):
                    CF[bl * 15 + row, 204 + (g * 2 + half) * 16 + half * 8 + bl] = 1.0
    CF[:, 332:588] = np.eye(16).reshape(1, 256)
    MW = np.zeros((128, 12, 256), np.float32)
    i = np.arange(128)
    for h in range(12):
        MW[:, h, 0:128] = (j[:, None] <= i[None, :]) * np.exp(-_ah[h] * (i[None, :] - 64))
        MW[:, h, 128:256] = (j[:, None] >= i[None, :]) * np.exp(-_ah[h] * (i[None, :] - 64) - 128 * _ah[h])
    BS = np.zeros((16, 12, 129), np.float32)
    pos = np.arange(129)
    for h in range(12):
        BS[:, h, :] = -_ah[h] * (128 - pos)
    return CF, MW.reshape(128, 12 * 256), BS.reshape(16, 12 * 129)


class Tok:
    __slots__ = ("name", "w", "r", "sem", "dcnt")

    def __init__(self, name):
        self.name = name
        self.w = None
        self.r = {}
        self.sem = None
        self.dcnt = 0


class Sched:
    ENG = ("pe", "act", "dve", "pool", "sp")

    def __init__(self, nc, es):
        self.nc = nc
        self.es = es
        self.ops = {e: [] for e in self.ENG}
        self.cnt = {e: 0 for e in self.ENG}
        self.waited = {e: {} for e in self.ENG}
        self.esem = {e: es.enter_context(nc.semaphore("sem_" + e)) for e in self.ENG if e != "sp"}
        self.nsem = 0
        self.final = {}
        self.owners = []
        self.pending = {e: [] for e in self.ENG}

    def barrier(self, exclude=()):
        targets = {("e", e): self.cnt[e] for e in self.ENG if e != "sp"}
        for o in self.owners:
            if o not in exclude:
                targets[("d", o)] = o.dcnt
        for en in self.ENG:
            wd = self.waited[en]
            for k, v in targets.items():
                if v > 0 and k != ("e", en) and wd.get(k, 0) < v:
                    wd[k] = v
                    self.pending[en].append((k, v))

    def _collect(self, eng, reads, writes):
        need = {}

        def add(key, val):
            if need.get(key, 0) < val:
                need[key] = val

        for t in reads:
            if t.w is not None:
                add(*t.w)
        for t in writes:
            if t.w is not None:
                add(*t.w)
            for k, v in t.r.items():
                add(k, v)
        waits = []
        wd = self.waited[eng]
        for key, val in need.items():
            if key == ("e", "pe") and eng == "pe":
                continue
            if wd.get(key, 0) >= val:
                continue
            wd[key] = val
            waits.append((key, val))
        return waits

    def _commit(self, ev, reads, writes):
        for t in writes:
            t.w = ev
            t.r = {}
        for t in reads:
            if t not in writes:
                if t.r.get(ev[0], 0) < ev[1]:
                    t.r[ev[0]] = ev[1]

    def op(self, eng, fn, reads=(), writes=()):
        waits = self._collect(eng, reads, writes)
        self.cnt[eng] += 1
        ev = (("e", eng), self.cnt[eng])
        self._commit(ev, reads, writes)
        waits = self.pending[eng] + waits
        self.pending[eng] = []
        self.ops[eng].append((waits, fn, ("e", eng)))

    def dma(self, q, out, in_, reads=(), writes=(), owner=None, final=False):
        if owner is None:
            owner = writes[0] if writes else reads[0]
        if owner.sem is None:
            owner.sem = self.es.enter_context(self.nc.semaphore("dsem%d" % self.nsem))
            self.nsem += 1
            self.owners.append(owner)
        waits = self._collect(q, reads, writes)
        owner.dcnt += 16
        ev = (("d", owner), owner.dcnt)
        self._commit(ev, reads, writes)
        if final:
            self.final[ev[0]] = ev[1]
        waits = self.pending[q] + waits
        self.pending[q] = []
        self.ops[q].append((waits, lambda e, o=out, i=in_: e.dma_start(out=o, in_=i), ("d", owner)))

    def _sem(self, key):
        return self.esem[key[1]] if key[0] == "e" else key[1].sem

    def emit(self, block):
        engmap = {"pe": block.tensor, "act": block.scalar, "dve": block.vector,
                  "pool": block.gpsimd, "sp": block.sync}
        for en in self.ENG:
            ops = self.ops[en]
            fin = self.final if en == "sp" else None
            pend = self.pending[en]

            def body(eng, ops=ops, fin=fin, pend=pend):
                for waits, fn, sig in ops:
                    for key, val in waits:
                        eng.wait_ge(self._sem(key), val)
                    ins = fn(eng)
                    if sig[0] == "e":
                        ins.then_inc(self.esem[sig[1]], 1)
                    else:
                        ins.then_inc(sig[1].sem, 16)
                for key, val in pend:
                    eng.wait_ge(self._sem(key), val)
                if fin is not None:
                    for key, val in fin.items():
                        eng.wait_ge(self._sem(key), val)

            engmap[en](body)


class Arena:
    def __init__(self, tensor, nbytes):
        self.t = tensor
        self.cap = nbytes
        self.top = 0
        self.rtop = nbytes

    def mark(self):
        return self.top

    def release(self, m):
        self.top = m

    def rmark(self):
        return self.rtop

    def rrelease(self, m):
        self.rtop = m

    def alloc(self, shape, dtype, name="", side="L"):
        esz = 4 if dtype == F32 else 2
        n = 1
        for s in shape[1:]:
            n *= s
        nb = (n * esz + 31) // 32 * 32
        if side == "L":
            off = self.top
            self.top += nb
        else:
            self.rtop -= nb
            off = self.rtop
        assert self.top <= self.rtop, "SBUF arena overflow at %s: L=%d R=%d" % (name, self.top, self.rtop)
        ap = self.t[0:shape[0], off // 2: off // 2 + n * esz // 2]
        if dtype == F32:
            ap = ap.bitcast(F32)
        if len(shape) == 3:
            ap = ap.rearrange("p (a b) -> p a b", b=shape[2])
        elif len(shape) == 4:
            ap = ap.rearrange("p (a b c) -> p a b c", b=shape[2], c=shape[3])
        return ap


class _Stop(Exception):
    pass


import os as _os2
SKIP = int(_os2.environ.get('SKIP', '0'))


def build_program(stage=99):
    nc = bass.Bass("TRN2", target_bir_lowering=False)
    es = contextlib.ExitStack()

    def din(name, shape):
        return nc.dram_tensor(name, list(shape), F32, kind="ExternalInput").ap()

    def dout(name, shape):
        return nc.dram_tensor(name, list(shape), F32, kind="ExternalOutput").ap()

    xp = din("xp", (NT, D))
    xs = din("xs", (NS, D))
    call = din("call", (17, D))
    ckv = [din("ckv0", (NS, 128, 1024)), din("ckv1", (NS, 512, 1024)), din("ckv2", (NS, 2048, 1024))]
    spool = din("spool", (NS * 15, 512))
    w_ada = din("w_ada", (D, 6 * D))
    b_ada = din("b_ada", (96, 128))
    w_in = din("w_in", (D, 9216))
    w_ua = din("w_ua", (512, D))
    w_pool = din("w_pool", (4, 128, 128))
    vecs = din("vecs", (52, 128))
    w_up = din("w_up", (512, D))
    w_out = din("w_out", (D, D))
    w_mu = din("w_mu", (D, DFF))
    w_md = din("w_md", (DFF, D))
    cfd = din("cf", (128, NCF))
    mwd = din("mw", (128, 12 * 256))
    bsd = din("bs", (16, 12 * 129))

    yp = dout("yp", (NT, D))
    ys = dout("ys", (NS, D))
    kvp = [dout("kvp0", (128, 1024)), dout("kvp1", (512, 1024)), dout("kvp2", (2048, 1024))]
    poolp = dout("poolp", (15, 512))
    kvs = [dout("kvs0", (NS, 128, 1024)), dout("kvs1", (NS, 512, 1024)), dout("kvs2", (NS, 2048, 1024))]
    pools = dout("pools", (NS, 15, 512))
    qscr = nc.dram_tensor("qscr", [NS, 1536], F32, kind="Internal").ap()

    ARENA_BYTES = 207 * 1024
    arena_t = es.enter_context(nc.sbuf_tensor("arena", [128, ARENA_BYTES // 2], BF16))
    AR = Arena(arena_t, ARENA_BYTES)
    S = Sched(nc, es)
    psb = []
    for i in range(8):
        t = es.enter_context(nc.psum_tensor("ps%d" % i, [128, 512], F32))
        psb.append((t, Tok("ps%d" % i)))
    pstate = [0]

    def psn():
        t = psb[pstate[0] % 6]
        pstate[0] += 1
        return t

    PSD = psb[6]
    PSE = psb[7]

    cf = AR.alloc([128, NCF], F32, "cf")
    T_cf = Tok("cf")
    ident = cf[:, 0:128]
    cbias = cf[:, 128:140]
    invc = cf[:, 140:204]
    selw = cf[:, 204:332]
    i16 = cf[:, 332:588]
    identb = AR.alloc([128, 128], BF16, "identb")
    onesb = AR.alloc([128, 128], BF16, "onesb")
    T_cb = Tok("constb")
    vT1 = AR.alloc([128, 96], F32, "vT1")
    vT2 = AR.alloc([128, 52], F32, "vT2")
    T_vT = Tok("vT")
    modT = AR.alloc([128, 96, 17], F32, "modT")
    T_mod = Tok("modT")
    A1 = AR.alloc([128, 16, 17], F32, "A1")
    A2 = AR.alloc([128, 16, 17], F32, "A2")
    T_A = Tok("A")
    wpool = AR.alloc([128, 4, 128], BF16, "wpool")
    T_wpool = Tok("wpool")
    oaT = AR.alloc([128, 4, NT], BF16, "oaT")
    T_oa = [Tok("oa%d" % s) for s in range(4)]
    oaTs = AR.alloc([128, 4, NS], BF16, "oaTs")
    T_oas = Tok("oaTs")
    pTs = AR.alloc([128, 4, NS], BF16, "pTs")
    T_pTs = Tok("pTs")
    x1Ts = AR.alloc([128, 16, NS], F32, "x1Ts")
    T_x1s = Tok("x1Ts")
    hTs = AR.alloc([128, 16, NS], BF16, "hTs")
    T_hs = Tok("hTs")
    mixTs = AR.alloc([128, 16, NS], BF16, "mixTs")
    T_mixs = Tok("mixTs")
    smallf = AR.alloc([128, 64], F32, "smallf")
    T_small = Tok("smallf")
    WSLOT = 8 * 1024
    wsl = [(AR.alloc([128, WSLOT // 2], BF16, "ws%d" % i), Tok("ws%d" % i)) for i in range(3)]
    wstate = [0]
    tmpf = [(AR.alloc([128, 512], F32, "tmpf%d" % i), Tok("tmpf%d" % i)) for i in range(3)]
    tstate = [0]

    def tmpn():
        t = tmpf[tstate[0] % 3]
        tstate[0] += 1
        return t

    tmpb = [(AR.alloc([128, 512], BF16, "tmpb%d" % i), Tok("tmpb%d" % i)) for i in range(3)]
    bstate = [0]

    def tmpbn():
        t = tmpb[bstate[0] % 3]
        bstate[0] += 1
        return t

    bhsm = [(AR.alloc([128, 160], F32, "bhsm%d" % i), Tok("bhsm%d" % i)) for i in range(2)]
    bhstate = [0]
    qkvs = AR.alloc([128, 40, NS], F32, "qkvs")
    T_qkvs = Tok("qkvs")

    T_cp = Tok("cpy")
    cplist = []
    for g, W in enumerate((128, 512, 2048)):
        for b in range(NS):
            nrow = W - 1
            r = 0
            while r < nrow:
                n = min(512, nrow - r)
                cplist.append((kvs[g][b, r:r + n, :], ckv[g][b, r + 1:r + 1 + n, :]))
                r += n
    spool3 = spool.rearrange("(b r) c -> b r c", r=15)
    for b in range(NS):
        cplist.append((pools[b, 0:14, :], spool3[b, 1:15, :]))
    cpstate = [0]

    def drip(n=1):
        for _ in range(n):
            if cpstate[0] < len(cplist):
                o, i_ = cplist[cpstate[0]]
                cpstate[0] += 1
                S.dma("act", o, i_, owner=T_cp, final=True)

    def run_jobs(jobs):
        n = len(jobs)
        base = wstate[0]

        def issue(k):
            sl, tok = wsl[(base + k) % 3]
            for o, i_ in jobs[k][0](sl):
                S.dma("pool", o, i_, writes=[tok])

        for k in range(min(2, n)):
            issue(k)
        for k in range(n):
            if k + 2 < n:
                issue(k + 2)
            sl, tok = wsl[(base + k) % 3]
            drip(1)
            jobs[k][1](sl, tok)
        wstate[0] = (base + n) % 3

    def wview(sl, kk, cols):
        return sl[:, 0:kk * cols].rearrange("p (k c) -> p k c", c=cols)

    def rows_view(w, r0, nk, c0, cols):
        return w[r0:r0 + nk * 128, c0:c0 + cols].rearrange("(k p) c -> p k c", p=128)

    try:
        S.dma("sp", cf, cfd, writes=[T_cf])
        S.dma("pool", wpool, w_pool.rearrange("g c d -> c g d"), writes=[T_wpool])
        S.op("dve", lambda e: e.tensor_copy(out=identb, in_=ident), reads=[T_cf], writes=[T_cb])
        S.op("dve", lambda e: e.memset(onesb, 1.0), writes=[T_cb])

        m0 = AR.mark()
        call_sb = AR.alloc([17, D], F32, "call_sb")
        T_call = Tok("call")
        sT = AR.alloc([128, 16, 17], BF16, "sT")
        T_sT = Tok("sT")
        v1 = AR.alloc([96, 128], F32, "v1")
        v2 = AR.alloc([52, 128], F32, "v2")
        T_v = Tok("v12")
        S.dma("sp", call_sb, call, writes=[T_call])
        S.dma("sp", v1, b_ada, writes=[T_v])
        S.dma("sp", v2, vecs, writes=[T_v])
        S.op("act", lambda e: e.activation(out=call_sb, in_=call_sb, func=AF.Silu), writes=[T_call])
        pt, ptk = psn()

        def f(e):
            ins = None
            for kc in range(16):
                ins = e.matmul(pt[:, kc * 17:(kc + 1) * 17], call_sb[0:17, kc * 128:(kc + 1) * 128],
                               ident[0:17, 0:17], start=True, stop=True)
            return ins
        S.op("pe", f, reads=[T_call, T_cf], writes=[ptk])
        S.op("dve", lambda e: e.tensor_copy(out=sT, in_=pt[:, 0:272].rearrange("p (k c) -> p k c", c=17)),
             reads=[ptk], writes=[T_sT])
        pt2, ptk2 = psn()

        def f(e):
            e.matmul(pt2[:, 0:96], v1[0:96, :], ident[0:96, 0:96], start=True, stop=True)
            return e.matmul(pt2[:, 96:148], v2[0:52, :], ident[0:52, 0:52], start=True, stop=True)
        S.op("pe", f, reads=[T_v, T_cf], writes=[ptk2])

        def f(e):
            e.tensor_copy(out=vT1, in_=pt2[:, 0:96])
            return e.tensor_copy(out=vT2, in_=pt2[:, 96:148])
        S.op("dve", f, reads=[ptk2], writes=[T_vT])
        g1T = vT2[:, 0:16]
        g2T = vT2[:, 16:32]
        gfT = vT2[:, 32:48]
        pscT = vT2[:, 48:52]

        jobs = []
        for j in range(48):
            def loads(sl, j=j):
                return [(wview(sl, 16, 256), rows_view(w_ada, 0, 16, j * 256, 256))]

            def comp(sl, tok, j=j):
                wv = wview(sl, 16, 256)
                p, pk = psn()

                def f(e):
                    ins = None
                    for cc in range(2):
                        for kc in range(16):
                            ins = e.matmul(p[:, cc * 17:(cc + 1) * 17], wv[:, kc, cc * 128:(cc + 1) * 128],
                                           sT[:, kc, :], start=(kc == 0), stop=(kc == 15))
                    return ins
                S.op("pe", f, reads=[tok, T_sT], writes=[pk])

                def f2(e):
                    ins = None
                    for cc in range(2):
                        c = j * 2 + cc
                        ins = e.tensor_scalar(out=modT[:, c, :], in0=p[:, cc * 17:(cc + 1) * 17],
                                              scalar1=vT1[:, c:c + 1], scalar2=None, op0=ALU.add)
                    return ins
                S.op("dve", f2, reads=[pk, T_vT], writes=[T_mod])
            jobs.append((loads, comp))
        run_jobs(jobs)

        def f(e):
            ins = None
            for kc in range(16):
                e.tensor_scalar(out=A1[:, kc, :], in0=modT[:, 16 + kc, :], scalar1=1.0, scalar2=g1T[:, kc:kc + 1],
                                op0=ALU.add, op1=ALU.mult)
                ins = e.tensor_scalar(out=A2[:, kc, :], in0=modT[:, 64 + kc, :], scalar1=1.0, scalar2=g2T[:, kc:kc + 1],
                                      op0=ALU.add, op1=ALU.mult)
            return ins
        S.op("dve", f, reads=[T_mod, T_vT], writes=[T_A])
        S.barrier(exclude=(T_cp,))
        AR.release(m0)
        if stage <= 0:
            raise _Stop()
        SH1, GT1, SH2, GT2 = 0, 32, 48, 80

        def rstd_from_sum(out_ap, in_ap, toks_r, toks_w):
            S.op("dve", lambda e: e.tensor_scalar(out=out_ap, in0=in_ap, scalar1=1.0 / D, scalar2=EPS,
                                                  op0=ALU.mult, op1=ALU.add), reads=toks_r, writes=toks_w)
            S.op("act", lambda e: e.activation(out=out_ap, in_=out_ap, func=AF.Sqrt), writes=toks_w)
            S.op("dve", lambda e: e.reciprocal(out=out_ap, in_=out_ap), writes=toks_w)

        def build_h(xst, xrows, ntok, h_dst=None, T_h=None, x_dst=None, T_x=None, sample=False):
            xt, xtk = xst[0][xst[1][0] % len(xst[0])]
            xst[1][0] += 1
            xv = xt[0:ntok, :]
            S.dma("sp", xv, xrows, writes=[xtk])
            sm, smk = bhsm[bhstate[0] % 2]
            bhstate[0] += 1
            if h_dst is not None:
                jb, jbk = tmpbn()
                ss4 = sm[0:ntok, 0:4]
                ss = sm[0:ntok, 8:9]
                dm = sm[0:ntok, 16:16 + ntok]

                S.op("dve", lambda e: e.memzero(ss4), writes=[smk])
                jb2, jbk2 = tmpbn()
                jbs = [jb[0:ntok, :], jb2[0:ntok, :]]

                def f(e):
                    ins = None
                    for q in range(4):
                        ins = e.activation(out=jbs[q % 2], in_=xv[:, q * 512:(q + 1) * 512],
                                           func=AF.Square, accum_out=ss4[:, q:q + 1])
                    return ins
                S.op("act", f, reads=[xtk], writes=[jbk, jbk2, smk])
                S.op("dve", lambda e: e.tensor_reduce(out=ss, in_=ss4, axis=AX.X, op=ALU.add), writes=[smk])
                rstd_from_sum(ss, ss, [smk], [smk])
                S.op("dve", lambda e: e.tensor_scalar(out=dm, in0=ident[0:ntok, 0:ntok], scalar1=ss, scalar2=None,
                                                      op0=ALU.mult), reads=[T_cf], writes=[smk])
            for kq in range(4):
                if x_dst is not None:
                    p2, pk2 = psn()

                    def f(e, kq=kq, p2=p2):
                        ins = None
                        for jj in range(4):
                            kc = kq * 4 + jj
                            ins = e.matmul(p2[:, jj * ntok:(jj + 1) * ntok], xv[:, kc * 128:(kc + 1) * 128],
                                           ident[0:ntok, 0:ntok], start=True, stop=True)
                        return ins
                    S.op("pe", f, reads=[xtk, T_cf], writes=[pk2])
                    S.op("act", lambda e, kq=kq, p2=p2: e.activation(
                        out=x_dst(kq), in_=p2[:, 0:4 * ntok].rearrange("p (a b) -> p a b", b=ntok), func=AF.Copy),
                        reads=[pk2], writes=[T_x])
                if h_dst is None:
                    continue
                p, pk = psn()

                def f(e, kq=kq, p=p):
                    ins = None
                    for jj in range(4):
                        kc = kq * 4 + jj
                        ins = e.matmul(p[:, jj * ntok:(jj + 1) * ntok], xv[:, kc * 128:(kc + 1) * 128], dm,
                                       start=True, stop=True)
                    return ins
                S.op("pe", f, reads=[xtk, smk], writes=[pk])
                if not sample:
                    if kq % 2 == 0:
                        def f(e, kq=kq, p=p):
                            ins = None
                            for jj in range(4):
                                kc = kq * 4 + jj
                                ins = e.tensor_scalar(out=h_dst(kc), in0=p[:, jj * ntok:(jj + 1) * ntok],
                                                      scalar1=A1[:, kc, 16:17], scalar2=modT[:, SH1 + kc, 16:17],
                                                      op0=ALU.mult, op1=ALU.add)
                            return ins
                        S.op("dve", f, reads=[pk, T_A, T_mod], writes=[T_h])
                    else:
                        def f(e, kq=kq, p=p):
                            ins = None
                            for jj in range(4):
                                kc = kq * 4 + jj
                                ins = e.activation(out=h_dst(kc), in_=p[:, jj * ntok:(jj + 1) * ntok],
                                                   func=AF.Identity, bias=modT[:, SH1 + kc, 16:17],
                                                   scale=A1[:, kc, 16:17])
                            return ins
                        S.op("act", f, reads=[pk, T_A, T_mod], writes=[T_h])
                else:
                    t2, t2k = tmpn()

                    def f(e, kq=kq, p=p, t2=t2):
                        ins = None
                        for jj in range(4):
                            kc = kq * 4 + jj
                            ins = e.tensor_tensor(out=t2[:, jj * 16:(jj + 1) * 16], in0=p[:, jj * ntok:(jj + 1) * ntok],
                                                  in1=A1[:, kc, 0:16], op=ALU.mult)
                        return ins
                    S.op("dve", f, reads=[pk, T_A], writes=[t2k])

                    def f(e, kq=kq, t2=t2):
                        ins = None
                        for jj in range(4):
                            kc = kq * 4 + jj
                            ins = e.tensor_tensor(out=h_dst(kc), in0=t2[:, jj * 16:(jj + 1) * 16],
                                                  in1=modT[:, SH1 + kc, 0:16], op=ALU.add)
                        return ins
                    S.op("dve", f, reads=[t2k, T_mod], writes=[T_h])

        T_xst = [Tok("xst0"), Tok("xst1")]
        T_yst = [Tok("yst0"), Tok("yst1"), Tok("yst2")]

        def make_xst(n):
            return ([(AR.alloc([128, D], F32, "xst%d" % i), T_xst[i]) for i in range(n)], [0])

        mr1 = AR.rmark()
        hT = AR.alloc([128, 16, NT], BF16, "hT", side="R")
        T_h = [Tok("hT%d" % c) for c in range(4)]
        m1a = AR.mark()
        xst = make_xst(2)
        build_h(xst, xs, NS, h_dst=lambda kc: hTs[:, kc, :], T_h=T_hs,
                x_dst=lambda kq: x1Ts[:, kq * 4:(kq + 1) * 4, :], T_x=T_x1s, sample=True)
        for sub in range(16):
            build_h(xst, xp[sub * 128:(sub + 1) * 128, :], 128,
                    h_dst=lambda kc, sub=sub: hT[:, kc, sub * 128:(sub + 1) * 128], T_h=T_h[sub // 4])
        S.barrier(exclude=(T_cp,))
        AR.release(m1a)
        if stage <= 1:
            raise _Stop()

        m1d = AR.mark()
        mw = AR.alloc([128, 12, 256], BF16, "mw")
        T_mw = Tok("mw")
        S.dma("pool", mw, mwd.rearrange("p (h c) -> p h c", c=256), writes=[T_mw])
        qT = AR.alloc([128, 3, NT], BF16, "qT")
        kT = AR.alloc([128, 3, NT], BF16, "kT")
        T_q = [Tok("qT%d" % g) for g in range(3)]
        T_k = [Tok("kT%d" % g) for g in range(3)]
        T_vst = Tok("vst")
        Vt = AR.alloc([128, 3, 16, 128], BF16, "Vt")
        T_V = [Tok("V%d" % g) for g in range(3)]
        acc = AR.alloc([128, 2, NT], F32, "acc")
        T_acc = Tok("acc")
        nmx = AR.alloc([128, 2, 3, 4], F32, "nmx")
        T_nmx = Tok("nmx")
        bcol = AR.alloc([128, 8], F32, "bcol")
        T_bcol = Tok("bcol")
        ptr = [(AR.alloc([128, 256], BF16, "ptr%d" % i), Tok("ptr%d" % i)) for i in range(2)]
        ptm = [(AR.alloc([128, 256], BF16, "ptm%d" % i), Tok("ptm%d" % i)) for i in range(2)]
        pti = [0]

        def deint(ap2, d, t0, n):
            if d == 1:
                return ap2[:, t0:t0 + n]
            return ap2.rearrange("p (r u) -> p u r", r=d)[:, t0 // d:(t0 + n) // d, :]

        def nat(ap2, d):
            if d == 1:
                return ap2
            return ap2.rearrange("p (u r) -> p u r", r=d)

        for s in range(4):
            vst = oaT[:, s, :]
            ajobs = []
            for ty in (1, 2, 0):
                for g in range(3):
                    def loads(sl, ty=ty, s=s, g=g):
                        c0 = (Q0, K0, V0)[ty] + g * 512 + s * 128
                        return [(wview(sl, 16, 128), rows_view(w_in, 0, 16, c0, 128))]

                    def comp(sl, tok, ty=ty, s=s, g=g, vst=vst):
                        wv = wview(sl, 16, 128)
                        d = DIL[g]
                        for c in range(4):
                            p, pk = psn()

                            def f(e, p=p, c=c):
                                ins = None
                                for kc in range(16):
                                    ins = e.matmul(p[:, :], wv[:, kc, :], hT[:, kc, c * 512:(c + 1) * 512],
                                                   start=(kc == 0), stop=(kc == 15))
                                return ins
                            S.op("pe", f, reads=[tok, T_h[c]], writes=[pk])
                            if ty == 2:
                                dk = T_vst
                                dv = deint(vst, d, c * 512, 512)
                            else:
                                buf = qT if ty == 0 else kT
                                dk = (T_q if ty == 0 else T_k)[g]
                                dv = deint(buf[:, g, :], d, c * 512, 512)
                            if not (SKIP & 1):
                                S.op("dve", lambda e, p=p, dv=dv, d=d: e.tensor_copy(out=dv, in_=nat(p[:, :], d)),
                                     reads=[pk], writes=[dk])
                            if ty != 2 and not (SKIP & 2):
                                sq, sqk = tmpbn()
                                S.op("act", lambda e, p=p, sq=sq: e.activation(out=sq[:, :], in_=p[:, :], func=AF.Square),
                                     reads=[pk, dk], writes=[sqk])
                                p2, pk2 = psn()
                                S.op("pe", lambda e, p2=p2, sq=sq: e.matmul(p2[:, :], onesb, sq[:, :], start=True, stop=True),
                                     reads=[sqk, T_cb], writes=[pk2])
                                S.op("dve", lambda e, p2=p2, c=c: e.tensor_reduce(
                                    out=nmx[:, ty, g, c:c + 1], in_=p2[:, :], axis=AX.X, op=ALU.max),
                                    reads=[pk2], writes=[T_nmx])
                        if ty == 2:
                            for bq in range(4):
                                p, pk = psn()
                                pb = p[:, :].bitcast(BF16)

                                def f(e, pb=pb, bq=bq):
                                    ins = None
                                    for jj in range(4):
                                        blk = bq * 4 + jj
                                        ins = e.transpose(pb[:, jj * 128:(jj + 1) * 128], vst[:, blk * 128:(blk + 1) * 128], identb)
                                    return ins
                                S.op("pe", f, reads=[T_vst, T_cb], writes=[pk])
                                S.op("act", lambda e, pb=pb, bq=bq: e.activation(
                                    out=Vt[:, g, bq * 4:(bq + 1) * 4, :],
                                    in_=pb[:, 0:512].rearrange("p (a b) -> p a b", b=128), func=AF.Copy),
                                    reads=[pk], writes=[T_V[g]])
                        if SKIP & 4:
                            return
                        p, pk = psn()

                        def f(e, p=p):
                            ins = None
                            for kc in range(16):
                                ins = e.matmul(p[:, 0:16], wv[:, kc, :], hTs[:, kc, :], start=(kc == 0), stop=(kc == 15))
                            return ins
                        S.op("pe", f, reads=[tok, T_hs], writes=[pk])
                        S.op("act", lambda e, p=p: e.activation(out=qkvs[:, ty * 12 + g * 4 + s, :], in_=p[:, 0:16], func=AF.Copy),
                             reads=[pk], writes=[T_qkvs])
                    ajobs.append((loads, comp))
            import os as _os
            if stage <= 1.2:
                ajobs = ajobs[:int(_os.environ.get('NJOBS', '9'))]
            run_jobs(ajobs)
            if stage <= 1.2:
                raise _Stop()

            S.op("dve", lambda e: e.tensor_reduce(out=smallf[:, 8:14].rearrange("p (a b) -> p a b", b=3),
                                                  in_=nmx[:, 0:2, :, :], axis=AX.X, op=ALU.max),
                 reads=[T_nmx], writes=[T_small])
            S.op("dve", lambda e: e.tensor_tensor(out=smallf[:, 16:19], in0=smallf[:, 8:11], in1=smallf[:, 11:14],
                                                  op=ALU.mult), writes=[T_small])
            S.op("act", lambda e: e.activation(out=smallf[:, 16:19], in_=smallf[:, 16:19], func=AF.Sqrt), writes=[T_small])

            S.op("dve", lambda e: e.tensor_reduce(out=smallf[:, 20:21], in_=smallf[:, 16:19], axis=AX.X, op=ALU.max),
                 writes=[T_small])

            def f(e, s=s):
                ins = None
                for g in range(3):
                    h = g * 4 + s
                    ins = e.scalar_tensor_tensor(out=bcol[:, g:g + 1], in0=smallf[:, 20:21], scalar=-1.02 * SCALE,
                                                 in1=cbias[:, h:h + 1], op0=ALU.mult, op1=ALU.add)
                return ins
            S.op("dve", f, reads=[T_cf], writes=[T_small, T_bcol])
            if stage <= 1.4:
                raise _Stop()

            for g in range(3):
                d = DIL[g]
                h = g * 4 + s
                nb = 16 // d
                for r in range(d):
                    for j in range(nb):
                        blk = r * nb + j
                        wd = 256 if j > 0 else 128
                        pS, pSk = psn()

                        def f(e, pS=pS, g=g, blk=blk, j=j):
                            ins = e.matmul(pS[:, 0:128], kT[:, g, blk * 128:(blk + 1) * 128], qT[:, g, blk * 128:(blk + 1) * 128],
                                           start=True, stop=True)
                            if j > 0:
                                ins = e.matmul(pS[:, 128:256], kT[:, g, (blk - 1) * 128:blk * 128],
                                               qT[:, g, blk * 128:(blk + 1) * 128], start=True, stop=True)
                            return ins
                        S.op("pe", f, reads=[T_k[g], T_q[g]], writes=[pSk])
                        pr, prk = ptr[pti[0] % 2]
                        pm, pmk = ptm[pti[0] % 2]
                        pti[0] += 1
                        S.op("act", lambda e, pS=pS, pr=pr, wd=wd, g=g: e.activation(
                            out=pr[:, 0:wd], in_=pS[:, 0:wd], func=AF.Exp, bias=bcol[:, g:g + 1], scale=SCALE),
                            reads=[pSk, T_bcol], writes=[prk])
                        S.op("dve", lambda e, pr=pr, pm=pm, wd=wd, h=h: e.tensor_tensor(
                            out=pm[:, 0:wd], in0=pr[:, 0:wd], in1=mw[:, h, 0:wd], op=ALU.mult),
                            reads=[prk, T_mw], writes=[pmk])
                        pN, pNk = psn()

                        def f(e, pN=pN, pm=pm, g=g, blk=blk, j=j):
                            e.matmul(pN[:, 0:128], Vt[:, g, blk, :], pm[:, 0:128], start=True, stop=(j == 0))
                            if j > 0:
                                e.matmul(pN[:, 0:128], Vt[:, g, blk - 1, :], pm[:, 128:256], start=False, stop=True)
                            ins = e.matmul(pN[:, 128:256], onesb, pm[:, 0:128], start=True, stop=(j == 0))
                            if j > 0:
                                ins = e.matmul(pN[:, 128:256], onesb, pm[:, 128:256], start=False, stop=True)
                            return ins
                        S.op("pe", f, reads=[T_V[g], pmk, T_cb], writes=[pNk])
                        if d == 1:
                            av = acc[:, :, j * 128:(j + 1) * 128]
                        else:
                            av = acc.rearrange("p c (u r) -> p c u r", r=d)[:, :, j * 128:(j + 1) * 128, r]
                        pv = pN[:, 0:256].rearrange("p (c q) -> p c q", q=128)
                        if g == 0:
                            S.op("act", lambda e, av=av, pv=pv: e.activation(out=av, in_=pv, func=AF.Copy),
                                 reads=[pNk], writes=[T_acc])
                        else:
                            S.op("dve", lambda e, av=av, pv=pv: e.tensor_tensor(out=av, in0=av, in1=pv, op=ALU.add),
                                 reads=[pNk], writes=[T_acc])
            if stage <= 1.6:
                raise _Stop()
            for c in range(4):
                sl_ = slice(c * 512, (c + 1) * 512)

                S.op("dve", lambda e, sl_=sl_: e.reciprocal(out=acc[:, 1, sl_], in_=acc[:, 1, sl_]),
                     reads=[T_vst], writes=[T_acc, T_vst])
                S.op("dve", lambda e, sl_=sl_, s=s: e.tensor_tensor(out=oaT[:, s, sl_], in0=acc[:, 0, sl_], in1=acc[:, 1, sl_],
                                                                    op=ALU.mult),
                     writes=[T_acc, T_oa[s], T_vst])
        S.barrier(exclude=(T_cp,))
        AR.release(m1d)
        if stage <= 2:
            raise _Stop()

        m1c = AR.mark()
        kvst = [(AR.alloc([128, 256], F32, "kvst%d" % i), Tok("kvst%d" % i)) for i in range(3)]
        kvi = [0]
        kjobs = []
        for g, W in enumerate((128, 512, 2048)):
            for kv in range(2):
                for half in range(2):
                    def loads(sl, g=g, kv=kv, half=half):
                        c0 = (K0 if kv == 0 else V0) + g * 512 + half * 256
                        return [(wview(sl, 16, 256), rows_view(w_in, 0, 16, c0, 256))]

                    def comp(sl, tok, g=g, kv=kv, half=half, W=W):
                        wv = wview(sl, 16, 256)
                        for tt in range(W // 128):
                            t0 = NT - W + tt * 128
                            p, pk = psn()

                            def f(e, p=p, t0=t0):
                                ins = None
                                for kc in range(16):
                                    ins = e.matmul(p[:, 0:256], hT[:, kc, t0:t0 + 128], wv[:, kc, :],
                                                   start=(kc == 0), stop=(kc == 15))
                                return ins
                            S.op("pe", f, reads=[tok, T_h[t0 // 512]], writes=[pk])
                            st, stk = kvst[kvi[0] % 3]
                            kvi[0] += 1
                            if kvi[0] % 2 == 0:
                                S.op("act", lambda e, p=p, st=st: e.activation(out=st[:, :], in_=p[:, 0:256], func=AF.Copy),
                                     reads=[pk], writes=[stk])
                            else:
                                S.op("dve", lambda e, p=p, st=st: e.tensor_copy(out=st[:, :], in_=p[:, 0:256]),
                                     reads=[pk], writes=[stk])
                            S.dma("sp", kvp[g][tt * 128:(tt + 1) * 128, kv * 512 + half * 256:kv * 512 + (half + 1) * 256],
                                  st[:, :], reads=[stk], owner=stk, final=True)
                    kjobs.append((loads, comp))
        run_jobs(kjobs)
        S.barrier(exclude=(T_cp,))
        AR.release(m1c)
        if stage <= 3:
            raise _Stop()

        pT = AR.alloc([128, 4, NT], BF16, "pT")
        T_pT = [Tok("pT%d" % g) for g in range(4)]
        m1b = AR.mark()
        ubuf = AR.alloc([128, 16 + NT], F32, "ubuf")
        T_u = Tok("ubuf")
        sa = AR.alloc([128, 16 + NT], F32, "sa")
        sbb = AR.alloc([128, 16 + NT], F32, "sbb")
        T_sa = Tok("sa")
        T_sb = Tok("sb")
        zb = AR.alloc([128, NT], BF16, "zb")
        T_z = Tok("zb")
        zs = AR.alloc([128, 4, NS], BF16, "zs")
        T_zs = Tok("zs")
        shist = AR.alloc([128, 4, NS], F32, "shist")
        T_sh = Tok("shist")
        sp_sb = [AR.alloc([120, 512], F32, "sp_sb%d" % h) for h in range(2)]
        T_sp = Tok("sp_sb")
        utok = AR.alloc([16, 512], F32, "utok")
        T_utok = Tok("utok")
        utoks = AR.alloc([16, 512], F32, "utoks")
        T_utoks = Tok("utoks")
        for h in range(2):
            S.dma("sp", sp_sb[h], spool[h * 120:(h + 1) * 120, :], writes=[T_sp])

        def f(e):
            e.memzero(ubuf[:, 0:16])
            e.memzero(sa[:, 0:16])
            return e.memzero(sbb[:, 0:16])
        S.op("dve", f, writes=[T_u, T_sa, T_sb])
        p, pk = psn()

        def f(e, p=p):
            ins = None
            for g in range(4):
                for h in range(2):
                    ins = e.matmul(p[:, g * 16:(g + 1) * 16], sp_sb[h][0:120, g * 128:(g + 1) * 128],
                                   selw[0:120, (g * 2 + h) * 16:(g * 2 + h + 1) * 16], start=(h == 0), stop=(h == 1))
            return ins
        S.op("pe", f, reads=[T_sp, T_cf], writes=[pk])
        S.op("dve", lambda e, p=p: e.tensor_copy(out=shist, in_=p[:, 0:64].rearrange("p (g b) -> p g b", b=16)),
             reads=[pk], writes=[T_sh])
        pu_tok, pu_tokk = PSD
        pus_tok, pus_tokk = PSE

        ujobs = []
        for jh in range(2):
            def loads(sl, jh=jh):
                return [(wview(sl, 16, 256), rows_view(w_in, 0, 16, U0 + jh * 256, 256))]

            def comp(sl, tok, jh=jh):
                wv = wview(sl, 16, 256)
                for gg in range(2):
                    g = jh * 2 + gg
                    w = 2 << g
                    for c in range(4):
                        p, pk = psn()

                        def f(e, p=p, c=c, gg=gg):
                            ins = None
                            for kc in range(16):
                                ins = e.matmul(p[:, :], wv[:, kc, gg * 128:(gg + 1) * 128], hT[:, kc, c * 512:(c + 1) * 512],
                                               start=(kc == 0), stop=(kc == 15))
                            return ins
                        S.op("pe", f, reads=[tok, T_h[c]], writes=[pk])
                        S.op("act", lambda e, p=p, c=c: e.activation(out=ubuf[:, 16 + c * 512:16 + (c + 1) * 512],
                                                                     in_=p[:, :], func=AF.Copy),
                             reads=[pk], writes=[T_u])
                    p, pk = psn()

                    def f(e, p=p, gg=gg):
                        ins = None
                        for kc in range(16):
                            ins = e.matmul(p[:, 0:16], wv[:, kc, gg * 128:(gg + 1) * 128], hTs[:, kc, :],
                                           start=(kc == 0), stop=(kc == 15))
                        return ins
                    S.op("pe", f, reads=[tok, T_hs], writes=[pk])
                    S.op("act", lambda e, p=p, g=g: e.activation(out=qkvs[:, 36 + g, :], in_=p[:, 0:16], func=AF.Copy),
                         reads=[pk], writes=[T_qkvs])
                    S.op("pe", lambda e, g=g: e.matmul(pu_tok[0:16, g * 128:(g + 1) * 128], ubuf[:, NT:NT + 16], ident,
                                                       start=True, stop=True), reads=[T_u, T_cf], writes=[pu_tokk])
                    S.op("pe", lambda e, g=g: e.matmul(pus_tok[0:16, g * 128:(g + 1) * 128], qkvs[:, 36 + g, :], ident,
                                                       start=True, stop=True), reads=[T_qkvs, T_cf], writes=[pus_tokk])
                    src, srck = ubuf, T_u
                    bufs = [(sa, T_sa), (sbb, T_sb)]
                    sh = 1
                    for step in range(g + 1):
                        dst, dstk = bufs[step % 2]
                        S.op("dve", lambda e, src=src, dst=dst, sh=sh: e.tensor_tensor(
                            out=dst[:, 16:16 + NT], in0=src[:, 16:16 + NT], in1=src[:, 16 - sh:16 - sh + NT], op=ALU.add),
                            reads=[srck], writes=[dstk])
                        src, srck = dst, dstk
                        sh *= 2
                    S.op("dve", lambda e, src=src, w=w: e.scalar_tensor_tensor(
                        out=zb[:, :], in0=src[:, 16:16 + NT], scalar=1.0 / w, in1=ubuf[:, 16:16 + NT],
                        op0=ALU.mult, op1=ALU.subtract), reads=[srck, T_u], writes=[T_z])
                    tt, ttk = tmpn()

                    S.op("dve", lambda e, src=src, g=g, tt=tt: e.tensor_tensor(
                        out=tt[:, 0:16], in0=src[:, 16:32], in1=invc[:, g * 16:(g + 1) * 16], op=ALU.mult),
                        reads=[srck, T_cf], writes=[ttk])
                    S.op("dve", lambda e, tt=tt: e.tensor_tensor(out=zb[:, 0:16], in0=tt[:, 0:16], in1=ubuf[:, 16:32],
                                                                 op=ALU.subtract), reads=[ttk, T_u], writes=[T_z])
                    for c in range(4):
                        p, pk = psn()
                        S.op("pe", lambda e, p=p, c=c, g=g: e.matmul(p[:, :], wpool[:, g, :], zb[:, c * 512:(c + 1) * 512],
                                                                     start=True, stop=True),
                             reads=[T_wpool, T_z], writes=[pk])
                        S.op("act", lambda e, p=p, c=c, g=g: e.activation(
                            out=pT[:, g, c * 512:(c + 1) * 512], in_=p[:, :], func=AF.Identity, scale=pscT[:, g:g + 1]),
                            reads=[pk, T_vT], writes=[T_pT[g]])
                    tt, ttk = tmpn()

                    S.op("dve", lambda e, g=g, tt=tt: e.tensor_tensor(out=tt[:, 0:16], in0=shist[:, g, :],
                                                                      in1=qkvs[:, 36 + g, :], op=ALU.add),
                         reads=[T_sh, T_qkvs], writes=[ttk])
                    S.op("dve", lambda e, g=g, w=w, tt=tt: e.scalar_tensor_tensor(
                        out=zs[:, g, :], in0=tt[:, 0:16], scalar=1.0 / w, in1=qkvs[:, 36 + g, :],
                        op0=ALU.mult, op1=ALU.subtract), reads=[ttk, T_qkvs], writes=[T_zs])
                    p, pk = psn()
                    S.op("pe", lambda e, p=p, g=g: e.matmul(p[:, 0:16], wpool[:, g, :], zs[:, g, :], start=True, stop=True),
                         reads=[T_wpool, T_zs], writes=[pk])
                    S.op("act", lambda e, p=p, g=g: e.activation(out=pTs[:, g, :], in_=p[:, 0:16], func=AF.Identity,
                                                                 scale=pscT[:, g:g + 1]),
                         reads=[pk, T_vT], writes=[T_pTs])
            ujobs.append((loads, comp))
        run_jobs(ujobs)
        S.op("dve", lambda e: e.tensor_copy(out=utok, in_=pu_tok[0:16, :]), reads=[pu_tokk], writes=[T_utok])
        S.op("dve", lambda e: e.tensor_copy(out=utoks, in_=pus_tok[0:16, :]), reads=[pus_tokk], writes=[T_utoks])
        S.dma("sp", poolp, utok[1:16, :], reads=[T_utok], owner=T_utok, final=True)
        S.dma("sp", pools[:, 14, :], utoks, reads=[T_utoks], owner=T_utoks, final=True)
        S.barrier(exclude=(T_cp,))
        AR.release(m1b)
        AR.rrelease(mr1)
        if stage <= 4:
            raise _Stop()

        msa = AR.mark()
        tokq = AR.alloc([16, 3, 1536], F32, "tokq")
        T_tokq = Tok("tokq")
        for ty in range(3):
            for g in range(3):
                p, pk = psn()

                def f(e, p=p, ty=ty, g=g):
                    ins = None
                    for hh in range(4):
                        ins = e.matmul(p[0:16, hh * 128:(hh + 1) * 128], qkvs[:, ty * 12 + g * 4 + hh, :], ident,
                                       start=True, stop=True)
                    return ins
                S.op("pe", f, reads=[T_qkvs, T_cf], writes=[pk])
                S.op("act", lambda e, p=p, ty=ty, g=g: e.activation(out=tokq[:, ty, g * 512:(g + 1) * 512], in_=p[0:16, :],
                                                                    func=AF.Copy), reads=[pk], writes=[T_tokq])
        T_qscr = Tok("qscr")
        S.dma("sp", qscr, tokq[:, 0, :], reads=[T_tokq], writes=[T_qscr])
        for g, W in enumerate((128, 512, 2048)):
            S.dma("sp", kvs[g][:, W - 1, 0:512], tokq[:, 1, g * 512:(g + 1) * 512], reads=[T_tokq], owner=T_tokq, final=True)
            S.dma("sp", kvs[g][:, W - 1, 512:1024], tokq[:, 2, g * 512:(g + 1) * 512], reads=[T_tokq], owner=T_tokq, final=True)
        bs_sb = AR.alloc([16, 12, 129], F32, "bs_sb")
        T_bs = Tok("bs")
        S.dma("sp", bs_sb, bsd.rearrange("p (h c) -> p h c", c=129), writes=[T_bs])
        sT_all = AR.alloc([128, 16, 12], F32, "sT_all")
        T_sTa = Tok("sT_all")
        stok = AR.alloc([16, 12, 129], F32, "stok")
        T_stok = Tok("stok")
        sm16 = AR.alloc([16, 64], F32, "sm16")
        T_sm16 = Tok("sm16")
        prodt = AR.alloc([16, 1536], F32, "prodt")
        T_prodt = Tok("prodt")
        kh = [(AR.alloc([128, 512], F32, "kh%d" % i), Tok("kh%d" % i)) for i in range(2)]
        qb = [(AR.alloc([128, 512], F32, "qb%d" % i), Tok("qb%d" % i)) for i in range(2)]
        wT = AR.alloc([128, 12, 16], F32, "wT")
        T_wT = Tok("wT")
        wTm = AR.alloc([128, 12, 16, 16], BF16, "wTm")
        T_wTm = Tok("wTm")
        otok = AR.alloc([16, 512], F32, "otok")
        T_otok = Tok("otok")

        S.op("dve", lambda e: e.tensor_tensor(out=prodt, in0=tokq[:, 0, :], in1=tokq[:, 1, :], op=ALU.mult),
             reads=[T_tokq], writes=[T_prodt])
        S.op("dve", lambda e: e.tensor_reduce(out=stok[:, :, 128:129], in_=prodt.rearrange("p (h x) -> p h x", x=128),
                                              axis=AX.X, op=ALU.add), reads=[T_prodt], writes=[T_stok])
        it = 0
        for b in range(NS):
            for g in range(3):
                d = DIL[g]
                kt, ktk = kh[it % 2]
                qt, qtk = qb[it % 2]
                it += 1
                S.dma("sp", kt, ckv[g][b, :, 0:512].rearrange("(j x) c -> j x c", x=d)[:, 0, :], writes=[ktk])
                S.dma("sp", qt, qscr[b:b + 1, g * 512:(g + 1) * 512].partition_broadcast(128), reads=[T_qscr], writes=[qtk])

                S.op("dve", lambda e, kt=kt, qt=qt: e.tensor_tensor(out=kt, in0=kt, in1=qt, op=ALU.mult),
                     reads=[qtk], writes=[ktk])
                S.op("dve", lambda e, kt=kt, b=b, g=g: e.tensor_reduce(
                    out=sT_all[:, b, g * 4:(g + 1) * 4], in_=kt.rearrange("p (h x) -> p h x", x=128), axis=AX.X, op=ALU.add),
                    reads=[ktk], writes=[T_sTa])
        for g3 in range(3):
            p, pk = psn()

            def f(e, p=p, g3=g3):
                ins = None
                for hh in range(4):
                    ins = e.matmul(p[0:16, hh * 128:(hh + 1) * 128], sT_all[:, :, g3 * 4 + hh], ident, start=True, stop=True)
                return ins
            S.op("pe", f, reads=[T_sTa, T_cf], writes=[pk])
            S.op("dve", lambda e, p=p, g3=g3: e.tensor_copy(
                out=stok[:, g3 * 4:(g3 + 1) * 4, 0:128], in_=p[0:16, :].rearrange("p (h x) -> p h x", x=128)),
                reads=[pk], writes=[T_stok])
        mx = sm16[:, 0:12]
        den = sm16[:, 12:24]
        Mx = sm16[:, 24:28]
        fco = sm16[:, 28:40]
        dtot = sm16[:, 40:44]

        def dv(fn, reads=(), writes=()):
            S.op("dve", fn, reads=list(reads), writes=list(writes))

        g3v = lambda ap: ap.rearrange("p (g s) -> p g s", s=4)
        s3v = lambda ap: ap.rearrange("p (g s) -> p s g", s=4)
        dv(lambda e: e.scalar_tensor_tensor(out=stok, in0=stok, scalar=SCALE, in1=bs_sb, op0=ALU.mult, op1=ALU.add),
           [T_bs], [T_stok])
        dv(lambda e: e.tensor_reduce(out=mx, in_=stok, axis=AX.X, op=ALU.max), [T_stok], [T_sm16])
        dv(lambda e: e.tensor_tensor(out=stok, in0=stok, in1=mx.unsqueeze(2).to_broadcast([16, 12, 129]), op=ALU.subtract),
           [T_sm16], [T_stok])
        S.op("act", lambda e: e.activation(out=stok, in_=stok, func=AF.Exp), writes=[T_stok])
        dv(lambda e: e.tensor_reduce(out=den, in_=stok, axis=AX.X, op=ALU.add), [T_stok], [T_sm16])
        dv(lambda e: e.tensor_reduce(out=Mx, in_=s3v(mx), axis=AX.X, op=ALU.max), [], [T_sm16])
        dv(lambda e: e.tensor_tensor(out=g3v(fco), in0=g3v(mx), in1=Mx.unsqueeze(1).to_broadcast([16, 3, 4]), op=ALU.subtract),
           [], [T_sm16])
        S.op("act", lambda e: e.activation(out=fco, in_=fco, func=AF.Exp), writes=[T_sm16])
        dv(lambda e: e.tensor_tensor(out=den, in0=den, in1=fco, op=ALU.mult), [], [T_sm16])
        dv(lambda e: e.tensor_reduce(out=dtot, in_=s3v(den), axis=AX.X, op=ALU.add), [], [T_sm16])
        dv(lambda e: e.reciprocal(out=dtot, in_=dtot), [], [T_sm16])
        dv(lambda e: e.tensor_tensor(out=g3v(fco), in0=g3v(fco), in1=dtot.unsqueeze(1).to_broadcast([16, 3, 4]), op=ALU.mult),
           [], [T_sm16])
        dv(lambda e: e.tensor_tensor(out=stok, in0=stok, in1=fco.unsqueeze(2).to_broadcast([16, 12, 129]), op=ALU.mult),
           [T_sm16], [T_stok])
        p, pk = psn()

        def f(e, p=p):
            ins = None
            for hh in range(12):
                ins = e.matmul(p[:, hh * 16:(hh + 1) * 16], stok[:, hh, 0:128], ident[0:16, 0:16], start=True, stop=True)
            return ins
        S.op("pe", f, reads=[T_stok, T_cf], writes=[pk])
        S.op("dve", lambda e, p=p: e.tensor_copy(out=wT, in_=p[:, 0:192].rearrange("p (h b) -> p h b", b=16)),
             reads=[pk], writes=[T_wT])

        def f(e):
            ins = None
            for bp in range(16):
                ins = e.tensor_tensor(out=wTm[:, :, bp, :], in0=wT,
                                      in1=i16[:, bp * 16:(bp + 1) * 16].unsqueeze(1).to_broadcast([128, 12, 16]), op=ALU.mult)
            return ins
        S.op("dve", f, reads=[T_wT, T_cf], writes=[T_wTm])
        pO, pOk = PSD
        vall = AR.alloc([128, 48, 512], BF16, "vall")
        T_vall = Tok("vall")
        for b in range(NS):
            for g in range(3):
                d = DIL[g]
                S.dma("pool", vall[:, b * 3 + g, :], ckv[g][b, :, 512:1024].rearrange("(j x) c -> j x c", x=d)[:, 0, :],
                      writes=[T_vall])

        def f(e):
            ins = None
            for hh in range(4):
                for b in range(NS):
                    for g in range(3):
                        first = (b == 0 and g == 0)
                        last = (b == NS - 1 and g == 2)
                        ins = e.matmul(pO[0:16, hh * 128:(hh + 1) * 128], wTm[:, g * 4 + hh, b, :],
                                       vall[:, b * 3 + g, hh * 128:(hh + 1) * 128], start=first, stop=last)
            return ins
        S.op("pe", f, reads=[T_vall, T_wTm], writes=[pOk])
        dv(lambda e: e.tensor_tensor(out=prodt.rearrange("p (h x) -> p h x", x=128),
                                     in0=tokq[:, 2, :].rearrange("p (h x) -> p h x", x=128),
                                     in1=stok[:, :, 128:129].to_broadcast([16, 12, 128]), op=ALU.mult),
           [T_stok, T_tokq], [T_prodt])
        dv(lambda e: e.tensor_tensor(out=otok, in0=pO[0:16, :], in1=prodt[:, 0:512], op=ALU.add), [pOk, T_prodt], [T_otok])
        dv(lambda e: e.tensor_tensor(out=otok, in0=otok, in1=prodt[:, 512:1024], op=ALU.add), [T_prodt], [T_otok])
        dv(lambda e: e.tensor_tensor(out=otok, in0=otok, in1=prodt[:, 1024:1536], op=ALU.add), [T_prodt], [T_otok])
        p, pk = psn()

        def f(e, p=p):
            ins = None
            for hh in range(4):
                ins = e.matmul(p[:, hh * 16:(hh + 1) * 16], otok[:, hh * 128:(hh + 1) * 128], ident[0:16, 0:16], start=True, stop=True)
            return ins
        S.op("pe", f, reads=[T_otok, T_cf], writes=[pk])
        S.op("dve", lambda e, p=p: e.tensor_copy(out=oaTs, in_=p[:, 0:64].rearrange("p (h b) -> p h b", b=16)),
             reads=[pk], writes=[T_oas])
        S.barrier(exclude=(T_cp,))
        AR.release(msa)
        if stage <= 5:
            raise _Stop()

        class Chunk:
            pass

        def resid_update(ck, p, pk, fc, GT):
            n = ck.n
            if not ck.sample:
                S.op("dve", lambda e: e.scalar_tensor_tensor(out=ck.x1(fc), in0=p[:, 0:n], scalar=modT[:, GT + fc, 16:17],
                                                             in1=ck.x1(fc), op0=ALU.mult, op1=ALU.add),
                     reads=[pk, T_mod], writes=[ck.T_x1])
            else:
                tt, ttk = tmpn()

                S.op("dve", lambda e: e.tensor_tensor(out=tt[:, 0:n], in0=p[:, 0:n], in1=modT[:, GT + fc, 0:16], op=ALU.mult),
                     reads=[pk, T_mod], writes=[ttk])
                S.op("dve", lambda e: e.tensor_tensor(out=ck.x1(fc), in0=ck.x1(fc), in1=tt[:, 0:n], op=ALU.add),
                     reads=[ttk], writes=[ck.T_x1])

        for tile in range(2):
            t0 = tile * 1024
            mt = AR.mark()
            mrt = AR.rmark()
            mixT = AR.alloc([128, 16, 1024], BF16, "mixT")
            T_mix = [Tok("mixT%d_%d" % (tile, c)) for c in range(2)]
            mh = AR.mark()
            hT2 = AR.alloc([128, 16, 1024], BF16, "hT2")
            T_h2 = [Tok("hT2%d_%d" % (tile, c)) for c in range(2)]
            T_x1 = [Tok("x1T%d_%d" % (tile, c)) for c in range(2)]
            T_aT = [Tok("aT%d_%d" % (tile, i)) for i in range(2)]
            T_aTs = [Tok("aTs%d_%d" % (tile, i)) for i in range(2)]
            def mk_chunks(hbuf, x1buf, aTl, aTsl, t0=t0, tile=tile, mixT=mixT, T_mix=T_mix, T_h2=T_h2, T_x1=T_x1,
                          T_aT=T_aT, T_aTs=T_aTs):
                chunks = []
                for c in range(2):
                    ck = Chunk()
                    ck.sample = False
                    ck.n = 512
                    ck.tok0 = t0 + c * 512
                    ck.x1 = lambda kc, c=c: x1buf[:, kc, c * 512:(c + 1) * 512]
                    ck.T_x1 = T_x1[c]
                    ck.h = lambda kc, c=c: hbuf[:, kc, c * 512:(c + 1) * 512]
                    ck.T_h = T_h2[c]
                    ck.mix = lambda kc, c=c: mixT[:, kc, c * 512:(c + 1) * 512]
                    ck.T_mix = T_mix[c]
                    ck.oa = lambda kc, ck=ck: oaT[:, kc, ck.tok0:ck.tok0 + 512]
                    ck.T_oa = T_oa
                    ck.pp = lambda kc, ck=ck: pT[:, kc, ck.tok0:ck.tok0 + 512]
                    ck.T_pp = T_pT
                    ck.a = lambda i, cc, c=c: aTl[i][:, cc, c * 512:(c + 1) * 512]
                    ck.T_a = T_aT
                    chunks.append(ck)
                if tile == 0:
                    ck = Chunk()
                    ck.sample = True
                    ck.n = NS
                    ck.x1 = lambda kc: x1Ts[:, kc, :]
                    ck.T_x1 = T_x1s
                    ck.h = lambda kc: hTs[:, kc, :]
                    ck.T_h = T_hs
                    ck.mix = lambda kc: mixTs[:, kc, :]
                    ck.T_mix = T_mixs
                    ck.oa = lambda kc: oaTs[:, kc, :]
                    ck.T_oa = [T_oas] * 4
                    ck.pp = lambda kc: pTs[:, kc, :]
                    ck.T_pp = [T_pTs] * 4
                    ck.a = lambda i, cc: aTsl[i][:, cc, :]
                    ck.T_a = T_aTs
                    chunks.append(ck)
                return chunks

            chunks = mk_chunks(hT2, None, None, None)

            mx_ = AR.mark()
            xst = make_xst(2)
            for c in range(2):
                for sub in range(4):
                    tk = t0 + c * 512 + sub * 128
                    build_h(xst, xp[tk:tk + 128, :], 128,
                            h_dst=lambda kc, c=c, sub=sub: hT2[:, kc, c * 512 + sub * 128:c * 512 + (sub + 1) * 128],
                            T_h=T_h2[c])
            S.barrier(exclude=(T_cp,))
            AR.release(mx_)

            jobs = []
            for fc in range(16):
                for br in range(2):
                    def loads(sl, fc=fc, br=br):
                        return [(sl[:, 0:2048].rearrange("p (k c) -> p k c", c=128),
                                 rows_view(w_in, 0, 16, (GA0, GB0)[br] + fc * 128, 128)),
                                (sl[:, 2048:2560].rearrange("p (k c) -> p k c", c=128),
                                 rows_view((w_ua, w_up)[br], 0, 4, fc * 128, 128))]

                    def comp(sl, tok, fc=fc, br=br):
                        wg = sl[:, 0:2048].rearrange("p (k c) -> p k c", c=128)
                        wu = sl[:, 2048:2560].rearrange("p (k c) -> p k c", c=128)
                        for ck in chunks:
                            n = ck.n
                            pG, pGk = psn()
                            pU, pUk = psn()

                            def f(e, pG=pG, ck=ck, n=n):
                                ins = None
                                for kc in range(16):
                                    ins = e.matmul(pG[:, 0:n], wg[:, kc, :], ck.h(kc), start=(kc == 0), stop=(kc == 15))
                                return ins
                            S.op("pe", f, reads=[tok, ck.T_h], writes=[pGk])
                            src = ck.oa if br == 0 else ck.pp
                            srck = ck.T_oa if br == 0 else ck.T_pp

                            def f(e, pU=pU, src=src, n=n):
                                ins = None
                                for kc in range(4):
                                    ins = e.matmul(pU[:, 0:n], wu[:, kc, :], src(kc), start=(kc == 0), stop=(kc == 3))
                                return ins
                            S.op("pe", f, reads=[tok] + list(srck), writes=[pUk])
                            sg, sgk = tmpn()
                            S.op("act", lambda e, pG=pG, sg=sg, n=n: e.activation(out=sg[:, 0:n], in_=pG[:, 0:n], func=AF.Sigmoid),
                                 reads=[pGk], writes=[sgk])
                            if br == 0:
                                S.op("dve", lambda e, pU=pU, sg=sg, n=n, ck=ck: e.tensor_tensor(
                                    out=ck.mix(fc), in0=sg[:, 0:n], in1=pU[:, 0:n], op=ALU.mult),
                                    reads=[pUk, sgk], writes=[ck.T_mix])
                            else:
                                S.op("dve", lambda e, pU=pU, sg=sg, n=n: e.tensor_tensor(
                                    out=sg[:, 0:n], in0=sg[:, 0:n], in1=pU[:, 0:n], op=ALU.mult), reads=[pUk], writes=[sgk])
                                S.op("dve", lambda e, sg=sg, n=n, ck=ck: e.tensor_tensor(
                                    out=ck.mix(fc), in0=ck.mix(fc), in1=sg[:, 0:n], op=ALU.add), reads=[sgk], writes=[ck.T_mix])
                    jobs.append((loads, comp))
            run_jobs(jobs)
            S.barrier(exclude=(T_cp,))
            AR.release(mh)

            x1T = AR.alloc([128, 16, 1024], F32, "x1T", side="R")
            chunks = mk_chunks(None, x1T, None, None)
            mx_ = AR.mark()
            xst = make_xst(2)
            for c in range(2):
                for sub in range(4):
                    tk = t0 + c * 512 + sub * 128
                    build_h(xst, xp[tk:tk + 128, :], 128,
                            x_dst=lambda kq, c=c, sub=sub: x1T[:, kq * 4:(kq + 1) * 4, c * 512 + sub * 128:c * 512 + (sub + 1) * 128],
                            T_x=T_x1[c])
            S.barrier(exclude=(T_cp,))
            AR.release(mx_)

            jobs = []
            for jc in range(8):
                def loads(sl, jc=jc):
                    return [(wview(sl, 16, 256), rows_view(w_out, 0, 16, jc * 256, 256))]

                def comp(sl, tok, jc=jc):
                    wv = wview(sl, 16, 256)
                    for ff in range(2):
                        fc = jc * 2 + ff
                        for ck in chunks:
                            n = ck.n
                            p, pk = psn()

                            def f(e, p=p, ck=ck, n=n, ff=ff):
                                ins = None
                                for kc in range(16):
                                    ins = e.matmul(p[:, 0:n], wv[:, kc, ff * 128:(ff + 1) * 128], ck.mix(kc),
                                                   start=(kc == 0), stop=(kc == 15))
                                return ins
                            S.op("pe", f, reads=[tok, ck.T_mix], writes=[pk])
                            resid_update(ck, p, pk, fc, GT1)
                jobs.append((loads, comp))
            run_jobs(jobs)
            S.barrier(exclude=(T_cp,))
            AR.release(mt)

            h2T = AR.alloc([128, 16, 1024], BF16, "h2T")
            aTl = [AR.alloc([128, 2, 1024], BF16, "aT%d" % i) for i in range(2)]
            aTsl = [AR.alloc([128, 2, NS], BF16, "aTs%d" % i) for i in range(2)]
            chunks = mk_chunks(h2T, x1T, aTl, aTsl)
            rstdb = AR.alloc([128, 512], F32, "rstdb")
            T_rstdb = Tok("rstdb%d" % tile)
            yst = [(AR.alloc([128, 512], F32, "yst%d" % i), T_yst[i]) for i in range(3)]
            ysi = [0]

            for ck in chunks:
                n = ck.n
                pS_, pSk_ = psn()
                for kc in range(16):
                    sq, sqk = tmpbn()
                    S.op("act", lambda e, sq=sq, ck=ck, kc=kc, n=n: e.activation(out=sq[:, 0:n], in_=ck.x1(kc), func=AF.Square),
                         reads=[ck.T_x1], writes=[sqk])
                    S.op("pe", lambda e, sq=sq, kc=kc, n=n, pS_=pS_: e.matmul(pS_[:, 0:n], onesb, sq[:, 0:n], start=(kc == 0),
                                                                               stop=(kc == 15)), reads=[sqk, T_cb], writes=[pSk_])
                rstd_from_sum(rstdb[:, 0:n], pS_[:, 0:n], [pSk_], [T_rstdb])
                for kc in range(16):
                    tt, ttk = tmpn()
                    S.op("dve", lambda e, tt=tt, ck=ck, kc=kc, n=n: e.tensor_tensor(out=tt[:, 0:n], in0=ck.x1(kc), in1=rstdb[:, 0:n],
                                                                                    op=ALU.mult),
                         reads=[ck.T_x1, T_rstdb], writes=[ttk])
                    if not ck.sample:
                        S.op("act", lambda e, tt=tt, ck=ck, kc=kc, n=n: e.activation(
                            out=ck.h(kc), in_=tt[:, 0:n], func=AF.Identity, bias=modT[:, SH2 + kc, 16:17], scale=A2[:, kc, 16:17]),
                            reads=[ttk, T_A, T_mod], writes=[ck.T_h])
                    else:
                        S.op("dve", lambda e, tt=tt, kc=kc, n=n: e.tensor_tensor(out=tt[:, 0:n], in0=tt[:, 0:n],
                                                                                 in1=A2[:, kc, 0:16], op=ALU.mult),
                             reads=[T_A], writes=[ttk])
                        S.op("dve", lambda e, tt=tt, ck=ck, kc=kc, n=n: e.tensor_tensor(
                            out=ck.h(kc), in0=tt[:, 0:n], in1=modT[:, SH2 + kc, 0:16], op=ALU.add),
                            reads=[ttk, T_mod], writes=[ck.T_h])

            NG = DFF // 256
            jobs = []

            def up_job(g):
                def loads(sl):
                    return [(wview(sl, 16, 256), rows_view(w_mu, 0, 16, g * 256, 256))]

                def comp(sl, tok):
                    wv = wview(sl, 16, 256)
                    for cc in range(2):
                        for ck in chunks:
                            n = ck.n
                            p, pk = psn()

                            def f(e, p=p, ck=ck, n=n, cc=cc):
                                ins = None
                                for kc in range(16):
                                    ins = e.matmul(p[:, 0:n], wv[:, kc, cc * 128:(cc + 1) * 128], ck.h(kc),
                                                   start=(kc == 0), stop=(kc == 15))
                                return ins
                            S.op("pe", f, reads=[tok, ck.T_h], writes=[pk])
                            rl, rlk = tmpbn()
                            S.op("act", lambda e, p=p, rl=rl, n=n: e.activation(out=rl[:, 0:n], in_=p[:, 0:n], func=AF.Relu),
                                 reads=[pk], writes=[rlk])
                            S.op("dve", lambda e, rl=rl, ck=ck, n=n, cc=cc: e.tensor_tensor(
                                out=ck.a(g % 2, cc), in0=rl[:, 0:n], in1=rl[:, 0:n], op=ALU.mult),
                                reads=[rlk], writes=[ck.T_a[g % 2]])
                return (loads, comp)

            def down_job(g):
                def loads(sl):
                    return [(wview(sl, 2, 2048), rows_view(w_md, g * 256, 2, 0, 2048))]

                def comp(sl, tok):
                    wv = wview(sl, 2, 2048)
                    for fc in range(16):
                        for ck in chunks:
                            n = ck.n
                            p, pk = psn()

                            def f(e, p=p, ck=ck, n=n, fc=fc):
                                ins = None
                                for cc in range(2):
                                    ins = e.matmul(p[:, 0:n], wv[:, cc, fc * 128:(fc + 1) * 128], ck.a(g % 2, cc),
                                                   start=(cc == 0), stop=(cc == 1))
                                return ins
                            S.op("pe", f, reads=[tok, ck.T_a[g % 2]], writes=[pk])
                            resid_update(ck, p, pk, fc, GT2)
                return (loads, comp)

            for g in range(NG):
                jobs.append(up_job(g))
                if g > 0:
                    jobs.append(down_job(g - 1))
            jobs.append(down_job(NG - 1))
            run_jobs(jobs)

            for ck in chunks:
                n = ck.n
                for kc in range(16):
                    S.op("act", lambda e, ck=ck, kc=kc: e.activation(out=ck.h(kc), in_=ck.x1(kc), func=AF.Square),
                         reads=[ck.T_x1], writes=[ck.T_h])
                    S.op("dve", lambda e, ck=ck, kc=kc: e.tensor_scalar(out=ck.x1(kc), in0=ck.x1(kc), scalar1=gfT[:, kc:kc + 1],
                                                                        scalar2=None, op0=ALU.mult),
                         reads=[ck.T_h, T_vT], writes=[ck.T_x1])
                nsub = 1 if ck.sample else 4
                m = NS if ck.sample else 128
                for sub in range(nsub):
                    pS_, pSk_ = psn()

                    def f(e, pS_=pS_, ck=ck, sub=sub, m=m):
                        ins = None
                        for kc in range(16):
                            ins = e.matmul(pS_[0:m, 0:1], ck.h(kc)[:, sub * m:(sub + 1) * m], onesb[:, 0:1],
                                           start=(kc == 0), stop=(kc == 15))
                        return ins
                    S.op("pe", f, reads=[ck.T_h, T_cb], writes=[pSk_])
                    rs, rsk = bhsm[bhstate[0] % 2]
                    bhstate[0] += 1
                    rstd_from_sum(rs[0:m, 0:1], pS_[0:m, 0:1], [pSk_], [rsk])
                    for nq in range(4):
                        p, pk = psn()

                        def f(e, p=p, ck=ck, sub=sub, m=m, nq=nq):
                            ins = None
                            for jj in range(4):
                                kc = nq * 4 + jj
                                ins = e.matmul(p[0:m, jj * 128:(jj + 1) * 128], ck.x1(kc)[:, sub * m:(sub + 1) * m], ident,
                                               start=True, stop=True)
                            return ins
                        S.op("pe", f, reads=[ck.T_x1, T_cf], writes=[pk])
                        st, stk = yst[ysi[0] % 3]
                        ysi[0] += 1
                        S.op("act", lambda e, p=p, st=st, rs=rs, m=m: e.activation(out=st[0:m, :], in_=p[0:m, :], func=AF.Identity,
                                                                                   scale=rs[0:m, 0:1]),
                             reads=[pk, rsk], writes=[stk])
                        if ck.sample:
                            S.dma("sp", ys[:, nq * 512:(nq + 1) * 512], st[0:m, :], reads=[stk], owner=stk, final=True)
                        else:
                            r0 = ck.tok0 + sub * 128
                            S.dma("sp", yp[r0:r0 + 128, nq * 512:(nq + 1) * 512], st[:, :], reads=[stk], owner=stk, final=True)
            S.barrier(exclude=(T_cp,))
            AR.release(mt)
            AR.rrelease(mrt)

    except _Stop:
        pass
    drip(10000)
    with nc.Block() as block:
        S.emit(block)
    es.close()
    return nc


_CACHE = {}


def kernel(**inp):
    f32 = lambda a: np.ascontiguousarray(np.asarray(a, dtype=np.float32))
    CF, MW, BS = host_consts()
    vecs = np.concatenate([f32(inp["norm_mix_g"]).reshape(16, 128), f32(inp["norm_mlp_g"]).reshape(16, 128),
                           f32(inp["norm_final_g"]).reshape(16, 128), f32(inp["pool_scale"]).reshape(4, 128)], axis=0)
    shared = {
        "w_ada": f32(inp["w_ada"])[0], "b_ada": f32(inp["b_ada"]).reshape(96, 128), "w_in": f32(inp["w_in"])[0],
        "w_ua": f32(inp["w_up_attn"])[0], "w_pool": f32(inp["w_pool"])[0], "vecs": f32(vecs),
        "w_up": f32(inp["w_up_pool"])[0], "w_out": f32(inp["w_out"])[0], "w_mu": f32(inp["w_mlp_up"])[0],
        "w_md": f32(inp["w_mlp_down"])[0], "cf": CF, "mw": MW, "bs": BS,
    }
    xp = f32(inp["x_prompt"])
    xs = f32(inp["x_sample"])[:, 0, :]
    cp = f32(inp["c_prompt"])
    cs = f32(inp["c_sample"])
    c0 = f32(inp["cache_kv_w128"])[0]
    c1 = f32(inp["cache_kv_w512"])[0]
    c2 = f32(inp["cache_kv_w2048"])[0]
    sp = f32(inp["state_pool"])[0]
    in_maps = []
    for c in range(NCORES):
        sl = slice(c * NS, (c + 1) * NS)
        m = dict(shared)
        m["xp"] = xp[c]
        m["xs"] = xs[sl]
        m["call"] = np.concatenate([cs[sl], cp[c:c + 1]], axis=0)
        m["ckv0"] = c0[sl].reshape(NS, 128, 1024)
        m["ckv1"] = c1[sl].reshape(NS, 512, 1024)
        m["ckv2"] = c2[sl].reshape(NS, 2048, 1024)
        m["spool"] = sp[sl].reshape(NS * 15, 512)
        in_maps.append(m)
    if "nc" not in _CACHE:
        _CACHE["nc"] = build_program()
    res = run_bass_kernel_spmd(_CACHE["nc"], in_maps, core_ids=list(range(NCORES)))
    R = res.results
    cat = lambda k: np.stack([np.asarray(R[c][k], dtype=np.float32) for c in range(NCORES)], axis=0)
    y_p = cat("yp")
    y_s = np.concatenate([np.asarray(R[c]["ys"], np.float32) for c in range(NCORES)], axis=0).reshape(128, 1, D)
    kvp0 = cat("kvp0").reshape(1, 8, 128, 2, 4, 128)
    kvp1 = cat("kvp1").reshape(1, 8, 512, 2, 4, 128)
    kvp2 = cat("kvp2").reshape(1, 8, 2048, 2, 4, 128)
    poolp = cat("poolp").reshape(1, 8, 15, 512)
    ccat = lambda k: np.concatenate([np.asarray(R[c][k], np.float32) for c in range(NCORES)], axis=0)
    kvs0 = ccat("kvs0").reshape(1, 128, 128, 2, 4, 128)
    kvs1 = ccat("kvs1").reshape(1, 128, 512, 2, 4, 128)
    kvs2 = ccat("kvs2").reshape(1, 128, 2048, 2, 4, 128)
    pools = ccat("pools").reshape(1, 128, 15, 512)
    return (y_p, y_s, kvp0, kvp1, kvp2, poolp, kvs0, kvs1, kvs2, pools)
```

```python
import contextlib
import numpy as np
import concourse.bass as bass
import concourse.mybir as mybir
from concourse.bass_utils import run_bass_kernel_spmd

F32 = mybir.dt.float32
BF16 = mybir.dt.bfloat16
AF = mybir.ActivationFunctionType
ALU = mybir.AluOpType
AX = mybir.AxisListType

NCORES = 8
D = 2048
NT = 2048
NS = 16
KC = 16
DFF = 8192
Q0, K0, V0, U0, GA0, GB0 = 0, 1536, 3072, 4608, 5120, 7168
EPS = 1e-6
SCALE = 128.0 ** -0.5
DIL = (1, 4, 16)
NCF = 588

_slopes = 2.0 ** (-8.0 * np.arange(1, 13) / 12.0)
_dilh = np.array([1, 1, 1, 1, 4, 4, 4, 4, 16, 16, 16, 16], np.float64)
_ah = _slopes * _dilh


def host_consts():
    CF = np.zeros((128, NCF), np.float32)
    CF[:, 0:128] = np.eye(128)
    j = np.arange(128)
    CF[:, 128:140] = (j[:, None] - 64) * _ah[None, :]
    for g, w in enumerate((2, 4, 8, 16)):
        pos = np.arange(16)
        CF[:, 140 + g * 16:140 + (g + 1) * 16] = 1.0 / np.minimum(pos + 1, w)
    for g, w in enumerate((2, 4, 8, 16)):
        for half in range(2):
            for bl in range(8):
                for row in range(15 - (w - 1), 15):
                    CF[bl * 15 + row, 204 + (g * 2 + half) * 16 + half * 8 + bl] = 1.0
    CF[:, 332:588] = np.eye(16).reshape(1, 256)
    MW = np.zeros((128, 12, 256), np.float32)
    i = np.arange(128)
    for h in range(12):
        MW[:, h, 0:128] = (j[:, None] <= i[None, :]) * np.exp(-_ah[h] * (i[None, :] - 64))
        MW[:, h, 128:256] = (j[:, None] >= i[None, :]) * np.exp(-_ah[h] * (i[None, :] - 64) - 128 * _ah[h])
    BS = np.zeros((16, 12, 129), np.float32)
    pos = np.arange(129)
    for h in range(12):
        BS[:, h, :] = -_ah[h] * (128 - pos)
    return CF, MW.reshape(128, 12 * 256), BS.reshape(16, 12 * 129)


class Tok:
    __slots__ = ("name", "w", "r", "sem", "dcnt")

    def __init__(self, name):
        self.name = name
        self.w = None
        self.r = {}
        self.sem = None
        self.dcnt = 0


class Sched:
    ENG = ("pe", "act", "dve", "pool", "sp")

    def __init__(self, nc, es):
        self.nc = nc
        self.es = es
        self.ops = {e: [] for e in self.ENG}
        self.cnt = {e: 0 for e in self.ENG}
        self.waited = {e: {} for e in self.ENG}
        self.esem = {e: es.enter_context(nc.semaphore("sem_" + e)) for e in self.ENG if e != "sp"}
        self.nsem = 0
        self.final = {}
        self.owners = []
        self.pending = {e: [] for e in self.ENG}

    def barrier(self, exclude=()):
        targets = {("e", e): self.cnt[e] for e in self.ENG if e != "sp"}
        for o in self.owners:
            if o not in exclude:
                targets[("d", o)] = o.dcnt
        for en in self.ENG:
            wd = self.waited[en]
            for k, v in targets.items():
                if v > 0 and k != ("e", en) and wd.get(k, 0) < v:
                    wd[k] = v
                    self.pending[en].append((k, v))

    def _collect(self, eng, reads, writes):
        need = {}

        def add(key, val):
            if need.get(key, 0) < val:
                need[key] = val

        for t in reads:
            if t.w is not None:
                add(*t.w)
        for t in writes:
            if t.w is not None:
                add(*t.w)
            for k, v in t.r.items():
                add(k, v)
        waits = []
        wd = self.waited[eng]
        for key, val in need.items():
            if key == ("e", "pe") and eng == "pe":
                continue
            if wd.get(key, 0) >= val:
                continue
            wd[key] = val
            waits.append((key, val))
        return waits

    def _commit(self, ev, reads, writes):
        for t in writes:
            t.w = ev
            t.r = {}
        for t in reads:
            if t not in writes:
                if t.r.get(ev[0], 0) < ev[1]:
                    t.r[ev[0]] = ev[1]

    def op(self, eng, fn, reads=(), writes=()):
        waits = self._collect(eng, reads, writes)
        self.cnt[eng] += 1
        ev = (("e", eng), self.cnt[eng])
        self._commit(ev, reads, writes)
        waits = self.pending[eng] + waits
        self.pending[eng] = []
        self.ops[eng].append((waits, fn, ("e", eng)))

    def dma(self, q, out, in_, reads=(), writes=(), owner=None, final=False):
        if owner is None:
            owner = writes[0] if writes else reads[0]
        if owner.sem is None:
            owner.sem = self.es.enter_context(self.nc.semaphore("dsem%d" % self.nsem))
            self.nsem += 1
            self.owners.append(owner)
        waits = self._collect(q, reads, writes)
        owner.dcnt += 16
        ev = (("d", owner), owner.dcnt)
        self._commit(ev, reads, writes)
        if final:
            self.final[ev[0]] = ev[1]
        waits = self.pending[q] + waits
        self.pending[q] = []
        self.ops[q].append((waits, lambda e, o=out, i=in_: e.dma_start(out=o, in_=i), ("d", owner)))

    def _sem(self, key):
        return self.esem[key[1]] if key[0] == "e" else key[1].sem

    def emit(self, block):
        engmap = {"pe": block.tensor, "act": block.scalar, "dve": block.vector,
                  "pool": block.gpsimd, "sp": block.sync}
        for en in self.ENG:
            ops = self.ops[en]
            fin = self.final if en == "sp" else None
            pend = self.pending[en]

            def body(eng, ops=ops, fin=fin, pend=pend):
                for waits, fn, sig in ops:
                    for key, val in waits:
                        eng.wait_ge(self._sem(key), val)
                    ins = fn(eng)
                    if sig[0] == "e":
                        ins.then_inc(self.esem[sig[1]], 1)
                    else:
                        ins.then_inc(sig[1].sem, 16)
                for key, val in pend:
                    eng.wait_ge(self._sem(key), val)
                if fin is not None:
                    for key, val in fin.items():
                        eng.wait_ge(self._sem(key), val)

            engmap[en](body)


class Arena:
    def __init__(self, tensor, nbytes):
        self.t = tensor
        self.cap = nbytes
        self.top = 0
        self.rtop = nbytes

    def mark(self):
        return self.top

    def release(self, m):
        self.top = m

    def rmark(self):
        return self.rtop

    def rrelease(self, m):
        self.rtop = m

    def alloc(self, shape, dtype, name="", side="L"):
        esz = 4 if dtype == F32 else 2
        n = 1
        for s in shape[1:]:
            n *= s
        nb = (n * esz + 31) // 32 * 32
        if side == "L":
            off = self.top
            self.top += nb
        else:
            self.rtop -= nb
            off = self.rtop
        assert self.top <= self.rtop, "SBUF arena overflow at %s: L=%d R=%d" % (name, self.top, self.rtop)
        ap = self.t[0:shape[0], off // 2: off // 2 + n * esz // 2]
        if dtype == F32:
            ap = ap.bitcast(F32)
        if len(shape) == 3:
            ap = ap.rearrange("p (a b) -> p a b", b=shape[2])
        elif len(shape) == 4:
            ap = ap.rearrange("p (a b c) -> p a b c", b=shape[2], c=shape[3])
        return ap


class _Stop(Exception):
    pass


import os as _os2
SKIP = int(_os2.environ.get('SKIP', '0'))


def build_program(stage=99):
    nc = bass.Bass("TRN2", target_bir_lowering=False)
    es = contextlib.ExitStack()

    def din(name, shape):
        return nc.dram_tensor(name, list(shape), F32, kind="ExternalInput").ap()

    def dout(name, shape):
        return nc.dram_tensor(name, list(shape), F32, kind="ExternalOutput").ap()

    xp = din("xp", (NT, D))
    xs = din("xs", (NS, D))
    call = din("call", (17, D))
    ckv = [din("ckv0", (NS, 128, 1024)), din("ckv1", (NS, 512, 1024)), din("ckv2", (NS, 2048, 1024))]
    spool = din("spool", (NS * 15, 512))
    w_ada = din("w_ada", (D, 6 * D))
    b_ada = din("b_ada", (96, 128))
    w_in = din("w_in", (D, 9216))
    w_ua = din("w_ua", (512, D))
    w_pool = din("w_pool", (4, 128, 128))
    vecs = din("vecs", (52, 128))
    w_up = din("w_up", (512, D))
    w_out = din("w_out", (D, D))
    w_mu = din("w_mu", (D, DFF))
    w_md = din("w_md", (DFF, D))
    cfd = din("cf", (128, NCF))
    mwd = din("mw", (128, 12 * 256))
    bsd = din("bs", (16, 12 * 129))

    yp = dout("yp", (NT, D))
    ys = dout("ys", (NS, D))
    kvp = [dout("kvp0", (128, 1024)), dout("kvp1", (512, 1024)), dout("kvp2", (2048, 1024))]
    poolp = dout("poolp", (15, 512))
    kvs = [dout("kvs0", (NS, 128, 1024)), dout("kvs1", (NS, 512, 1024)), dout("kvs2", (NS, 2048, 1024))]
    pools = dout("pools", (NS, 15, 512))
    qscr = nc.dram_tensor("qscr", [NS, 1536], F32, kind="Internal").ap()

    ARENA_BYTES = 207 * 1024
    arena_t = es.enter_context(nc.sbuf_tensor("arena", [128, ARENA_BYTES // 2], BF16))
    AR = Arena(arena_t, ARENA_BYTES)
    S = Sched(nc, es)
    psb = []
    for i in range(8):
        t = es.enter_context(nc.psum_tensor("ps%d" % i, [128, 512], F32))
        psb.append((t, Tok("ps%d" % i)))
    pstate = [0]

    def psn():
        t = psb[pstate[0] % 6]
        pstate[0] += 1
        return t

    PSD = psb[6]
    PSE = psb[7]

    cf = AR.alloc([128, NCF], F32, "cf")
    T_cf = Tok("cf")
    ident = cf[:, 0:128]
    cbias = cf[:, 128:140]
    invc = cf[:, 140:204]
    selw = cf[:, 204:332]
    i16 = cf[:, 332:588]
    identb = AR.alloc([128, 128], BF16, "identb")
    onesb = AR.alloc([128, 128], BF16, "onesb")
    T_cb = Tok("constb")
    vT1 = AR.alloc([128, 96], F32, "vT1")
    vT2 = AR.alloc([128, 52], F32, "vT2")
    T_vT = Tok("vT")
    modT = AR.alloc([128, 96, 17], F32, "modT")
    T_mod = Tok("modT")
    A1 = AR.alloc([128, 16, 17], F32, "A1")
    A2 = AR.alloc([128, 16, 17], F32, "A2")
    T_A = Tok("A")
    wpool = AR.alloc([128, 4, 128], BF16, "wpool")
    T_wpool = Tok("wpool")
    oaT = AR.alloc([128, 4, NT], BF16, "oaT")
    T_oa = [Tok("oa%d" % s) for s in range(4)]
    oaTs = AR.alloc([128, 4, NS], BF16, "oaTs")
    T_oas = Tok("oaTs")
    pTs = AR.alloc([128, 4, NS], BF16, "pTs")
    T_pTs = Tok("pTs")
    x1Ts = AR.alloc([128, 16, NS], F32, "x1Ts")
    T_x1s = Tok("x1Ts")
    hTs = AR.alloc([128, 16, NS], BF16, "hTs")
    T_hs = Tok("hTs")
    mixTs = AR.alloc([128, 16, NS], BF16, "mixTs")
    T_mixs = Tok("mixTs")
    smallf = AR.alloc([128, 64], F32, "smallf")
    T_small = Tok("smallf")
    WSLOT = 8 * 1024
    wsl = [(AR.alloc([128, WSLOT // 2], BF16, "ws%d" % i), Tok("ws%d" % i)) for i in range(3)]
    wstate = [0]
    tmpf = [(AR.alloc([128, 512], F32, "tmpf%d" % i), Tok("tmpf%d" % i)) for i in range(3)]
    tstate = [0]

    def tmpn():
        t = tmpf[tstate[0] % 3]
        tstate[0] += 1
        return t

    tmpb = [(AR.alloc([128, 512], BF16, "tmpb%d" % i), Tok("tmpb%d" % i)) for i in range(3)]
    bstate = [0]

    def tmpbn():
        t = tmpb[bstate[0] % 3]
        bstate[0] += 1
        return t

    bhsm = [(AR.alloc([128, 160], F32, "bhsm%d" % i), Tok("bhsm%d" % i)) for i in range(4)]
    bhstate = [0]
    qkvs = AR.alloc([128, 40, NS], F32, "qkvs")
    T_qkvs = Tok("qkvs")

    T_cp = Tok("cpy")
    cplist = []
    for g, W in enumerate((128, 512, 2048)):
        for b in range(NS):
            nrow = W - 1
            r = 0
            while r < nrow:
                n = min(512, nrow - r)
                cplist.append((kvs[g][b, r:r + n, :], ckv[g][b, r + 1:r + 1 + n, :]))
                r += n
    spool3 = spool.rearrange("(b r) c -> b r c", r=15)
    for b in range(NS):
        cplist.append((pools[b, 0:14, :], spool3[b, 1:15, :]))
    cpstate = [0]
    drip_on = [0]

    def drip(n=1):
        for _ in range(n):
            if cpstate[0] < len(cplist):
                o, i_ = cplist[cpstate[0]]
                cpstate[0] += 1
                S.dma("act", o, i_, owner=T_cp, final=True)

    def run_jobs(jobs):
        n = len(jobs)
        base = wstate[0]

        def issue(k):
            sl, tok = wsl[(base + k) % 3]
            for o, i_ in jobs[k][0](sl):
                S.dma("pool", o, i_, writes=[tok])

        for k in range(min(2, n)):
            issue(k)
        for k in range(n):
            if k + 2 < n:
                issue(k + 2)
            sl, tok = wsl[(base + k) % 3]
            if drip_on[0]:
                drip(drip_on[0])
            jobs[k][1](sl, tok)
        wstate[0] = (base + n) % 3

    def wview(sl, kk, cols):
        return sl[:, 0:kk * cols].rearrange("p (k c) -> p k c", c=cols)

    def rows_view(w, r0, nk, c0, cols):
        return w[r0:r0 + nk * 128, c0:c0 + cols].rearrange("(k p) c -> p k c", p=128)

    try:
        S.dma("sp", cf, cfd, writes=[T_cf])
        S.dma("pool", wpool, w_pool.rearrange("g c d -> c g d"), writes=[T_wpool])
        S.op("dve", lambda e: e.tensor_copy(out=identb, in_=ident), reads=[T_cf], writes=[T_cb])
        S.op("dve", lambda e: e.memset(onesb, 1.0), writes=[T_cb])

        m0 = AR.mark()
        call_sb = AR.alloc([17, D], F32, "call_sb")
        T_call = Tok("call")
        sT = AR.alloc([128, 16, 17], BF16, "sT")
        T_sT = Tok("sT")
        v1 = AR.alloc([96, 128], F32, "v1")
        v2 = AR.alloc([52, 128], F32, "v2")
        T_v = Tok("v12")
        S.dma("sp", call_sb, call, writes=[T_call])
        S.dma("sp", v1, b_ada, writes=[T_v])
        S.dma("sp", v2, vecs, writes=[T_v])
        S.op("act", lambda e: e.activation(out=call_sb, in_=call_sb, func=AF.Silu), writes=[T_call])
        pt, ptk = psn()

        def f(e):
            ins = None
            for kc in range(16):
                ins = e.matmul(pt[:, kc * 17:(kc + 1) * 17], call_sb[0:17, kc * 128:(kc + 1) * 128],
                               ident[0:17, 0:17], start=True, stop=True)
            return ins
        S.op("pe", f, reads=[T_call, T_cf], writes=[ptk])
        S.op("dve", lambda e: e.tensor_copy(out=sT, in_=pt[:, 0:272].rearrange("p (k c) -> p k c", c=17)),
             reads=[ptk], writes=[T_sT])
        pt2, ptk2 = psn()

        def f(e):
            e.matmul(pt2[:, 0:96], v1[0:96, :], ident[0:96, 0:96], start=True, stop=True)
            return e.matmul(pt2[:, 96:148], v2[0:52, :], ident[0:52, 0:52], start=True, stop=True)
        S.op("pe", f, reads=[T_v, T_cf], writes=[ptk2])

        def f(e):
            e.tensor_copy(out=vT1, in_=pt2[:, 0:96])
            return e.tensor_copy(out=vT2, in_=pt2[:, 96:148])
        S.op("dve", f, reads=[ptk2], writes=[T_vT])
        g1T = vT2[:, 0:16]
        g2T = vT2[:, 16:32]
        gfT = vT2[:, 32:48]
        pscT = vT2[:, 48:52]

        jobs = []
        for j in range(48):
            def loads(sl, j=j):
                return [(wview(sl, 16, 256), rows_view(w_ada, 0, 16, j * 256, 256))]

            def comp(sl, tok, j=j):
                wv = wview(sl, 16, 256)
                p, pk = psn()

                def f(e):
                    ins = None
                    for cc in range(2):
                        for kc in range(16):
                            ins = e.matmul(p[:, cc * 17:(cc + 1) * 17], wv[:, kc, cc * 128:(cc + 1) * 128],
                                           sT[:, kc, :], start=(kc == 0), stop=(kc == 15))
                    return ins
                S.op("pe", f, reads=[tok, T_sT], writes=[pk])

                def f2(e):
                    ins = None
                    for cc in range(2):
                        c = j * 2 + cc
                        ins = e.tensor_scalar(out=modT[:, c, :], in0=p[:, cc * 17:(cc + 1) * 17],
                                              scalar1=vT1[:, c:c + 1], scalar2=None, op0=ALU.add)
                    return ins
                S.op("dve", f2, reads=[pk, T_vT], writes=[T_mod])
            jobs.append((loads, comp))
        run_jobs(jobs)

        def f(e):
            ins = None
            for kc in range(16):
                e.tensor_scalar(out=A1[:, kc, :], in0=modT[:, 16 + kc, :], scalar1=1.0, scalar2=g1T[:, kc:kc + 1],
                                op0=ALU.add, op1=ALU.mult)
                ins = e.tensor_scalar(out=A2[:, kc, :], in0=modT[:, 64 + kc, :], scalar1=1.0, scalar2=g2T[:, kc:kc + 1],
                                      op0=ALU.add, op1=ALU.mult)
            return ins
        S.op("dve", f, reads=[T_mod, T_vT], writes=[T_A])
        S.barrier(exclude=(T_cp,))
        AR.release(m0)
        if stage <= 0:
            raise _Stop()
        SH1, GT1, SH2, GT2 = 0, 32, 48, 80

        def rstd_from_sum(out_ap, in_ap, toks_r, toks_w):
            S.op("dve", lambda e: e.tensor_scalar(out=out_ap, in0=in_ap, scalar1=1.0 / D, scalar2=EPS,
                                                  op0=ALU.mult, op1=ALU.add), reads=toks_r, writes=toks_w)
            S.op("act", lambda e: e.activation(out=out_ap, in_=out_ap, func=AF.Sqrt), writes=toks_w)
            S.op("dve", lambda e: e.reciprocal(out=out_ap, in_=out_ap), writes=toks_w)

        def build_h(xst, xrows, ntok, h_dst=None, T_h=None, x_dst=None, T_x=None, sample=False):
            xt, xtk = xst[0][xst[1][0] % len(xst[0])]
            xst[1][0] += 1
            xv = xt[0:ntok, :]
            S.dma("sp", xv, xrows, writes=[xtk])
            sm, smk = bhsm[bhstate[0] % 4]
            bhstate[0] += 1
            if h_dst is not None:
                jb, jbk = tmpbn()
                ss4 = sm[0:ntok, 0:4]
                ss = sm[0:ntok, 8:9]
                dm = sm[0:ntok, 16:16 + ntok]

                S.op("dve", lambda e: e.memzero(ss4), writes=[smk])
                jb2, jbk2 = tmpbn()
                jbs = [jb[0:ntok, :], jb2[0:ntok, :]]

                def f(e):
                    ins = None
                    for q in range(4):
                        ins = e.activation(out=jbs[q % 2], in_=xv[:, q * 512:(q + 1) * 512],
                                           func=AF.Square, accum_out=ss4[:, q:q + 1])
                    return ins
                S.op("act", f, reads=[xtk], writes=[jbk, jbk2, smk])
                S.op("dve", lambda e: e.tensor_reduce(out=ss, in_=ss4, axis=AX.X, op=ALU.add), writes=[smk])
                rstd_from_sum(ss, ss, [smk], [smk])
                S.op("dve", lambda e: e.tensor_scalar(out=dm, in0=ident[0:ntok, 0:ntok], scalar1=ss, scalar2=None,
                                                      op0=ALU.mult), reads=[T_cf], writes=[smk])
            for kq in range(4):
                if x_dst is not None:
                    p2, pk2 = psn()

                    def f(e, kq=kq, p2=p2):
                        ins = None
                        for jj in range(4):
                            kc = kq * 4 + jj
                            ins = e.matmul(p2[:, jj * ntok:(jj + 1) * ntok], xv[:, kc * 128:(kc + 1) * 128],
                                           ident[0:ntok, 0:ntok], start=True, stop=True)
                        return ins
                    S.op("pe", f, reads=[xtk, T_cf], writes=[pk2])
                    S.op("act", lambda e, kq=kq, p2=p2: e.activation(
                        out=x_dst(kq), in_=p2[:, 0:4 * ntok].rearrange("p (a b) -> p a b", b=ntok), func=AF.Copy),
                        reads=[pk2], writes=[T_x])
                if h_dst is None:
                    continue
                p, pk = psn()

                def f(e, kq=kq, p=p):
                    ins = None
                    for jj in range(4):
                        kc = kq * 4 + jj
                        ins = e.matmul(p[:, jj * ntok:(jj + 1) * ntok], xv[:, kc * 128:(kc + 1) * 128], dm,
                                       start=True, stop=True)
                    return ins
                S.op("pe", f, reads=[xtk, smk], writes=[pk])
                if not sample:
                    if kq % 2 == 0:
                        def f(e, kq=kq, p=p):
                            ins = None
                            for jj in range(4):
                                kc = kq * 4 + jj
                                ins = e.tensor_scalar(out=h_dst(kc), in0=p[:, jj * ntok:(jj + 1) * ntok],
                                                      scalar1=A1[:, kc, 16:17], scalar2=modT[:, SH1 + kc, 16:17],
                                                      op0=ALU.mult, op1=ALU.add)
                            return ins
                        S.op("dve", f, reads=[pk, T_A, T_mod], writes=[T_h])
                    else:
                        def f(e, kq=kq, p=p):
                            ins = None
                            for jj in range(4):
                                kc = kq * 4 + jj
                                ins = e.activation(out=h_dst(kc), in_=p[:, jj * ntok:(jj + 1) * ntok],
                                                   func=AF.Identity, bias=modT[:, SH1 + kc, 16:17],
                                                   scale=A1[:, kc, 16:17])
                            return ins
                        S.op("act", f, reads=[pk, T_A, T_mod], writes=[T_h])
                else:
                    t2, t2k = tmpn()

                    def f(e, kq=kq, p=p, t2=t2):
                        ins = None
                        for jj in range(4):
                            kc = kq * 4 + jj
                            ins = e.tensor_tensor(out=t2[:, jj * 16:(jj + 1) * 16], in0=p[:, jj * ntok:(jj + 1) * ntok],
                                                  in1=A1[:, kc, 0:16], op=ALU.mult)
                        return ins
                    S.op("dve", f, reads=[pk, T_A], writes=[t2k])

                    def f(e, kq=kq, t2=t2):
                        ins = None
                        for jj in range(4):
                            kc = kq * 4 + jj
                            ins = e.tensor_tensor(out=h_dst(kc), in0=t2[:, jj * 16:(jj + 1) * 16],
                                                  in1=modT[:, SH1 + kc, 0:16], op=ALU.add)
                        return ins
                    S.op("dve", f, reads=[t2k, T_mod], writes=[T_h])

        T_xst = [Tok("xst%d" % i) for i in range(4)]
        T_yst = [Tok("yst0"), Tok("yst1"), Tok("yst2")]

        def make_xst(n):
            return ([(AR.alloc([128, D], F32, "xst%d" % i), T_xst[i]) for i in range(n)], [0])

        mr1 = AR.rmark()
        hT = AR.alloc([128, 16, NT], BF16, "hT", side="R")
        T_h = [Tok("hT%d" % c) for c in range(4)]
        m1a = AR.mark()
        xst = make_xst(4)
        build_h(xst, xs, NS, h_dst=lambda kc: hTs[:, kc, :], T_h=T_hs,
                x_dst=lambda kq: x1Ts[:, kq * 4:(kq + 1) * 4, :], T_x=T_x1s, sample=True)
        for sub in range(16):
            build_h(xst, xp[sub * 128:(sub + 1) * 128, :], 128,
                    h_dst=lambda kc, sub=sub: hT[:, kc, sub * 128:(sub + 1) * 128], T_h=T_h[sub // 4])
        S.barrier(exclude=(T_cp,))
        AR.release(m1a)
        if stage <= 1:
            raise _Stop()

        m1d = AR.mark()
        mw = AR.alloc([128, 12, 256], BF16, "mw")
        T_mw = Tok("mw")
        S.dma("pool", mw, mwd.rearrange("p (h c) -> p h c", c=256), writes=[T_mw])
        qT = AR.alloc([128, 3, NT], BF16, "qT")
        kT = AR.alloc([128, 3, NT], BF16, "kT")
        T_q = [Tok("qT%d" % g) for g in range(3)]
        T_k = [Tok("kT%d" % g) for g in range(3)]
        T_vst = Tok("vst")
        Vt = AR.alloc([128, 3, 16, 128], BF16, "Vt")
        T_V = [Tok("V%d" % g) for g in range(3)]
        acc = AR.alloc([128, 2, NT], F32, "acc")
        T_acc = Tok("acc")
        nmx = AR.alloc([128, 2, 3, 4], F32, "nmx")
        T_nmx = Tok("nmx")
        bcol = AR.alloc([128, 8], F32, "bcol")
        T_bcol = Tok("bcol")
        ptr = [(AR.alloc([128, 256], BF16, "ptr%d" % i), Tok("ptr%d" % i)) for i in range(4)]
        ptm = [(AR.alloc([128, 256], BF16, "ptm%d" % i), Tok("ptm%d" % i)) for i in range(5)]
        pti = [0]

        def deint(ap2, d, t0, n):
            if d == 1:
                return ap2[:, t0:t0 + n]
            return ap2.rearrange("p (r u) -> p u r", r=d)[:, t0 // d:(t0 + n) // d, :]

        def nat(ap2, d):
            if d == 1:
                return ap2
            return ap2.rearrange("p (u r) -> p u r", r=d)

        drip_on[0] = 2
        for s in range(4):
            vst = oaT[:, s, :]
            ajobs = []
            for ty in (1, 2, 0):
                for g in range(3):
                    def loads(sl, ty=ty, s=s, g=g):
                        c0 = (Q0, K0, V0)[ty] + g * 512 + s * 128
                        return [(wview(sl, 16, 128), rows_view(w_in, 0, 16, c0, 128))]

                    def comp(sl, tok, ty=ty, s=s, g=g, vst=vst):
                        wv = wview(sl, 16, 128)
                        d = DIL[g]
                        lates = []
                        for c in range(4):
                            p, pk = psn()

                            def f(e, p=p, c=c):
                                ins = None
                                for kc in range(16):
                                    ins = e.matmul(p[:, :], wv[:, kc, :], hT[:, kc, c * 512:(c + 1) * 512],
                                                   start=(kc == 0), stop=(kc == 15))
                                return ins
                            S.op("pe", f, reads=[tok, T_h[c]], writes=[pk])
                            if ty == 2:
                                dk = T_vst
                                dv = deint(vst, d, c * 512, 512)
                            else:
                                buf = qT if ty == 0 else kT
                                dk = (T_q if ty == 0 else T_k)[g]
                                dv = deint(buf[:, g, :], d, c * 512, 512)
                            if not (SKIP & 1):
                                S.op("dve", lambda e, p=p, dv=dv, d=d: e.tensor_copy(out=dv, in_=nat(p[:, :], d)),
                                     reads=[pk], writes=[dk])
                            if ty != 2 and not (SKIP & 2):
                                sq, sqk = tmpbn()
                                S.op("act", lambda e, p=p, sq=sq: e.activation(out=sq[:, :], in_=p[:, :], func=AF.Square),
                                     reads=[pk, dk], writes=[sqk])

                                def late(sq=sq, sqk=sqk, c=c):
                                    p2, pk2 = psn()
                                    S.op("pe", lambda e: e.matmul(p2[:, :], onesb, sq[:, :], start=True, stop=True),
                                         reads=[sqk, T_cb], writes=[pk2])
                                    S.op("dve", lambda e: e.tensor_reduce(
                                        out=nmx[:, ty, g, c:c + 1], in_=p2[:, :], axis=AX.X, op=ALU.max),
                                        reads=[pk2], writes=[T_nmx])
                                if lates:
                                    lates.pop(0)()
                                lates.append(late)
                        while lates:
                            lates.pop(0)()
                        if ty == 2:
                            for bq in range(4):
                                p, pk = psn()
                                pb = p[:, :].bitcast(BF16)

                                def f(e, pb=pb, bq=bq):
                                    ins = None
                                    for jj in range(4):
                                        blk = bq * 4 + jj
                                        ins = e.transpose(pb[:, jj * 128:(jj + 1) * 128], vst[:, blk * 128:(blk + 1) * 128], identb)
                                    return ins
                                S.op("pe", f, reads=[T_vst, T_cb], writes=[pk])
                                S.op("act", lambda e, pb=pb, bq=bq: e.activation(
                                    out=Vt[:, g, bq * 4:(bq + 1) * 4, :],
                                    in_=pb[:, 0:512].rearrange("p (a b) -> p a b", b=128), func=AF.Copy),
                                    reads=[pk], writes=[T_V[g]])
                        if SKIP & 4:
                            return
                        p, pk = psn()

                        def f(e, p=p):
                            ins = None
                            for kc in range(16):
                                ins = e.matmul(p[:, 0:16], wv[:, kc, :], hTs[:, kc, :], start=(kc == 0), stop=(kc == 15))
                            return ins
                        S.op("pe", f, reads=[tok, T_hs], writes=[pk])
                        S.op("act", lambda e, p=p: e.activation(out=qkvs[:, ty * 12 + g * 4 + s, :], in_=p[:, 0:16], func=AF.Copy),
                             reads=[pk], writes=[T_qkvs])
                    ajobs.append((loads, comp))
            import os as _os
            if stage <= 1.2:
                ajobs = ajobs[:int(_os.environ.get('NJOBS', '9'))]
            run_jobs(ajobs)
            if stage <= 1.2:
                raise _Stop()

            S.op("dve", lambda e: e.tensor_reduce(out=smallf[:, 8:14].rearrange("p (a b) -> p a b", b=3),
                                                  in_=nmx[:, 0:2, :, :], axis=AX.X, op=ALU.max),
                 reads=[T_nmx], writes=[T_small])
            S.op("dve", lambda e: e.tensor_tensor(out=smallf[:, 16:19], in0=smallf[:, 8:11], in1=smallf[:, 11:14],
                                                  op=ALU.mult), writes=[T_small])
            S.op("act", lambda e: e.activation(out=smallf[:, 16:19], in_=smallf[:, 16:19], func=AF.Sqrt), writes=[T_small])

            S.op("dve", lambda e: e.tensor_reduce(out=smallf[:, 20:21], in_=smallf[:, 16:19], axis=AX.X, op=ALU.max),
                 writes=[T_small])

            def f(e, s=s):
                ins = None
                for g in range(3):
                    h = g * 4 + s
                    ins = e.scalar_tensor_tensor(out=bcol[:, g:g + 1], in0=smallf[:, 20:21], scalar=-1.02 * SCALE,
                                                 in1=cbias[:, h:h + 1], op0=ALU.mult, op1=ALU.add)
                return ins
            S.op("dve", f, reads=[T_cf], writes=[T_small, T_bcol])
            if stage <= 1.4:
                raise _Stop()

            units = []
            for g in range(3):
                d = DIL[g]
                nb = 16 // d
                for r in range(d):
                    for j in range(nb):
                        units.append((g, d, r, j, r * nb + j))

            def stage_a(u, s=s):
                g, d, r, j, blk = u
                h = g * 4 + s
                wd = 256 if j > 0 else 128
                pS, pSk = psn()

                def f(e):
                    ins = e.matmul(pS[:, 0:128], kT[:, g, blk * 128:(blk + 1) * 128], qT[:, g, blk * 128:(blk + 1) * 128],
                                   start=True, stop=True)
                    if j > 0:
                        ins = e.matmul(pS[:, 128:256], kT[:, g, (blk - 1) * 128:blk * 128],
                                       qT[:, g, blk * 128:(blk + 1) * 128], start=True, stop=True)
                    return ins
                S.op("pe", f, reads=[T_k[g], T_q[g]], writes=[pSk])
                pr, prk = ptr[pti[0] % len(ptr)]
                pm, pmk = ptm[pti[0] % len(ptm)]
                pti[0] += 1
                S.op("act", lambda e: e.activation(out=pr[:, 0:wd], in_=pS[:, 0:wd], func=AF.Exp, bias=bcol[:, g:g + 1],
                                                   scale=SCALE), reads=[pSk, T_bcol], writes=[prk])
                S.op("dve", lambda e: e.tensor_tensor(out=pm[:, 0:wd], in0=pr[:, 0:wd], in1=mw[:, h, 0:wd], op=ALU.mult),
                     reads=[prk, T_mw], writes=[pmk])
                return pm, pmk

            def stage_b(u, pm, pmk):
                g, d, r, j, blk = u
                pN, pNk = psn()

                def f(e):
                    e.matmul(pN[:, 0:128], Vt[:, g, blk, :], pm[:, 0:128], start=True, stop=(j == 0))
                    if j > 0:
                        e.matmul(pN[:, 0:128], Vt[:, g, blk - 1, :], pm[:, 128:256], start=False, stop=True)
                    ins = e.matmul(pN[:, 128:256], onesb, pm[:, 0:128], start=True, stop=(j == 0))
                    if j > 0:
                        ins = e.matmul(pN[:, 128:256], onesb, pm[:, 128:256], start=False, stop=True)
                    return ins
                S.op("pe", f, reads=[T_V[g], pmk, T_cb], writes=[pNk])
                if d == 1:
                    av = acc[:, :, j * 128:(j + 1) * 128]
                else:
                    av = acc.rearrange("p c (u r) -> p c u r", r=d)[:, :, j * 128:(j + 1) * 128, r]
                pv = pN[:, 0:256].rearrange("p (c q) -> p c q", q=128)
                if g == 0:
                    S.op("act", lambda e: e.activation(out=av, in_=pv, func=AF.Copy), reads=[pNk], writes=[T_acc])
                else:
                    S.op("dve", lambda e: e.tensor_tensor(out=av, in0=av, in1=pv, op=ALU.add), reads=[pNk], writes=[T_acc])

            pend = []
            for u in units:
                pend.append((u, stage_a(u)))
                if len(pend) > 2:
                    u0, st = pend.pop(0)
                    stage_b(u0, *st)
            for u0, st in pend:
                stage_b(u0, *st)
            if stage <= 1.6:
                raise _Stop()
            for c in range(4):
                sl_ = slice(c * 512, (c + 1) * 512)

                S.op("dve", lambda e, sl_=sl_: e.reciprocal(out=acc[:, 1, sl_], in_=acc[:, 1, sl_]),
                     reads=[T_vst], writes=[T_acc, T_vst])
                S.op("dve", lambda e, sl_=sl_, s=s: e.tensor_tensor(out=oaT[:, s, sl_], in0=acc[:, 0, sl_], in1=acc[:, 1, sl_],
                                                                    op=ALU.mult),
                     writes=[T_acc, T_oa[s], T_vst])
        S.barrier(exclude=(T_cp,))
        AR.release(m1d)
        if stage <= 2:
            raise _Stop()

        m1c = AR.mark()
        kvst = [(AR.alloc([128, 256], F32, "kvst%d" % i), Tok("kvst%d" % i)) for i in range(3)]
        kvi = [0]
        kjobs = []
        for g, W in enumerate((128, 512, 2048)):
            for kv in range(2):
                for half in range(2):
                    def loads(sl, g=g, kv=kv, half=half):
                        c0 = (K0 if kv == 0 else V0) + g * 512 + half * 256
                        return [(wview(sl, 16, 256), rows_view(w_in, 0, 16, c0, 256))]

                    def comp(sl, tok, g=g, kv=kv, half=half, W=W):
                        wv = wview(sl, 16, 256)
                        for tt in range(W // 128):
                            t0 = NT - W + tt * 128
                            p, pk = psn()

                            def f(e, p=p, t0=t0):
                                ins = None
                                for kc in range(16):
                                    ins = e.matmul(p[:, 0:256], hT[:, kc, t0:t0 + 128], wv[:, kc, :],
                                                   start=(kc == 0), stop=(kc == 15))
                                return ins
                            S.op("pe", f, reads=[tok, T_h[t0 // 512]], writes=[pk])
                            st, stk = kvst[kvi[0] % 3]
                            kvi[0] += 1
                            if kvi[0] % 2 == 0:
                                S.op("act", lambda e, p=p, st=st: e.activation(out=st[:, :], in_=p[:, 0:256], func=AF.Copy),
                                     reads=[pk], writes=[stk])
                            else:
                                S.op("dve", lambda e, p=p, st=st: e.tensor_copy(out=st[:, :], in_=p[:, 0:256]),
                                     reads=[pk], writes=[stk])
                            S.dma("sp", kvp[g][tt * 128:(tt + 1) * 128, kv * 512 + half * 256:kv * 512 + (half + 1) * 256],
                                  st[:, :], reads=[stk], owner=stk, final=True)
                    kjobs.append((loads, comp))
        run_jobs(kjobs)
        S.barrier(exclude=(T_cp,))
        AR.release(m1c)
        if stage <= 3:
            raise _Stop()

        pT = AR.alloc([128, 4, NT], BF16, "pT")
        T_pT = [Tok("pT%d" % g) for g in range(4)]
        m1b = AR.mark()
        ubuf = AR.alloc([128, 16 + NT], F32, "ubuf")
        T_u = Tok("ubuf")
        sa = AR.alloc([128, 16 + NT], F32, "sa")
        sbb = AR.alloc([128, 16 + NT], F32, "sbb")
        T_sa = Tok("sa")
        T_sb = Tok("sb")
        zb = AR.alloc([128, NT], BF16, "zb")
        T_z = Tok("zb")
        zs = AR.alloc([128, 4, NS], BF16, "zs")
        T_zs = Tok("zs")
        shist = AR.alloc([128, 4, NS], F32, "shist")
        T_sh = Tok("shist")
        sp_sb = [AR.alloc([120, 512], F32, "sp_sb%d" % h) for h in range(2)]
        T_sp = Tok("sp_sb")
        utok = AR.alloc([16, 512], F32, "utok")
        T_utok = Tok("utok")
        utoks = AR.alloc([16, 512], F32, "utoks")
        T_utoks = Tok("utoks")
        for h in range(2):
            S.dma("sp", sp_sb[h], spool[h * 120:(h + 1) * 120, :], writes=[T_sp])

        def f(e):
            e.memzero(ubuf[:, 0:16])
            e.memzero(sa[:, 0:16])
            return e.memzero(sbb[:, 0:16])
        S.op("dve", f, writes=[T_u, T_sa, T_sb])
        p, pk = psn()

        def f(e, p=p):
            ins = None
            for g in range(4):
                for h in range(2):
                    ins = e.matmul(p[:, g * 16:(g + 1) * 16], sp_sb[h][0:120, g * 128:(g + 1) * 128],
                                   selw[0:120, (g * 2 + h) * 16:(g * 2 + h + 1) * 16], start=(h == 0), stop=(h == 1))
            return ins
        S.op("pe", f, reads=[T_sp, T_cf], writes=[pk])
        S.op("dve", lambda e, p=p: e.tensor_copy(out=shist, in_=p[:, 0:64].rearrange("p (g b) -> p g b", b=16)),
             reads=[pk], writes=[T_sh])
        pu_tok, pu_tokk = PSD
        pus_tok, pus_tokk = PSE

        ujobs = []
        for jh in range(2):
            def loads(sl, jh=jh):
                return [(wview(sl, 16, 256), rows_view(w_in, 0, 16, U0 + jh * 256, 256))]

            def comp(sl, tok, jh=jh):
                wv = wview(sl, 16, 256)
                for gg in range(2):
                    g = jh * 2 + gg
                    w = 2 << g
                    for c in range(4):
                        p, pk = psn()

                        def f(e, p=p, c=c, gg=gg):
                            ins = None
                            for kc in range(16):
                                ins = e.matmul(p[:, :], wv[:, kc, gg * 128:(gg + 1) * 128], hT[:, kc, c * 512:(c + 1) * 512],
                                               start=(kc == 0), stop=(kc == 15))
                            return ins
                        S.op("pe", f, reads=[tok, T_h[c]], writes=[pk])
                        S.op("act", lambda e, p=p, c=c: e.activation(out=ubuf[:, 16 + c * 512:16 + (c + 1) * 512],
                                                                     in_=p[:, :], func=AF.Copy),
                             reads=[pk], writes=[T_u])
                    p, pk = psn()

                    def f(e, p=p, gg=gg):
                        ins = None
                        for kc in range(16):
                            ins = e.matmul(p[:, 0:16], wv[:, kc, gg * 128:(gg + 1) * 128], hTs[:, kc, :],
                                           start=(kc == 0), stop=(kc == 15))
                        return ins
                    S.op("pe", f, reads=[tok, T_hs], writes=[pk])
                    S.op("act", lambda e, p=p, g=g: e.activation(out=qkvs[:, 36 + g, :], in_=p[:, 0:16], func=AF.Copy),
                         reads=[pk], writes=[T_qkvs])
                    S.op("pe", lambda e, g=g: e.matmul(pu_tok[0:16, g * 128:(g + 1) * 128], ubuf[:, NT:NT + 16], ident,
                                                       start=True, stop=True), reads=[T_u, T_cf], writes=[pu_tokk])
                    S.op("pe", lambda e, g=g: e.matmul(pus_tok[0:16, g * 128:(g + 1) * 128], qkvs[:, 36 + g, :], ident,
                                                       start=True, stop=True), reads=[T_qkvs, T_cf], writes=[pus_tokk])
                    src, srck = ubuf, T_u
                    bufs = [(sa, T_sa), (sbb, T_sb)]
                    sh = 1
                    for step in range(g + 1):
                        dst, dstk = bufs[step % 2]
                        S.op("dve", lambda e, src=src, dst=dst, sh=sh: e.tensor_tensor(
                            out=dst[:, 16:16 + NT], in0=src[:, 16:16 + NT], in1=src[:, 16 - sh:16 - sh + NT], op=ALU.add),
                            reads=[srck], writes=[dstk])
                        src, srck = dst, dstk
                        sh *= 2
                    S.op("dve", lambda e, src=src, w=w: e.scalar_tensor_tensor(
                        out=zb[:, :], in0=src[:, 16:16 + NT], scalar=1.0 / w, in1=ubuf[:, 16:16 + NT],
                        op0=ALU.mult, op1=ALU.subtract), reads=[srck, T_u], writes=[T_z])
                    tt, ttk = tmpn()

                    S.op("dve", lambda e, src=src, g=g, tt=tt: e.tensor_tensor(
                        out=tt[:, 0:16], in0=src[:, 16:32], in1=invc[:, g * 16:(g + 1) * 16], op=ALU.mult),
                        reads=[srck, T_cf], writes=[ttk])
                    S.op("dve", lambda e, tt=tt: e.tensor_tensor(out=zb[:, 0:16], in0=tt[:, 0:16], in1=ubuf[:, 16:32],
                                                                 op=ALU.subtract), reads=[ttk, T_u], writes=[T_z])
                    for c in range(4):
                        p, pk = psn()
                        S.op("pe", lambda e, p=p, c=c, g=g: e.matmul(p[:, :], wpool[:, g, :], zb[:, c * 512:(c + 1) * 512],
                                                                     start=True, stop=True),
                             reads=[T_wpool, T_z], writes=[pk])
                        S.op("act", lambda e, p=p, c=c, g=g: e.activation(
                            out=pT[:, g, c * 512:(c + 1) * 512], in_=p[:, :], func=AF.Identity, scale=pscT[:, g:g + 1]),
                            reads=[pk, T_vT], writes=[T_pT[g]])
                    tt, ttk = tmpn()

                    S.op("dve", lambda e, g=g, tt=tt: e.tensor_tensor(out=tt[:, 0:16], in0=shist[:, g, :],
                                                                      in1=qkvs[:, 36 + g, :], op=ALU.add),
                         reads=[T_sh, T_qkvs], writes=[ttk])
                    S.op("dve", lambda e, g=g, w=w, tt=tt: e.scalar_tensor_tensor(
                        out=zs[:, g, :], in0=tt[:, 0:16], scalar=1.0 / w, in1=qkvs[:, 36 + g, :],
                        op0=ALU.mult, op1=ALU.subtract), reads=[ttk, T_qkvs], writes=[T_zs])
                    p, pk = psn()
                    S.op("pe", lambda e, p=p, g=g: e.matmul(p[:, 0:16], wpool[:, g, :], zs[:, g, :], start=True, stop=True),
                         reads=[T_wpool, T_zs], writes=[pk])
                    S.op("act", lambda e, p=p, g=g: e.activation(out=pTs[:, g, :], in_=p[:, 0:16], func=AF.Identity,
                                                                 scale=pscT[:, g:g + 1]),
                         reads=[pk, T_vT], writes=[T_pTs])
            ujobs.append((loads, comp))
        run_jobs(ujobs)
        S.op("dve", lambda e: e.tensor_copy(out=utok, in_=pu_tok[0:16, :]), reads=[pu_tokk], writes=[T_utok])
        S.op("dve", lambda e: e.tensor_copy(out=utoks, in_=pus_tok[0:16, :]), reads=[pus_tokk], writes=[T_utoks])
        S.dma("sp", poolp, utok[1:16, :], reads=[T_utok], owner=T_utok, final=True)
        S.dma("sp", pools[:, 14, :], utoks, reads=[T_utoks], owner=T_utoks, final=True)
        S.barrier(exclude=(T_cp,))
        AR.release(m1b)
        AR.rrelease(mr1)
        if stage <= 4:
            raise _Stop()

        msa = AR.mark()
        tokq = AR.alloc([16, 3, 1536], F32, "tokq")
        T_tokq = Tok("tokq")
        for ty in range(3):
            for g in range(3):
                p, pk = psn()

                def f(e, p=p, ty=ty, g=g):
                    ins = None
                    for hh in range(4):
                        ins = e.matmul(p[0:16, hh * 128:(hh + 1) * 128], qkvs[:, ty * 12 + g * 4 + hh, :], ident,
                                       start=True, stop=True)
                    return ins
                S.op("pe", f, reads=[T_qkvs, T_cf], writes=[pk])
                S.op("act", lambda e, p=p, ty=ty, g=g: e.activation(out=tokq[:, ty, g * 512:(g + 1) * 512], in_=p[0:16, :],
                                                                    func=AF.Copy), reads=[pk], writes=[T_tokq])
        T_qscr = Tok("qscr")
        S.dma("sp", qscr, tokq[:, 0, :], reads=[T_tokq], writes=[T_qscr])
        for g, W in enumerate((128, 512, 2048)):
            S.dma("sp", kvs[g][:, W - 1, 0:512], tokq[:, 1, g * 512:(g + 1) * 512], reads=[T_tokq], owner=T_tokq, final=True)
            S.dma("sp", kvs[g][:, W - 1, 512:1024], tokq[:, 2, g * 512:(g + 1) * 512], reads=[T_tokq], owner=T_tokq, final=True)
        bs_sb = AR.alloc([16, 12, 129], F32, "bs_sb")
        T_bs = Tok("bs")
        S.dma("sp", bs_sb, bsd.rearrange("p (h c) -> p h c", c=129), writes=[T_bs])
        sT_all = AR.alloc([128, 16, 12], F32, "sT_all")
        T_sTa = Tok("sT_all")
        stok = AR.alloc([16, 12, 129], F32, "stok")
        T_stok = Tok("stok")
        sm16 = AR.alloc([16, 64], F32, "sm16")
        T_sm16 = Tok("sm16")
        prodt = AR.alloc([16, 1536], F32, "prodt")
        T_prodt = Tok("prodt")
        kh = [(AR.alloc([128, 512], F32, "kh%d" % i), Tok("kh%d" % i)) for i in range(2)]
        qb = [(AR.alloc([128, 512], F32, "qb%d" % i), Tok("qb%d" % i)) for i in range(2)]
        wT = AR.alloc([128, 12, 16], F32, "wT")
        T_wT = Tok("wT")
        wTm = AR.alloc([128, 12, 16, 16], BF16, "wTm")
        T_wTm = Tok("wTm")
        otok = AR.alloc([16, 512], F32, "otok")
        T_otok = Tok("otok")

        S.op("dve", lambda e: e.tensor_tensor(out=prodt, in0=tokq[:, 0, :], in1=tokq[:, 1, :], op=ALU.mult),
             reads=[T_tokq], writes=[T_prodt])
        S.op("dve", lambda e: e.tensor_reduce(out=stok[:, :, 128:129], in_=prodt.rearrange("p (h x) -> p h x", x=128),
                                              axis=AX.X, op=ALU.add), reads=[T_prodt], writes=[T_stok])
        it = 0
        for b in range(NS):
            for g in range(3):
                d = DIL[g]
                kt, ktk = kh[it % 2]
                qt, qtk = qb[it % 2]
                it += 1
                S.dma("sp", kt, ckv[g][b, :, 0:512].rearrange("(j x) c -> j x c", x=d)[:, 0, :], writes=[ktk])
                S.dma("sp", qt, qscr[b:b + 1, g * 512:(g + 1) * 512].partition_broadcast(128), reads=[T_qscr], writes=[qtk])

                S.op("dve", lambda e, kt=kt, qt=qt: e.tensor_tensor(out=kt, in0=kt, in1=qt, op=ALU.mult),
                     reads=[qtk], writes=[ktk])
                S.op("dve", lambda e, kt=kt, b=b, g=g: e.tensor_reduce(
                    out=sT_all[:, b, g * 4:(g + 1) * 4], in_=kt.rearrange("p (h x) -> p h x", x=128), axis=AX.X, op=ALU.add),
                    reads=[ktk], writes=[T_sTa])
        for g3 in range(3):
            p, pk = psn()

            def f(e, p=p, g3=g3):
                ins = None
                for hh in range(4):
                    ins = e.matmul(p[0:16, hh * 128:(hh + 1) * 128], sT_all[:, :, g3 * 4 + hh], ident, start=True, stop=True)
                return ins
            S.op("pe", f, reads=[T_sTa, T_cf], writes=[pk])
            S.op("dve", lambda e, p=p, g3=g3: e.tensor_copy(
                out=stok[:, g3 * 4:(g3 + 1) * 4, 0:128], in_=p[0:16, :].rearrange("p (h x) -> p h x", x=128)),
                reads=[pk], writes=[T_stok])
        mx = sm16[:, 0:12]
        den = sm16[:, 12:24]
        Mx = sm16[:, 24:28]
        fco = sm16[:, 28:40]
        dtot = sm16[:, 40:44]

        def dv(fn, reads=(), writes=()):
            S.op("dve", fn, reads=list(reads), writes=list(writes))

        g3v = lambda ap: ap.rearrange("p (g s) -> p g s", s=4)
        s3v = lambda ap: ap.rearrange("p (g s) -> p s g", s=4)
        dv(lambda e: e.scalar_tensor_tensor(out=stok, in0=stok, scalar=SCALE, in1=bs_sb, op0=ALU.mult, op1=ALU.add),
           [T_bs], [T_stok])
        dv(lambda e: e.tensor_reduce(out=mx, in_=stok, axis=AX.X, op=ALU.max), [T_stok], [T_sm16])
        dv(lambda e: e.tensor_tensor(out=stok, in0=stok, in1=mx.unsqueeze(2).to_broadcast([16, 12, 129]), op=ALU.subtract),
           [T_sm16], [T_stok])
        S.op("act", lambda e: e.activation(out=stok, in_=stok, func=AF.Exp), writes=[T_stok])
        dv(lambda e: e.tensor_reduce(out=den, in_=stok, axis=AX.X, op=ALU.add), [T_stok], [T_sm16])
        dv(lambda e: e.tensor_reduce(out=Mx, in_=s3v(mx), axis=AX.X, op=ALU.max), [], [T_sm16])
        dv(lambda e: e.tensor_tensor(out=g3v(fco), in0=g3v(mx), in1=Mx.unsqueeze(1).to_broadcast([16, 3, 4]), op=ALU.subtract),
           [], [T_sm16])
        S.op("act", lambda e: e.activation(out=fco, in_=fco, func=AF.Exp), writes=[T_sm16])
        dv(lambda e: e.tensor_tensor(out=den, in0=den, in1=fco, op=ALU.mult), [], [T_sm16])
        dv(lambda e: e.tensor_reduce(out=dtot, in_=s3v(den), axis=AX.X, op=ALU.add), [], [T_sm16])
        dv(lambda e: e.reciprocal(out=dtot, in_=dtot), [], [T_sm16])
        dv(lambda e: e.tensor_tensor(out=g3v(fco), in0=g3v(fco), in1=dtot.unsqueeze(1).to_broadcast([16, 3, 4]), op=ALU.mult),
           [], [T_sm16])
        dv(lambda e: e.tensor_tensor(out=stok, in0=stok, in1=fco.unsqueeze(2).to_broadcast([16, 12, 129]), op=ALU.mult),
           [T_sm16], [T_stok])
        p, pk = psn()

        def f(e, p=p):
            ins = None
            for hh in range(12):
                ins = e.matmul(p[:, hh * 16:(hh + 1) * 16], stok[:, hh, 0:128], ident[0:16, 0:16], start=True, stop=True)
            return ins
        S.op("pe", f, reads=[T_stok, T_cf], writes=[pk])
        S.op("dve", lambda e, p=p: e.tensor_copy(out=wT, in_=p[:, 0:192].rearrange("p (h b) -> p h b", b=16)),
             reads=[pk], writes=[T_wT])

        def f(e):
            ins = None
            for bp in range(16):
                ins = e.tensor_tensor(out=wTm[:, :, bp, :], in0=wT,
                                      in1=i16[:, bp * 16:(bp + 1) * 16].unsqueeze(1).to_broadcast([128, 12, 16]), op=ALU.mult)
            return ins
        S.op("dve", f, reads=[T_wT, T_cf], writes=[T_wTm])
        pO, pOk = PSD
        vall = AR.alloc([128, 48, 512], BF16, "vall")
        T_vall = Tok("vall")
        for b in range(NS):
            for g in range(3):
                d = DIL[g]
                S.dma("pool", vall[:, b * 3 + g, :], ckv[g][b, :, 512:1024].rearrange("(j x) c -> j x c", x=d)[:, 0, :],
                      writes=[T_vall])

        def f(e):
            ins = None
            for hh in range(4):
                for b in range(NS):
                    for g in range(3):
                        first = (b == 0 and g == 0)
                        last = (b == NS - 1 and g == 2)
                        ins = e.matmul(pO[0:16, hh * 128:(hh + 1) * 128], wTm[:, g * 4 + hh, b, :],
                                       vall[:, b * 3 + g, hh * 128:(hh + 1) * 128], start=first, stop=last)
            return ins
        S.op("pe", f, reads=[T_vall, T_wTm], writes=[pOk])
        dv(lambda e: e.tensor_tensor(out=prodt.rearrange("p (h x) -> p h x", x=128),
                                     in0=tokq[:, 2, :].rearrange("p (h x) -> p h x", x=128),
                                     in1=stok[:, :, 128:129].to_broadcast([16, 12, 128]), op=ALU.mult),
           [T_stok, T_tokq], [T_prodt])
        dv(lambda e: e.tensor_tensor(out=otok, in0=pO[0:16, :], in1=prodt[:, 0:512], op=ALU.add), [pOk, T_prodt], [T_otok])
        dv(lambda e: e.tensor_tensor(out=otok, in0=otok, in1=prodt[:, 512:1024], op=ALU.add), [T_prodt], [T_otok])
        dv(lambda e: e.tensor_tensor(out=otok, in0=otok, in1=prodt[:, 1024:1536], op=ALU.add), [T_prodt], [T_otok])
        p, pk = psn()

        def f(e, p=p):
            ins = None
            for hh in range(4):
                ins = e.matmul(p[:, hh * 16:(hh + 1) * 16], otok[:, hh * 128:(hh + 1) * 128], ident[0:16, 0:16], start=True, stop=True)
            return ins
        S.op("pe", f, reads=[T_otok, T_cf], writes=[pk])
        S.op("dve", lambda e, p=p: e.tensor_copy(out=oaTs, in_=p[:, 0:64].rearrange("p (h b) -> p h b", b=16)),
             reads=[pk], writes=[T_oas])
        S.barrier(exclude=(T_cp,))
        AR.release(msa)
        if stage <= 5:
            raise _Stop()

        class Chunk:
            pass

        drip_on[0] = 1

        def resid_update(ck, p, pk, fc, GT):
            n = ck.n
            if not ck.sample:
                S.op("dve", lambda e: e.scalar_tensor_tensor(out=ck.x1(fc), in0=p[:, 0:n], scalar=modT[:, GT + fc, 16:17],
                                                             in1=ck.x1(fc), op0=ALU.mult, op1=ALU.add),
                     reads=[pk, T_mod], writes=[ck.T_x1])
            else:
                tt, ttk = tmpn()

                S.op("dve", lambda e: e.tensor_tensor(out=tt[:, 0:n], in0=p[:, 0:n], in1=modT[:, GT + fc, 0:16], op=ALU.mult),
                     reads=[pk, T_mod], writes=[ttk])
                S.op("dve", lambda e: e.tensor_tensor(out=ck.x1(fc), in0=ck.x1(fc), in1=tt[:, 0:n], op=ALU.add),
                     reads=[ttk], writes=[ck.T_x1])

        for tile in range(2):
            t0 = tile * 1024
            mt = AR.mark()
            mrt = AR.rmark()
            mixT = AR.alloc([128, 16, 1024], BF16, "mixT")
            T_mix = [Tok("mixT%d_%d" % (tile, c)) for c in range(2)]
            mh = AR.mark()
            hT2 = AR.alloc([128, 16, 1024], BF16, "hT2")
            T_h2 = [Tok("hT2%d_%d" % (tile, c)) for c in range(2)]
            T_x1 = [Tok("x1T%d_%d" % (tile, c)) for c in range(2)]
            T_aT = [Tok("aT%d_%d" % (tile, i)) for i in range(2)]
            T_aTs = [Tok("aTs%d_%d" % (tile, i)) for i in range(2)]
            def mk_chunks(hbuf, x1buf, aTl, aTsl, t0=t0, tile=tile, mixT=mixT, T_mix=T_mix, T_h2=T_h2, T_x1=T_x1,
                          T_aT=T_aT, T_aTs=T_aTs):
                chunks = []
                for c in range(2):
                    ck = Chunk()
                    ck.sample = False
                    ck.n = 512
                    ck.tok0 = t0 + c * 512
                    ck.x1 = lambda kc, c=c: x1buf[:, kc, c * 512:(c + 1) * 512]
                    ck.T_x1 = T_x1[c]
                    ck.h = lambda kc, c=c: hbuf[:, kc, c * 512:(c + 1) * 512]
                    ck.T_h = T_h2[c]
                    ck.mix = lambda kc, c=c: mixT[:, kc, c * 512:(c + 1) * 512]
                    ck.T_mix = T_mix[c]
                    ck.oa = lambda kc, ck=ck: oaT[:, kc, ck.tok0:ck.tok0 + 512]
                    ck.T_oa = T_oa
                    ck.pp = lambda kc, ck=ck: pT[:, kc, ck.tok0:ck.tok0 + 512]
                    ck.T_pp = T_pT
                    ck.a = lambda i, cc, c=c: aTl[i][:, cc, c * 512:(c + 1) * 512]
                    ck.T_a = T_aT
                    chunks.append(ck)
                if tile == 0:
                    ck = Chunk()
                    ck.sample = True
                    ck.n = NS
                    ck.x1 = lambda kc: x1Ts[:, kc, :]
                    ck.T_x1 = T_x1s
                    ck.h = lambda kc: hTs[:, kc, :]
                    ck.T_h = T_hs
                    ck.mix = lambda kc: mixTs[:, kc, :]
                    ck.T_mix = T_mixs
                    ck.oa = lambda kc: oaTs[:, kc, :]
                    ck.T_oa = [T_oas] * 4
                    ck.pp = lambda kc: pTs[:, kc, :]
                    ck.T_pp = [T_pTs] * 4
                    ck.a = lambda i, cc: aTsl[i][:, cc, :]
                    ck.T_a = T_aTs
                    chunks.append(ck)
                return chunks

            chunks = mk_chunks(hT2, None, None, None)

            mx_ = AR.mark()
            xst = make_xst(4)
            for c in range(2):
                for sub in range(4):
                    tk = t0 + c * 512 + sub * 128
                    build_h(xst, xp[tk:tk + 128, :], 128,
                            h_dst=lambda kc, c=c, sub=sub: hT2[:, kc, c * 512 + sub * 128:c * 512 + (sub + 1) * 128],
                            T_h=T_h2[c])
            S.barrier(exclude=(T_cp,))
            AR.release(mx_)

            jobs = []
            for fc in range(16):
                for br in range(2):
                    def loads(sl, fc=fc, br=br):
                        return [(sl[:, 0:2048].rearrange("p (k c) -> p k c", c=128),
                                 rows_view(w_in, 0, 16, (GA0, GB0)[br] + fc * 128, 128)),
                                (sl[:, 2048:2560].rearrange("p (k c) -> p k c", c=128),
                                 rows_view((w_ua, w_up)[br], 0, 4, fc * 128, 128))]

                    def comp(sl, tok, fc=fc, br=br):
                        wg = sl[:, 0:2048].rearrange("p (k c) -> p k c", c=128)
                        wu = sl[:, 2048:2560].rearrange("p (k c) -> p k c", c=128)
                        for ck in chunks:
                            n = ck.n
                            pG, pGk = psn()
                            pU, pUk = psn()

                            def f(e, pG=pG, ck=ck, n=n):
                                ins = None
                                for kc in range(16):
                                    ins = e.matmul(pG[:, 0:n], wg[:, kc, :], ck.h(kc), start=(kc == 0), stop=(kc == 15))
                                return ins
                            S.op("pe", f, reads=[tok, ck.T_h], writes=[pGk])
                            src = ck.oa if br == 0 else ck.pp
                            srck = ck.T_oa if br == 0 else ck.T_pp

                            def f(e, pU=pU, src=src, n=n):
                                ins = None
                                for kc in range(4):
                                    ins = e.matmul(pU[:, 0:n], wu[:, kc, :], src(kc), start=(kc == 0), stop=(kc == 3))
                                return ins
                            S.op("pe", f, reads=[tok] + list(srck), writes=[pUk])
                            sg, sgk = tmpn()
                            S.op("act", lambda e, pG=pG, sg=sg, n=n: e.activation(out=sg[:, 0:n], in_=pG[:, 0:n], func=AF.Sigmoid),
                                 reads=[pGk], writes=[sgk])
                            if br == 0:
                                S.op("dve", lambda e, pU=pU, sg=sg, n=n, ck=ck: e.tensor_tensor(
                                    out=ck.mix(fc), in0=sg[:, 0:n], in1=pU[:, 0:n], op=ALU.mult),
                                    reads=[pUk, sgk], writes=[ck.T_mix])
                            else:
                                S.op("dve", lambda e, pU=pU, sg=sg, n=n: e.tensor_tensor(
                                    out=sg[:, 0:n], in0=sg[:, 0:n], in1=pU[:, 0:n], op=ALU.mult), reads=[pUk], writes=[sgk])
                                S.op("dve", lambda e, sg=sg, n=n, ck=ck: e.tensor_tensor(
                                    out=ck.mix(fc), in0=ck.mix(fc), in1=sg[:, 0:n], op=ALU.add), reads=[sgk], writes=[ck.T_mix])
                    jobs.append((loads, comp))
            run_jobs(jobs)
            S.barrier(exclude=(T_cp,))
            AR.release(mh)

            x1T = AR.alloc([128, 16, 1024], F32, "x1T", side="R")
            chunks = mk_chunks(None, x1T, None, None)
            mx_ = AR.mark()
            xst = make_xst(2)
            for c in range(2):
                for sub in range(4):
                    tk = t0 + c * 512 + sub * 128
                    build_h(xst, xp[tk:tk + 128, :], 128,
                            x_dst=lambda kq, c=c, sub=sub: x1T[:, kq * 4:(kq + 1) * 4, c * 512 + sub * 128:c * 512 + (sub + 1) * 128],
                            T_x=T_x1[c])
            S.barrier(exclude=(T_cp,))
            AR.release(mx_)

            jobs = []
            for jc in range(8):
                def loads(sl, jc=jc):
                    return [(wview(sl, 16, 256), rows_view(w_out, 0, 16, jc * 256, 256))]

                def comp(sl, tok, jc=jc):
                    wv = wview(sl, 16, 256)
                    for ff in range(2):
                        fc = jc * 2 + ff
                        for ck in chunks:
                            n = ck.n
                            p, pk = psn()

                            def f(e, p=p, ck=ck, n=n, ff=ff):
                                ins = None
                                for kc in range(16):
                                    ins = e.matmul(p[:, 0:n], wv[:, kc, ff * 128:(ff + 1) * 128], ck.mix(kc),
                                                   start=(kc == 0), stop=(kc == 15))
                                return ins
                            S.op("pe", f, reads=[tok, ck.T_mix], writes=[pk])
                            resid_update(ck, p, pk, fc, GT1)
                jobs.append((loads, comp))
            run_jobs(jobs)
            S.barrier(exclude=(T_cp,))
            AR.release(mt)

            h2T = AR.alloc([128, 16, 1024], BF16, "h2T")
            aTl = [AR.alloc([128, 2, 1024], BF16, "aT%d" % i) for i in range(2)]
            aTsl = [AR.alloc([128, 2, NS], BF16, "aTs%d" % i) for i in range(2)]
            chunks = mk_chunks(h2T, x1T, aTl, aTsl)
            rstdb = AR.alloc([128, 512], F32, "rstdb")
            T_rstdb = Tok("rstdb%d" % tile)
            yst = [(AR.alloc([128, 512], F32, "yst%d" % i), T_yst[i]) for i in range(3)]
            ysi = [0]

            for ck in chunks:
                n = ck.n
                pS_, pSk_ = psn()
                for kc in range(16):
                    sq, sqk = tmpbn()
                    S.op("act", lambda e, sq=sq, ck=ck, kc=kc, n=n: e.activation(out=sq[:, 0:n], in_=ck.x1(kc), func=AF.Square),
                         reads=[ck.T_x1], writes=[sqk])
                    S.op("pe", lambda e, sq=sq, kc=kc, n=n, pS_=pS_: e.matmul(pS_[:, 0:n], onesb, sq[:, 0:n], start=(kc == 0),
                                                                               stop=(kc == 15)), reads=[sqk, T_cb], writes=[pSk_])
                rstd_from_sum(rstdb[:, 0:n], pS_[:, 0:n], [pSk_], [T_rstdb])
                for kc in range(16):
                    tt, ttk = tmpn()
                    S.op("dve", lambda e, tt=tt, ck=ck, kc=kc, n=n: e.tensor_tensor(out=tt[:, 0:n], in0=ck.x1(kc), in1=rstdb[:, 0:n],
                                                                                    op=ALU.mult),
                         reads=[ck.T_x1, T_rstdb], writes=[ttk])
                    if not ck.sample:
                        S.op("act", lambda e, tt=tt, ck=ck, kc=kc, n=n: e.activation(
                            out=ck.h(kc), in_=tt[:, 0:n], func=AF.Identity, bias=modT[:, SH2 + kc, 16:17], scale=A2[:, kc, 16:17]),
                            reads=[ttk, T_A, T_mod], writes=[ck.T_h])
                    else:
                        S.op("dve", lambda e, tt=tt, kc=kc, n=n: e.tensor_tensor(out=tt[:, 0:n], in0=tt[:, 0:n],
                                                                                 in1=A2[:, kc, 0:16], op=ALU.mult),
                             reads=[T_A], writes=[ttk])
                        S.op("dve", lambda e, tt=tt, ck=ck, kc=kc, n=n: e.tensor_tensor(
                            out=ck.h(kc), in0=tt[:, 0:n], in1=modT[:, SH2 + kc, 0:16], op=ALU.add),
                            reads=[ttk, T_mod], writes=[ck.T_h])

            NG = DFF // 256
            jobs = []

            def up_job(g):
                def loads(sl):
                    return [(wview(sl, 16, 256), rows_view(w_mu, 0, 16, g * 256, 256))]

                def comp(sl, tok):
                    wv = wview(sl, 16, 256)
                    for cc in range(2):
                        for ck in chunks:
                            n = ck.n
                            p, pk = psn()

                            def f(e, p=p, ck=ck, n=n, cc=cc):
                                ins = None
                                for kc in range(16):
                                    ins = e.matmul(p[:, 0:n], wv[:, kc, cc * 128:(cc + 1) * 128], ck.h(kc),
                                                   start=(kc == 0), stop=(kc == 15))
                                return ins
                            S.op("pe", f, reads=[tok, ck.T_h], writes=[pk])
                            rl, rlk = tmpbn()
                            S.op("act", lambda e, p=p, rl=rl, n=n: e.activation(out=rl[:, 0:n], in_=p[:, 0:n], func=AF.Relu),
                                 reads=[pk], writes=[rlk])
                            S.op("dve", lambda e, rl=rl, ck=ck, n=n, cc=cc: e.tensor_tensor(
                                out=ck.a(g % 2, cc), in0=rl[:, 0:n], in1=rl[:, 0:n], op=ALU.mult),
                                reads=[rlk], writes=[ck.T_a[g % 2]])
                return (loads, comp)

            def down_job(g):
                def loads(sl):
                    return [(wview(sl, 2, 2048), rows_view(w_md, g * 256, 2, 0, 2048))]

                def comp(sl, tok):
                    wv = wview(sl, 2, 2048)
                    for fc in range(16):
                        for ck in chunks:
                            n = ck.n
                            p, pk = psn()

                            def f(e, p=p, ck=ck, n=n, fc=fc):
                                ins = None
                                for cc in range(2):
                                    ins = e.matmul(p[:, 0:n], wv[:, cc, fc * 128:(fc + 1) * 128], ck.a(g % 2, cc),
                                                   start=(cc == 0), stop=(cc == 1))
                                return ins
                            S.op("pe", f, reads=[tok, ck.T_a[g % 2]], writes=[pk])
                            resid_update(ck, p, pk, fc, GT2)
                return (loads, comp)

            for g in range(NG):
                jobs.append(up_job(g))
                if g > 0:
                    jobs.append(down_job(g - 1))
            jobs.append(down_job(NG - 1))
            run_jobs(jobs)

            for ck in chunks:
                n = ck.n
                for kc in range(16):
                    S.op("act", lambda e, ck=ck, kc=kc: e.activation(out=ck.h(kc), in_=ck.x1(kc), func=AF.Square),
                         reads=[ck.T_x1], writes=[ck.T_h])
                    S.op("dve", lambda e, ck=ck, kc=kc: e.tensor_scalar(out=ck.x1(kc), in0=ck.x1(kc), scalar1=gfT[:, kc:kc + 1],
                                                                        scalar2=None, op0=ALU.mult),
                         reads=[ck.T_h, T_vT], writes=[ck.T_x1])
                nsub = 1 if ck.sample else 4
                m = NS if ck.sample else 128
                for sub in range(nsub):
                    pS_, pSk_ = psn()

                    def f(e, pS_=pS_, ck=ck, sub=sub, m=m):
                        ins = None
                        for kc in range(16):
                            ins = e.matmul(pS_[0:m, 0:1], ck.h(kc)[:, sub * m:(sub + 1) * m], onesb[:, 0:1],
                                           start=(kc == 0), stop=(kc == 15))
                        return ins
                    S.op("pe", f, reads=[ck.T_h, T_cb], writes=[pSk_])
                    rs, rsk = bhsm[bhstate[0] % 4]
                    bhstate[0] += 1
                    rstd_from_sum(rs[0:m, 0:1], pS_[0:m, 0:1], [pSk_], [rsk])
                    for nq in range(4):
                        p, pk = psn()

                        def f(e, p=p, ck=ck, sub=sub, m=m, nq=nq):
                            ins = None
                            for jj in range(4):
                                kc = nq * 4 + jj
                                ins = e.matmul(p[0:m, jj * 128:(jj + 1) * 128], ck.x1(kc)[:, sub * m:(sub + 1) * m], ident,
                                               start=True, stop=True)
                            return ins
                        S.op("pe", f, reads=[ck.T_x1, T_cf], writes=[pk])
                        st, stk = yst[ysi[0] % 3]
                        ysi[0] += 1
                        S.op("act", lambda e, p=p, st=st, rs=rs, m=m: e.activation(out=st[0:m, :], in_=p[0:m, :], func=AF.Identity,
                                                                                   scale=rs[0:m, 0:1]),
                             reads=[pk, rsk], writes=[stk])
                        if ck.sample:
                            S.dma("sp", ys[:, nq * 512:(nq + 1) * 512], st[0:m, :], reads=[stk], owner=stk, final=True)
                        else:
                            r0 = ck.tok0 + sub * 128
                            S.dma("sp", yp[r0:r0 + 128, nq * 512:(nq + 1) * 512], st[:, :], reads=[stk], owner=stk, final=True)
            S.barrier(exclude=(T_cp,))
            AR.release(mt)
            AR.rrelease(mrt)

    except _Stop:
        pass
    drip(10000)
    with nc.Block() as block:
        S.emit(block)
    es.close()
    return nc


_CACHE = {}


def kernel(**inp):
    f32 = lambda a: np.ascontiguousarray(np.asarray(a, dtype=np.float32))
    CF, MW, BS = host_consts()
    vecs = np.concatenate([f32(inp["norm_mix_g"]).reshape(16, 128), f32(inp["norm_mlp_g"]).reshape(16, 128),
                           f32(inp["norm_final_g"]).reshape(16, 128), f32(inp["pool_scale"]).reshape(4, 128)], axis=0)
    shared = {
        "w_ada": f32(inp["w_ada"])[0], "b_ada": f32(inp["b_ada"]).reshape(96, 128), "w_in": f32(inp["w_in"])[0],
        "w_ua": f32(inp["w_up_attn"])[0], "w_pool": f32(inp["w_pool"])[0], "vecs": f32(vecs),
        "w_up": f32(inp["w_up_pool"])[0], "w_out": f32(inp["w_out"])[0], "w_mu": f32(inp["w_mlp_up"])[0],
        "w_md": f32(inp["w_mlp_down"])[0], "cf": CF, "mw": MW, "bs": BS,
    }
    xp = f32(inp["x_prompt"])
    xs = f32(inp["x_sample"])[:, 0, :]
    cp = f32(inp["c_prompt"])
    cs = f32(inp["c_sample"])
    c0 = f32(inp["cache_kv_w128"])[0]
    c1 = f32(inp["cache_kv_w512"])[0]
    c2 = f32(inp["cache_kv_w2048"])[0]
    sp = f32(inp["state_pool"])[0]
    in_maps = []
    for c in range(NCORES):
        sl = slice(c * NS, (c + 1) * NS)
        m = dict(shared)
        m["xp"] = xp[c]
        m["xs"] = xs[sl]
        m["call"] = np.concatenate([cs[sl], cp[c:c + 1]], axis=0)
        m["ckv0"] = c0[sl].reshape(NS, 128, 1024)
        m["ckv1"] = c1[sl].reshape(NS, 512, 1024)
        m["ckv2"] = c2[sl].reshape(NS, 2048, 1024)
        m["spool"] = sp[sl].reshape(NS * 15, 512)
        in_maps.append(m)
    if "nc" not in _CACHE:
        _CACHE["nc"] = build_program()
    res = run_bass_kernel_spmd(_CACHE["nc"], in_maps, core_ids=list(range(NCORES)))
    R = res.results
    cat = lambda k: np.stack([np.asarray(R[c][k], dtype=np.float32) for c in range(NCORES)], axis=0)
    y_p = cat("yp")
    y_s = np.concatenate([np.asarray(R[c]["ys"], np.float32) for c in range(NCORES)], axis=0).reshape(128, 1, D)
    kvp0 = cat("kvp0").reshape(1, 8, 128, 2, 4, 128)
    kvp1 = cat("kvp1").reshape(1, 8, 512, 2, 4, 128)
    kvp2 = cat("kvp2").reshape(1, 8, 2048, 2, 4, 128)
    poolp = cat("poolp").reshape(1, 8, 15, 512)
    ccat = lambda k: np.concatenate([np.asarray(R[c][k], np.float32) for c in range(NCORES)], axis=0)
    kvs0 = ccat("kvs0").reshape(1, 128, 128, 2, 4, 128)
    kvs1 = ccat("kvs1").reshape(1, 128, 512, 2, 4, 128)
    kvs2 = ccat("kvs2").reshape(1, 128, 2048, 2, 4, 128)
    pools = ccat("pools").reshape(1, 128, 15, 512)
    return (y_p, y_s, kvp0, kvp1, kvp2, poolp, kvs0, kvs1, kvs2, pools)
```

```python
import contextlib
import numpy as np
import concourse.bass as bass
import concourse.mybir as mybir
from concourse.bass_utils import run_bass_kernel_spmd

F32 = mybir.dt.float32
BF16 = mybir.dt.bfloat16
AF = mybir.ActivationFunctionType
ALU = mybir.AluOpType
AX = mybir.AxisListType

NCORES = 8
D = 2048
NT = 2048
NS = 16
KC = 16
DFF = 8192
Q0, K0, V0, U0, GA0, GB0 = 0, 1536, 3072, 4608, 5120, 7168
EPS = 1e-6
SCALE = 128.0 ** -0.5
DIL = (1, 4, 16)
NCF = 588

_slopes = 2.0 ** (-8.0 * np.arange(1, 13) / 12.0)
_dilh = np.array([1, 1, 1, 1, 4, 4, 4, 4, 16, 16, 16, 16], np.float64)
_ah = _slopes * _dilh


def host_consts():
    CF = np.zeros((128, NCF), np.float32)
    CF[:, 0:128] = np.eye(128)
    j = np.arange(128)
    CF[:, 128:140] = (j[:, None] - 64) * _ah[None, :]
    for g, w in enumerate((2, 4, 8, 16)):
        pos = np.arange(16)
        CF[:, 140 + g * 16:140 + (g + 1) * 16] = 1.0 / np.minimum(pos + 1, w)
    for g, w in enumerate((2, 4, 8, 16)):
        for half in range(2):
            for bl in range(8):
                for row in range(15 - (w - 1), 15):
                    CF[bl * 15 + row, 204 + (g * 2 + half) * 16 + half * 8 + bl] = 1.0
    CF[:, 332:588] = np.eye(16).reshape(1, 256)
    MW = np.zeros((128, 12, 256), np.float32)
    i = np.arange(128)
    for h in range(12):
        MW[:, h, 0:128] = (j[:, None] <= i[None, :]) * np.exp(-_ah[h] * (i[None, :] - 64))
        MW[:, h, 128:256] = (j[:, None] >= i[None, :]) * np.exp(-_ah[h] * (i[None, :] - 64) - 128 * _ah[h])
    BS = np.zeros((16, 12, 129), np.float32)
    pos = np.arange(129)
    for h in range(12):
        BS[:, h, :] = -_ah[h] * (128 - pos)
    return CF, MW.reshape(128, 12 * 256), BS.reshape(16, 12 * 129)


class Tok:
    __slots__ = ("name", "w", "r", "sem", "dcnt")

    def __init__(self, name):
        self.name = name
        self.w = None
        self.r = {}
        self.sem = None
        self.dcnt = 0


class Sched:
    ENG = ("pe", "act", "dve", "pool", "sp")

    def __init__(self, nc, es):
        self.nc = nc
        self.es = es
        self.ops = {e: [] for e in self.ENG}
        self.cnt = {e: 0 for e in self.ENG}
        self.waited = {e: {} for e in self.ENG}
        self.esem = {e: es.enter_context(nc.semaphore("sem_" + e)) for e in self.ENG if e != "sp"}
        self.nsem = 0
        self.final = {}
        self.owners = []
        self.pending = {e: [] for e in self.ENG}

    def barrier(self, exclude=()):
        targets = {("e", e): self.cnt[e] for e in self.ENG if e != "sp"}
        for o in self.owners:
            if o not in exclude:
                targets[("d", o)] = o.dcnt
        for en in self.ENG:
            wd = self.waited[en]
            for k, v in targets.items():
                if v > 0 and k != ("e", en) and wd.get(k, 0) < v:
                    wd[k] = v
                    self.pending[en].append((k, v))

    def _collect(self, eng, reads, writes):
        need = {}

        def add(key, val):
            if need.get(key, 0) < val:
                need[key] = val

        for t in reads:
            if t.w is not None:
                add(*t.w)
        for t in writes:
            if t.w is not None:
                add(*t.w)
            for k, v in t.r.items():
                add(k, v)
        waits = []
        wd = self.waited[eng]
        for key, val in need.items():
            if key == ("e", "pe") and eng == "pe":
                continue
            if wd.get(key, 0) >= val:
                continue
            wd[key] = val
            waits.append((key, val))
        return waits

    def _commit(self, ev, reads, writes):
        for t in writes:
            t.w = ev
            t.r = {}
        for t in reads:
            if t not in writes:
                if t.r.get(ev[0], 0) < ev[1]:
                    t.r[ev[0]] = ev[1]

    def op(self, eng, fn, reads=(), writes=()):
        waits = self._collect(eng, reads, writes)
        self.cnt[eng] += 1
        ev = (("e", eng), self.cnt[eng])
        self._commit(ev, reads, writes)
        waits = self.pending[eng] + waits
        self.pending[eng] = []
        self.ops[eng].append((waits, fn, ("e", eng)))

    def dma(self, q, out, in_, reads=(), writes=(), owner=None, final=False):
        if owner is None:
            owner = writes[0] if writes else reads[0]
        if owner.sem is None:
            owner.sem = self.es.enter_context(self.nc.semaphore("dsem%d" % self.nsem))
            self.nsem += 1
            self.owners.append(owner)
        waits = self._collect(q, reads, writes)
        owner.dcnt += 16
        ev = (("d", owner), owner.dcnt)
        self._commit(ev, reads, writes)
        if final:
            self.final[ev[0]] = ev[1]
        waits = self.pending[q] + waits
        self.pending[q] = []
        self.ops[q].append((waits, lambda e, o=out, i=in_: e.dma_start(out=o, in_=i), ("d", owner)))

    def _sem(self, key):
        return self.esem[key[1]] if key[0] == "e" else key[1].sem

    def emit(self, block):
        engmap = {"pe": block.tensor, "act": block.scalar, "dve": block.vector,
                  "pool": block.gpsimd, "sp": block.sync}
        for en in self.ENG:
            ops = self.ops[en]
            fin = self.final if en == "sp" else None
            pend = self.pending[en]

            def body(eng, ops=ops, fin=fin, pend=pend):
                for waits, fn, sig in ops:
                    for key, val in waits:
                        eng.wait_ge(self._sem(key), val)
                    ins = fn(eng)
                    if sig[0] == "e":
                        ins.then_inc(self.esem[sig[1]], 1)
                    else:
                        ins.then_inc(sig[1].sem, 16)
                for key, val in pend:
                    eng.wait_ge(self._sem(key), val)
                if fin is not None:
                    for key, val in fin.items():
                        eng.wait_ge(self._sem(key), val)

            engmap[en](body)


class Arena:
    def __init__(self, tensor, nbytes):
        self.t = tensor
        self.cap = nbytes
        self.top = 0
        self.rtop = nbytes

    def mark(self):
        return self.top

    def release(self, m):
        self.top = m

    def rmark(self):
        return self.rtop

    def rrelease(self, m):
        self.rtop = m

    def alloc(self, shape, dtype, name="", side="L"):
        esz = 4 if dtype == F32 else 2
        n = 1
        for s in shape[1:]:
            n *= s
        nb = (n * esz + 31) // 32 * 32
        if side == "L":
            off = self.top
            self.top += nb
        else:
            self.rtop -= nb
            off = self.rtop
        assert self.top <= self.rtop, "SBUF arena overflow at %s: L=%d R=%d" % (name, self.top, self.rtop)
        ap = self.t[0:shape[0], off // 2: off // 2 + n * esz // 2]
        if dtype == F32:
            ap = ap.bitcast(F32)
        if len(shape) == 3:
            ap = ap.rearrange("p (a b) -> p a b", b=shape[2])
        elif len(shape) == 4:
            ap = ap.rearrange("p (a b c) -> p a b c", b=shape[2], c=shape[3])
        return ap


class _Stop(Exception):
    pass


import os as _os2
SKIP = int(_os2.environ.get('SKIP', '0'))


def build_program(stage=99):
    nc = bass.Bass("TRN2", target_bir_lowering=False)
    es = contextlib.ExitStack()

    def din(name, shape):
        return nc.dram_tensor(name, list(shape), F32, kind="ExternalInput").ap()

    def dout(name, shape):
        return nc.dram_tensor(name, list(shape), F32, kind="ExternalOutput").ap()

    xp = din("xp", (NT, D))
    xs = din("xs", (NS, D))
    call = din("call", (17, D))
    ckv = [din("ckv0", (NS, 128, 1024)), din("ckv1", (NS, 512, 1024)), din("ckv2", (NS, 2048, 1024))]
    spool = din("spool", (NS * 15, 512))
    w_ada = din("w_ada", (D, 6 * D))
    b_ada = din("b_ada", (96, 128))
    w_in = din("w_in", (D, 9216))
    w_ua = din("w_ua", (512, D))
    w_pool = din("w_pool", (4, 128, 128))
    vecs = din("vecs", (52, 128))
    w_up = din("w_up", (512, D))
    w_out = din("w_out", (D, D))
    w_mu = din("w_mu", (D, DFF))
    w_md = din("w_md", (DFF, D))
    cfd = din("cf", (128, NCF))
    mwd = din("mw", (128, 12 * 256))
    bsd = din("bs", (16, 12 * 129))

    yp = dout("yp", (NT, D))
    ys = dout("ys", (NS, D))
    kvp = [dout("kvp0", (128, 1024)), dout("kvp1", (512, 1024)), dout("kvp2", (2048, 1024))]
    poolp = dout("poolp", (15, 512))
    kvs = [dout("kvs0", (NS, 128, 1024)), dout("kvs1", (NS, 512, 1024)), dout("kvs2", (NS, 2048, 1024))]
    pools = dout("pools", (NS, 15, 512))
    qscr = nc.dram_tensor("qscr", [NS, 1536], F32, kind="Internal").ap()

    ARENA_BYTES = 207 * 1024
    arena_t = es.enter_context(nc.sbuf_tensor("arena", [128, ARENA_BYTES // 2], BF16))
    AR = Arena(arena_t, ARENA_BYTES)
    S = Sched(nc, es)
    psb = []
    for i in range(8):
        t = es.enter_context(nc.psum_tensor("ps%d" % i, [128, 512], F32))
        psb.append((t, Tok("ps%d" % i)))
    pstate = [0]

    def psn():
        t = psb[pstate[0] % 6]
        pstate[0] += 1
        return t

    PSD = psb[6]
    PSE = psb[7]

    cf = AR.alloc([128, NCF], F32, "cf")
    T_cf = Tok("cf")
    ident = cf[:, 0:128]
    cbias = cf[:, 128:140]
    invc = cf[:, 140:204]
    selw = cf[:, 204:332]
    i16 = cf[:, 332:588]
    identb = AR.alloc([128, 128], BF16, "identb")
    onesb = AR.alloc([128, 128], BF16, "onesb")
    T_cb = Tok("constb")
    vT1 = AR.alloc([128, 96], F32, "vT1")
    vT2 = AR.alloc([128, 52], F32, "vT2")
    T_vT = Tok("vT")
    modT = AR.alloc([128, 96, 17], F32, "modT")
    T_mod = Tok("modT")
    A1 = AR.alloc([128, 16, 17], F32, "A1")
    A2 = AR.alloc([128, 16, 17], F32, "A2")
    T_A = Tok("A")
    wpool = AR.alloc([128, 4, 128], BF16, "wpool")
    T_wpool = Tok("wpool")
    oaT = AR.alloc([128, 4, NT], BF16, "oaT")
    T_oa = [Tok("oa%d" % s) for s in range(4)]
    oaTs = AR.alloc([128, 4, NS], BF16, "oaTs")
    T_oas = Tok("oaTs")
    pTs = AR.alloc([128, 4, NS], BF16, "pTs")
    T_pTs = Tok("pTs")
    x1Ts = AR.alloc([128, 16, NS], F32, "x1Ts")
    T_x1s = Tok("x1Ts")
    hTs = AR.alloc([128, 16, NS], BF16, "hTs")
    T_hs = Tok("hTs")
    mixTs = AR.alloc([128, 16, NS], BF16, "mixTs")
    T_mixs = Tok("mixTs")
    smallf = AR.alloc([128, 64], F32, "smallf")
    T_small = Tok("smallf")
    WSLOT = 8 * 1024
    wsl = [(AR.alloc([128, WSLOT // 2], BF16, "ws%d" % i), Tok("ws%d" % i)) for i in range(3)]
    wstate = [0]
    tmpf = [(AR.alloc([128, 512], F32, "tmpf%d" % i), Tok("tmpf%d" % i)) for i in range(3)]
    tstate = [0]

    def tmpn():
        t = tmpf[tstate[0] % 3]
        tstate[0] += 1
        return t

    tmpb = [(AR.alloc([128, 512], BF16, "tmpb%d" % i), Tok("tmpb%d" % i)) for i in range(3)]
    bstate = [0]

    def tmpbn():
        t = tmpb[bstate[0] % 3]
        bstate[0] += 1
        return t

    bhsm = [(AR.alloc([128, 160], F32, "bhsm%d" % i), Tok("bhsm%d" % i)) for i in range(4)]
    bhstate = [0]
    qkvs = AR.alloc([128, 40, NS], F32, "qkvs")
    T_qkvs = Tok("qkvs")

    T_cp = Tok("cpy")
    cplist = []
    for g, W in enumerate((128, 512, 2048)):
        for b in range(NS):
            nrow = W - 1
            r = 0
            while r < nrow:
                n = min(512, nrow - r)
                cplist.append((kvs[g][b, r:r + n, :], ckv[g][b, r + 1:r + 1 + n, :]))
                r += n
    spool3 = spool.rearrange("(b r) c -> b r c", r=15)
    for b in range(NS):
        cplist.append((pools[b, 0:14, :], spool3[b, 1:15, :]))
    cpstate = [0]
    drip_on = [0]

    def drip(n=1):
        for _ in range(n):
            if cpstate[0] < len(cplist):
                o, i_ = cplist[cpstate[0]]
                cpstate[0] += 1
                S.dma("act", o, i_, owner=T_cp, final=True)

    def run_jobs(jobs):
        n = len(jobs)
        base = wstate[0]

        def issue(k):
            sl, tok = wsl[(base + k) % 3]
            for o, i_ in jobs[k][0](sl):
                S.dma("pool", o, i_, writes=[tok])

        for k in range(min(2, n)):
            issue(k)
        for k in range(n):
            if k + 2 < n:
                issue(k + 2)
            sl, tok = wsl[(base + k) % 3]
            if drip_on[0]:
                drip(drip_on[0])
            jobs[k][1](sl, tok)
        wstate[0] = (base + n) % 3

    def wview(sl, kk, cols):
        return sl[:, 0:kk * cols].rearrange("p (k c) -> p k c", c=cols)

    def rows_view(w, r0, nk, c0, cols):
        return w[r0:r0 + nk * 128, c0:c0 + cols].rearrange("(k p) c -> p k c", p=128)

    try:
        S.dma("sp", cf, cfd, writes=[T_cf])
        S.dma("pool", wpool, w_pool.rearrange("g c d -> c g d"), writes=[T_wpool])
        S.op("dve", lambda e: e.tensor_copy(out=identb, in_=ident), reads=[T_cf], writes=[T_cb])
        S.op("dve", lambda e: e.memset(onesb, 1.0), writes=[T_cb])

        m0 = AR.mark()
        call_sb = AR.alloc([17, D], F32, "call_sb")
        T_call = Tok("call")
        sT = AR.alloc([128, 16, 17], BF16, "sT")
        T_sT = Tok("sT")
        v1 = AR.alloc([96, 128], F32, "v1")
        v2 = AR.alloc([52, 128], F32, "v2")
        T_v = Tok("v12")
        S.dma("sp", call_sb, call, writes=[T_call])
        S.dma("sp", v1, b_ada, writes=[T_v])
        S.dma("sp", v2, vecs, writes=[T_v])
        S.op("act", lambda e: e.activation(out=call_sb, in_=call_sb, func=AF.Silu), writes=[T_call])
        pt, ptk = psn()

        def f(e):
            ins = None
            for kc in range(16):
                ins = e.matmul(pt[:, kc * 17:(kc + 1) * 17], call_sb[0:17, kc * 128:(kc + 1) * 128],
                               ident[0:17, 0:17], start=True, stop=True)
            return ins
        S.op("pe", f, reads=[T_call, T_cf], writes=[ptk])
        S.op("dve", lambda e: e.tensor_copy(out=sT, in_=pt[:, 0:272].rearrange("p (k c) -> p k c", c=17)),
             reads=[ptk], writes=[T_sT])
        pt2, ptk2 = psn()

        def f(e):
            e.matmul(pt2[:, 0:96], v1[0:96, :], ident[0:96, 0:96], start=True, stop=True)
            return e.matmul(pt2[:, 96:148], v2[0:52, :], ident[0:52, 0:52], start=True, stop=True)
        S.op("pe", f, reads=[T_v, T_cf], writes=[ptk2])

        def f(e):
            e.tensor_copy(out=vT1, in_=pt2[:, 0:96])
            return e.tensor_copy(out=vT2, in_=pt2[:, 96:148])
        S.op("dve", f, reads=[ptk2], writes=[T_vT])
        g1T = vT2[:, 0:16]
        g2T = vT2[:, 16:32]
        gfT = vT2[:, 32:48]
        pscT = vT2[:, 48:52]

        jobs = []
        for j in range(48):
            def loads(sl, j=j):
                return [(wview(sl, 16, 256), rows_view(w_ada, 0, 16, j * 256, 256))]

            def comp(sl, tok, j=j):
                wv = wview(sl, 16, 256)
                p, pk = psn()

                def f(e):
                    ins = None
                    for cc in range(2):
                        for kc in range(16):
                            ins = e.matmul(p[:, cc * 17:(cc + 1) * 17], wv[:, kc, cc * 128:(cc + 1) * 128],
                                           sT[:, kc, :], start=(kc == 0), stop=(kc == 15))
                    return ins
                S.op("pe", f, reads=[tok, T_sT], writes=[pk])

                def f2(e):
                    ins = None
                    for cc in range(2):
                        c = j * 2 + cc
                        ins = e.tensor_scalar(out=modT[:, c, :], in0=p[:, cc * 17:(cc + 1) * 17],
                                              scalar1=vT1[:, c:c + 1], scalar2=None, op0=ALU.add)
                    return ins
                S.op("dve", f2, reads=[pk, T_vT], writes=[T_mod])
            jobs.append((loads, comp))
        run_jobs(jobs)

        def f(e):
            ins = None
            for kc in range(16):
                e.tensor_scalar(out=A1[:, kc, :], in0=modT[:, 16 + kc, :], scalar1=1.0, scalar2=g1T[:, kc:kc + 1],
                                op0=ALU.add, op1=ALU.mult)
                ins = e.tensor_scalar(out=A2[:, kc, :], in0=modT[:, 64 + kc, :], scalar1=1.0, scalar2=g2T[:, kc:kc + 1],
                                      op0=ALU.add, op1=ALU.mult)
            return ins
        S.op("dve", f, reads=[T_mod, T_vT], writes=[T_A])
        S.barrier(exclude=(T_cp,))
        AR.release(m0)
        if stage <= 0:
            raise _Stop()
        SH1, GT1, SH2, GT2 = 0, 32, 48, 80

        def rstd_from_sum(out_ap, in_ap, toks_r, toks_w):
            S.op("dve", lambda e: e.tensor_scalar(out=out_ap, in0=in_ap, scalar1=1.0 / D, scalar2=EPS,
                                                  op0=ALU.mult, op1=ALU.add), reads=toks_r, writes=toks_w)
            S.op("act", lambda e: e.activation(out=out_ap, in_=out_ap, func=AF.Sqrt), writes=toks_w)
            S.op("dve", lambda e: e.reciprocal(out=out_ap, in_=out_ap), writes=toks_w)

        def build_h(xst, xrows, ntok, h_dst=None, T_h=None, x_dst=None, T_x=None, sample=False):
            xt, xtk = xst[0][xst[1][0] % len(xst[0])]
            xst[1][0] += 1
            xv = xt[0:ntok, :]
            S.dma("sp", xv, xrows, writes=[xtk])
            sm, smk = bhsm[bhstate[0] % 4]
            bhstate[0] += 1
            if h_dst is not None:
                jb, jbk = tmpbn()
                ss4 = sm[0:ntok, 0:4]
                ss = sm[0:ntok, 8:9]
                dm = sm[0:ntok, 16:16 + ntok]

                S.op("dve", lambda e: e.memzero(ss4), writes=[smk])
                jb2, jbk2 = tmpbn()
                jbs = [jb[0:ntok, :], jb2[0:ntok, :]]

                def f(e):
                    ins = None
                    for q in range(4):
                        ins = e.activation(out=jbs[q % 2], in_=xv[:, q * 512:(q + 1) * 512],
                                           func=AF.Square, accum_out=ss4[:, q:q + 1])
                    return ins
                S.op("act", f, reads=[xtk], writes=[jbk, jbk2, smk])
                S.op("dve", lambda e: e.tensor_reduce(out=ss, in_=ss4, axis=AX.X, op=ALU.add), writes=[smk])
                rstd_from_sum(ss, ss, [smk], [smk])
                S.op("dve", lambda e: e.tensor_scalar(out=dm, in0=ident[0:ntok, 0:ntok], scalar1=ss, scalar2=None,
                                                      op0=ALU.mult), reads=[T_cf], writes=[smk])
            for kq in range(4):
                if x_dst is not None:
                    p2, pk2 = psn()

                    def f(e, kq=kq, p2=p2):
                        ins = None
                        for jj in range(4):
                            kc = kq * 4 + jj
                            ins = e.matmul(p2[:, jj * ntok:(jj + 1) * ntok], xv[:, kc * 128:(kc + 1) * 128],
                                           ident[0:ntok, 0:ntok], start=True, stop=True)
                        return ins
                    S.op("pe", f, reads=[xtk, T_cf], writes=[pk2])
                    S.op("act", lambda e, kq=kq, p2=p2: e.activation(
                        out=x_dst(kq), in_=p2[:, 0:4 * ntok].rearrange("p (a b) -> p a b", b=ntok), func=AF.Copy),
                        reads=[pk2], writes=[T_x])
                if h_dst is None:
                    continue
                p, pk = psn()

                def f(e, kq=kq, p=p):
                    ins = None
                    for jj in range(4):
                        kc = kq * 4 + jj
                        ins = e.matmul(p[:, jj * ntok:(jj + 1) * ntok], xv[:, kc * 128:(kc + 1) * 128], dm,
                                       start=True, stop=True)
                    return ins
                S.op("pe", f, reads=[xtk, smk], writes=[pk])
                if not sample:
                    if kq % 2 == 0:
                        def f(e, kq=kq, p=p):
                            ins = None
                            for jj in range(4):
                                kc = kq * 4 + jj
                                ins = e.tensor_scalar(out=h_dst(kc), in0=p[:, jj * ntok:(jj + 1) * ntok],
                                                      scalar1=A1[:, kc, 16:17], scalar2=modT[:, SH1 + kc, 16:17],
                                                      op0=ALU.mult, op1=ALU.add)
                            return ins
                        S.op("dve", f, reads=[pk, T_A, T_mod], writes=[T_h])
                    else:
                        def f(e, kq=kq, p=p):
                            ins = None
                            for jj in range(4):
                                kc = kq * 4 + jj
                                ins = e.activation(out=h_dst(kc), in_=p[:, jj * ntok:(jj + 1) * ntok],
                                                   func=AF.Identity, bias=modT[:, SH1 + kc, 16:17],
                                                   scale=A1[:, kc, 16:17])
                            return ins
                        S.op("act", f, reads=[pk, T_A, T_mod], writes=[T_h])
                else:
                    t2, t2k = tmpn()

                    def f(e, kq=kq, p=p, t2=t2):
                        ins = None
                        for jj in range(4):
                            kc = kq * 4 + jj
                            ins = e.tensor_tensor(out=t2[:, jj * 16:(jj + 1) * 16], in0=p[:, jj * ntok:(jj + 1) * ntok],
                                                  in1=A1[:, kc, 0:16], op=ALU.mult)
                        return ins
                    S.op("dve", f, reads=[pk, T_A], writes=[t2k])

                    def f(e, kq=kq, t2=t2):
                        ins = None
                        for jj in range(4):
                            kc = kq * 4 + jj
                            ins = e.tensor_tensor(out=h_dst(kc), in0=t2[:, jj * 16:(jj + 1) * 16],
                                                  in1=modT[:, SH1 + kc, 0:16], op=ALU.add)
                        return ins
                    S.op("dve", f, reads=[t2k, T_mod], writes=[T_h])

        T_xst = [Tok("xst%d" % i) for i in range(4)]
        T_yst = [Tok("yst0"), Tok("yst1"), Tok("yst2")]

        def make_xst(n):
            return ([(AR.alloc([128, D], F32, "xst%d" % i), T_xst[i]) for i in range(n)], [0])

        mr1 = AR.rmark()
        hT = AR.alloc([128, 16, NT], BF16, "hT", side="R")
        T_h = [Tok("hT%d" % c) for c in range(4)]
        m1a = AR.mark()
        xst = make_xst(4)
        build_h(xst, xs, NS, h_dst=lambda kc: hTs[:, kc, :], T_h=T_hs,
                x_dst=lambda kq: x1Ts[:, kq * 4:(kq + 1) * 4, :], T_x=T_x1s, sample=True)
        for sub in range(16):
            build_h(xst, xp[sub * 128:(sub + 1) * 128, :], 128,
                    h_dst=lambda kc, sub=sub: hT[:, kc, sub * 128:(sub + 1) * 128], T_h=T_h[sub // 4])
        S.barrier(exclude=(T_cp,))
        AR.release(m1a)
        if stage <= 1:
            raise _Stop()

        m1d = AR.mark()
        mw = AR.alloc([128, 12, 256], BF16, "mw")
        T_mw = Tok("mw")
        S.dma("pool", mw, mwd.rearrange("p (h c) -> p h c", c=256), writes=[T_mw])
        qT = AR.alloc([128, 3, NT], BF16, "qT")
        kT = AR.alloc([128, 3, NT], BF16, "kT")
        T_q = [Tok("qT%d" % g) for g in range(3)]
        T_k = [Tok("kT%d" % g) for g in range(3)]
        T_vst = Tok("vst")
        Vt = AR.alloc([128, 3, 16, 128], BF16, "Vt")
        T_V = [Tok("V%d" % g) for g in range(3)]
        acc = AR.alloc([128, 2, NT], F32, "acc")
        T_acc = Tok("acc")
        nmx = AR.alloc([128, 2, 3, 4], F32, "nmx")
        T_nmx = Tok("nmx")
        bcol = AR.alloc([128, 8], F32, "bcol")
        T_bcol = Tok("bcol")
        ptr = [(AR.alloc([128, 256], BF16, "ptr%d" % i), Tok("ptr%d" % i)) for i in range(4)]
        ptm = [(AR.alloc([128, 256], BF16, "ptm%d" % i), Tok("ptm%d" % i)) for i in range(5)]
        pti = [0]

        def deint(ap2, d, t0, n):
            if d == 1:
                return ap2[:, t0:t0 + n]
            return ap2.rearrange("p (r u) -> p u r", r=d)[:, t0 // d:(t0 + n) // d, :]

        def nat(ap2, d):
            if d == 1:
                return ap2
            return ap2.rearrange("p (u r) -> p u r", r=d)

        drip_on[0] = 2
        for s in range(4):
            vst = oaT[:, s, :]
            ajobs = []
            for ty in (1, 2, 0):
                for g in range(3):
                    def loads(sl, ty=ty, s=s, g=g):
                        c0 = (Q0, K0, V0)[ty] + g * 512 + s * 128
                        return [(wview(sl, 16, 128), rows_view(w_in, 0, 16, c0, 128))]

                    def comp(sl, tok, ty=ty, s=s, g=g, vst=vst):
                        wv = wview(sl, 16, 128)
                        d = DIL[g]
                        lates = []
                        for c in range(4):
                            p, pk = psn()

                            def f(e, p=p, c=c):
                                ins = None
                                for kc in range(16):
                                    ins = e.matmul(p[:, :], wv[:, kc, :], hT[:, kc, c * 512:(c + 1) * 512],
                                                   start=(kc == 0), stop=(kc == 15))
                                return ins
                            S.op("pe", f, reads=[tok, T_h[c]], writes=[pk])
                            if ty == 2:
                                dk = T_vst
                                dv = deint(vst, d, c * 512, 512)
                            else:
                                buf = qT if ty == 0 else kT
                                dk = (T_q if ty == 0 else T_k)[g]
                                dv = deint(buf[:, g, :], d, c * 512, 512)
                            if not (SKIP & 1):
                                S.op("dve", lambda e, p=p, dv=dv, d=d: e.tensor_copy(out=dv, in_=nat(p[:, :], d)),
                                     reads=[pk], writes=[dk])
                            if ty != 2 and not (SKIP & 2):
                                sq, sqk = tmpbn()
                                S.op("act", lambda e, p=p, sq=sq: e.activation(out=sq[:, :], in_=p[:, :], func=AF.Square),
                                     reads=[pk, dk], writes=[sqk])

                                def late(sq=sq, sqk=sqk, c=c):
                                    p2, pk2 = psn()
                                    S.op("pe", lambda e: e.matmul(p2[:, :], onesb, sq[:, :], start=True, stop=True),
                                         reads=[sqk, T_cb], writes=[pk2])
                                    S.op("dve", lambda e: e.tensor_reduce(
                                        out=nmx[:, ty, g, c:c + 1], in_=p2[:, :], axis=AX.X, op=ALU.max),
                                        reads=[pk2], writes=[T_nmx])
                                if lates:
                                    lates.pop(0)()
                                lates.append(late)
                        while lates:
                            lates.pop(0)()
                        if ty == 2:
                            for bq in range(4):
                                p, pk = psn()
                                pb = p[:, :].bitcast(BF16)

                                def f(e, pb=pb, bq=bq):
                                    ins = None
                                    for jj in range(4):
                                        blk = bq * 4 + jj
                                        ins = e.transpose(pb[:, jj * 128:(jj + 1) * 128], vst[:, blk * 128:(blk + 1) * 128], identb)
                                    return ins
                                S.op("pe", f, reads=[T_vst, T_cb], writes=[pk])
                                S.op("act", lambda e, pb=pb, bq=bq: e.activation(
                                    out=Vt[:, g, bq * 4:(bq + 1) * 4, :],
                                    in_=pb[:, 0:512].rearrange("p (a b) -> p a b", b=128), func=AF.Copy),
                                    reads=[pk], writes=[T_V[g]])
                        if SKIP & 4:
                            return
                        p, pk = psn()

                        def f(e, p=p):
                            ins = None
                            for kc in range(16):
                                ins = e.matmul(p[:, 0:16], wv[:, kc, :], hTs[:, kc, :], start=(kc == 0), stop=(kc == 15))
                            return ins
                        S.op("pe", f, reads=[tok, T_hs], writes=[pk])
                        S.op("act", lambda e, p=p: e.activation(out=qkvs[:, ty * 12 + g * 4 + s, :], in_=p[:, 0:16], func=AF.Copy),
                             reads=[pk], writes=[T_qkvs])
                    ajobs.append((loads, comp))
            import os as _os
            if stage <= 1.2:
                ajobs = ajobs[:int(_os.environ.get('NJOBS', '9'))]
            run_jobs(ajobs)
            if stage <= 1.2:
                raise _Stop()

            S.op("dve", lambda e: e.tensor_reduce(out=smallf[:, 8:14].rearrange("p (a b) -> p a b", b=3),
                                                  in_=nmx[:, 0:2, :, :], axis=AX.X, op=ALU.max),
                 reads=[T_nmx], writes=[T_small])
            S.op("dve", lambda e: e.tensor_tensor(out=smallf[:, 16:19], in0=smallf[:, 8:11], in1=smallf[:, 11:14],
                                                  op=ALU.mult), writes=[T_small])
            S.op("act", lambda e: e.activation(out=smallf[:, 16:19], in_=smallf[:, 16:19], func=AF.Sqrt), writes=[T_small])

            S.op("dve", lambda e: e.tensor_reduce(out=smallf[:, 20:21], in_=smallf[:, 16:19], axis=AX.X, op=ALU.max),
                 writes=[T_small])

            def f(e, s=s):
                ins = None
                for g in range(3):
                    h = g * 4 + s
                    ins = e.scalar_tensor_tensor(out=bcol[:, g:g + 1], in0=smallf[:, 20:21], scalar=-1.02 * SCALE,
                                                 in1=cbias[:, h:h + 1], op0=ALU.mult, op1=ALU.add)
                return ins
            S.op("dve", f, reads=[T_cf], writes=[T_small, T_bcol])
            if stage <= 1.4:
                raise _Stop()

            units = []
            for g in range(3):
                d = DIL[g]
                nb = 16 // d
                for r in range(d):
                    for j in range(nb):
                        units.append((g, d, r, j, r * nb + j))

            def stage_a(u, s=s):
                g, d, r, j, blk = u
                h = g * 4 + s
                wd = 256 if j > 0 else 128
                pS, pSk = psn()

                def f(e):
                    ins = e.matmul(pS[:, 0:128], kT[:, g, blk * 128:(blk + 1) * 128], qT[:, g, blk * 128:(blk + 1) * 128],
                                   start=True, stop=True)
                    if j > 0:
                        ins = e.matmul(pS[:, 128:256], kT[:, g, (blk - 1) * 128:blk * 128],
                                       qT[:, g, blk * 128:(blk + 1) * 128], start=True, stop=True)
                    return ins
                S.op("pe", f, reads=[T_k[g], T_q[g]], writes=[pSk])
                pr, prk = ptr[pti[0] % len(ptr)]
                pm, pmk = ptm[pti[0] % len(ptm)]
                pti[0] += 1
                S.op("act", lambda e: e.activation(out=pr[:, 0:wd], in_=pS[:, 0:wd], func=AF.Exp, bias=bcol[:, g:g + 1],
                                                   scale=SCALE), reads=[pSk, T_bcol], writes=[prk])
                S.op("dve", lambda e: e.tensor_tensor(out=pm[:, 0:wd], in0=pr[:, 0:wd], in1=mw[:, h, 0:wd], op=ALU.mult),
                     reads=[prk, T_mw], writes=[pmk])
                return pm, pmk

            def stage_b(u, pm, pmk):
                g, d, r, j, blk = u
                pN, pNk = psn()

                def f(e):
                    e.matmul(pN[:, 0:128], Vt[:, g, blk, :], pm[:, 0:128], start=True, stop=(j == 0))
                    if j > 0:
                        e.matmul(pN[:, 0:128], Vt[:, g, blk - 1, :], pm[:, 128:256], start=False, stop=True)
                    ins = e.matmul(pN[:, 128:256], onesb, pm[:, 0:128], start=True, stop=(j == 0))
                    if j > 0:
                        ins = e.matmul(pN[:, 128:256], onesb, pm[:, 128:256], start=False, stop=True)
                    return ins
                S.op("pe", f, reads=[T_V[g], pmk, T_cb], writes=[pNk])
                if d == 1:
                    av = acc[:, :, j * 128:(j + 1) * 128]
                else:
                    av = acc.rearrange("p c (u r) -> p c u r", r=d)[:, :, j * 128:(j + 1) * 128, r]
                pv = pN[:, 0:256].rearrange("p (c q) -> p c q", q=128)
                if g == 0:
                    S.op("act", lambda e: e.activation(out=av, in_=pv, func=AF.Copy), reads=[pNk], writes=[T_acc])
                else:
                    S.op("dve", lambda e: e.tensor_tensor(out=av, in0=av, in1=pv, op=ALU.add), reads=[pNk], writes=[T_acc])

            pend = []
            for u in units:
                pend.append((u, stage_a(u)))
                if len(pend) > 2:
                    u0, st = pend.pop(0)
                    stage_b(u0, *st)
            for u0, st in pend:
                stage_b(u0, *st)
            if stage <= 1.6:
                raise _Stop()
            for c in range(4):
                sl_ = slice(c * 512, (c + 1) * 512)

                S.op("dve", lambda e, sl_=sl_: e.reciprocal(out=acc[:, 1, sl_], in_=acc[:, 1, sl_]),
                     reads=[T_vst], writes=[T_acc, T_vst])
                S.op("dve", lambda e, sl_=sl_, s=s: e.tensor_tensor(out=oaT[:, s, sl_], in0=acc[:, 0, sl_], in1=acc[:, 1, sl_],
                                                                    op=ALU.mult),
                     writes=[T_acc, T_oa[s], T_vst])
        S.barrier(exclude=(T_cp,))
        AR.release(m1d)
        if stage <= 2:
            raise _Stop()

        pT = AR.alloc([128, 4, NT], BF16, "pT")
        T_pT = [Tok("pT%d" % g) for g in range(4)]
        m1b = AR.mark()
        ubuf = AR.alloc([128, 16 + NT], F32, "ubuf")
        T_u = Tok("ubuf")
        sa = AR.alloc([128, 16 + NT], F32, "sa")
        sbb = AR.alloc([128, 16 + NT], F32, "sbb")
        T_sa = Tok("sa")
        T_sb = Tok("sb")
        zb = AR.alloc([128, NT], BF16, "zb")
        T_z = Tok("zb")
        zs = AR.alloc([128, 4, NS], BF16, "zs")
        T_zs = Tok("zs")
        shist = AR.alloc([128, 4, NS], F32, "shist")
        T_sh = Tok("shist")
        sp_sb = [AR.alloc([120, 512], F32, "sp_sb%d" % h) for h in range(2)]
        T_sp = Tok("sp_sb")
        utok = AR.alloc([16, 512], F32, "utok")
        T_utok = Tok("utok")
        utoks = AR.alloc([16, 512], F32, "utoks")
        T_utoks = Tok("utoks")
        for h in range(2):
            S.dma("sp", sp_sb[h], spool[h * 120:(h + 1) * 120, :], writes=[T_sp])

        def f(e):
            e.memzero(ubuf[:, 0:16])
            e.memzero(sa[:, 0:16])
            return e.memzero(sbb[:, 0:16])
        S.op("dve", f, writes=[T_u, T_sa, T_sb])
        p, pk = psn()

        def f(e, p=p):
            ins = None
            for g in range(4):
                for h in range(2):
                    ins = e.matmul(p[:, g * 16:(g + 1) * 16], sp_sb[h][0:120, g * 128:(g + 1) * 128],
                                   selw[0:120, (g * 2 + h) * 16:(g * 2 + h + 1) * 16], start=(h == 0), stop=(h == 1))
            return ins
        S.op("pe", f, reads=[T_sp, T_cf], writes=[pk])
        S.op("dve", lambda e, p=p: e.tensor_copy(out=shist, in_=p[:, 0:64].rearrange("p (g b) -> p g b", b=16)),
             reads=[pk], writes=[T_sh])
        pu_tok, pu_tokk = PSD
        pus_tok, pus_tokk = PSE

        ujobs = []
        for jh in range(2):
            def loads(sl, jh=jh):
                return [(wview(sl, 16, 256), rows_view(w_in, 0, 16, U0 + jh * 256, 256))]

            def comp(sl, tok, jh=jh):
                wv = wview(sl, 16, 256)
                for gg in range(2):
                    g = jh * 2 + gg
                    w = 2 << g
                    for c in range(4):
                        p, pk = psn()

                        def f(e, p=p, c=c, gg=gg):
                            ins = None
                            for kc in range(16):
                                ins = e.matmul(p[:, :], wv[:, kc, gg * 128:(gg + 1) * 128], hT[:, kc, c * 512:(c + 1) * 512],
                                               start=(kc == 0), stop=(kc == 15))
                            return ins
                        S.op("pe", f, reads=[tok, T_h[c]], writes=[pk])
                        S.op("act", lambda e, p=p, c=c: e.activation(out=ubuf[:, 16 + c * 512:16 + (c + 1) * 512],
                                                                     in_=p[:, :], func=AF.Copy),
                             reads=[pk], writes=[T_u])
                    p, pk = psn()

                    def f(e, p=p, gg=gg):
                        ins = None
                        for kc in range(16):
                            ins = e.matmul(p[:, 0:16], wv[:, kc, gg * 128:(gg + 1) * 128], hTs[:, kc, :],
                                           start=(kc == 0), stop=(kc == 15))
                        return ins
                    S.op("pe", f, reads=[tok, T_hs], writes=[pk])
                    S.op("act", lambda e, p=p, g=g: e.activation(out=qkvs[:, 36 + g, :], in_=p[:, 0:16], func=AF.Copy),
                         reads=[pk], writes=[T_qkvs])
                    S.op("pe", lambda e, g=g: e.matmul(pu_tok[0:16, g * 128:(g + 1) * 128], ubuf[:, NT:NT + 16], ident,
                                                       start=True, stop=True), reads=[T_u, T_cf], writes=[pu_tokk])
                    S.op("pe", lambda e, g=g: e.matmul(pus_tok[0:16, g * 128:(g + 1) * 128], qkvs[:, 36 + g, :], ident,
                                                       start=True, stop=True), reads=[T_qkvs, T_cf], writes=[pus_tokk])
                    src, srck = ubuf, T_u
                    bufs = [(sa, T_sa), (sbb, T_sb)]
                    sh = 1
                    for step in range(g + 1):
                        dst, dstk = bufs[step % 2]
                        S.op("dve", lambda e, src=src, dst=dst, sh=sh: e.tensor_tensor(
                            out=dst[:, 16:16 + NT], in0=src[:, 16:16 + NT], in1=src[:, 16 - sh:16 - sh + NT], op=ALU.add),
                            reads=[srck], writes=[dstk])
                        src, srck = dst, dstk
                        sh *= 2
                    S.op("dve", lambda e, src=src, w=w: e.scalar_tensor_tensor(
                        out=zb[:, :], in0=src[:, 16:16 + NT], scalar=1.0 / w, in1=ubuf[:, 16:16 + NT],
                        op0=ALU.mult, op1=ALU.subtract), reads=[srck, T_u], writes=[T_z])
                    tt, ttk = tmpn()

                    S.op("dve", lambda e, src=src, g=g, tt=tt: e.tensor_tensor(
                        out=tt[:, 0:16], in0=src[:, 16:32], in1=invc[:, g * 16:(g + 1) * 16], op=ALU.mult),
                        reads=[srck, T_cf], writes=[ttk])
                    S.op("dve", lambda e, tt=tt: e.tensor_tensor(out=zb[:, 0:16], in0=tt[:, 0:16], in1=ubuf[:, 16:32],
                                                                 op=ALU.subtract), reads=[ttk, T_u], writes=[T_z])
                    for c in range(4):
                        p, pk = psn()
                        S.op("pe", lambda e, p=p, c=c, g=g: e.matmul(p[:, :], wpool[:, g, :], zb[:, c * 512:(c + 1) * 512],
                                                                     start=True, stop=True),
                             reads=[T_wpool, T_z], writes=[pk])
                        S.op("act", lambda e, p=p, c=c, g=g: e.activation(
                            out=pT[:, g, c * 512:(c + 1) * 512], in_=p[:, :], func=AF.Identity, scale=pscT[:, g:g + 1]),
                            reads=[pk, T_vT], writes=[T_pT[g]])
                    tt, ttk = tmpn()

                    S.op("dve", lambda e, g=g, tt=tt: e.tensor_tensor(out=tt[:, 0:16], in0=shist[:, g, :],
                                                                      in1=qkvs[:, 36 + g, :], op=ALU.add),
                         reads=[T_sh, T_qkvs], writes=[ttk])
                    S.op("dve", lambda e, g=g, w=w, tt=tt: e.scalar_tensor_tensor(
                        out=zs[:, g, :], in0=tt[:, 0:16], scalar=1.0 / w, in1=qkvs[:, 36 + g, :],
                        op0=ALU.mult, op1=ALU.subtract), reads=[ttk, T_qkvs], writes=[T_zs])
                    p, pk = psn()
                    S.op("pe", lambda e, p=p, g=g: e.matmul(p[:, 0:16], wpool[:, g, :], zs[:, g, :], start=True, stop=True),
                         reads=[T_wpool, T_zs], writes=[pk])
                    S.op("act", lambda e, p=p, g=g: e.activation(out=pTs[:, g, :], in_=p[:, 0:16], func=AF.Identity,
                                                                 scale=pscT[:, g:g + 1]),
                         reads=[pk, T_vT], writes=[T_pTs])
            ujobs.append((loads, comp))
        run_jobs(ujobs)
        S.op("dve", lambda e: e.tensor_copy(out=utok, in_=pu_tok[0:16, :]), reads=[pu_tokk], writes=[T_utok])
        S.op("dve", lambda e: e.tensor_copy(out=utoks, in_=pus_tok[0:16, :]), reads=[pus_tokk], writes=[T_utoks])
        S.dma("sp", poolp, utok[1:16, :], reads=[T_utok], owner=T_utok, final=True)
        S.dma("sp", pools[:, 14, :], utoks, reads=[T_utoks], owner=T_utoks, final=True)
        m1c = AR.mark()
        kvst = [(AR.alloc([128, 256], F32, "kvst%d" % i), Tok("kvst%d" % i)) for i in range(4)]
        kvi = [0]
        kjobs = []
        for g, W in enumerate((128, 512, 2048)):
            for kv in range(2):
                for half in range(2):
                    def loads(sl, g=g, kv=kv, half=half):
                        c0 = (K0 if kv == 0 else V0) + g * 512 + half * 256
                        return [(wview(sl, 16, 256), rows_view(w_in, 0, 16, c0, 256))]

                    def comp(sl, tok, g=g, kv=kv, half=half, W=W):
                        wv = wview(sl, 16, 256)
                        for tt in range(W // 128):
                            t0 = NT - W + tt * 128
                            p, pk = psn()

                            def f(e, p=p, t0=t0):
                                ins = None
                                for kc in range(16):
                                    ins = e.matmul(p[:, 0:256], hT[:, kc, t0:t0 + 128], wv[:, kc, :],
                                                   start=(kc == 0), stop=(kc == 15))
                                return ins
                            S.op("pe", f, reads=[tok, T_h[t0 // 512]], writes=[pk])
                            st, stk = kvst[kvi[0] % 4]
                            kvi[0] += 1
                            S.op("act", lambda e, p=p, st=st: e.activation(out=st[:, :], in_=p[:, 0:256], func=AF.Copy),
                                 reads=[pk], writes=[stk])
                            S.dma("sp", kvp[g][tt * 128:(tt + 1) * 128, kv * 512 + half * 256:kv * 512 + (half + 1) * 256],
                                  st[:, :], reads=[stk], owner=stk, final=True)
                    kjobs.append((loads, comp))
        run_jobs(kjobs)

        S.barrier(exclude=(T_cp,))
        AR.release(m1c)
        AR.release(m1b)
        AR.rrelease(mr1)
        if stage <= 4:
            raise _Stop()

        msa = AR.mark()
        tokq = AR.alloc([16, 3, 1536], F32, "tokq")
        T_tokq = Tok("tokq")
        for ty in range(3):
            for g in range(3):
                p, pk = psn()

                def f(e, p=p, ty=ty, g=g):
                    ins = None
                    for hh in range(4):
                        ins = e.matmul(p[0:16, hh * 128:(hh + 1) * 128], qkvs[:, ty * 12 + g * 4 + hh, :], ident,
                                       start=True, stop=True)
                    return ins
                S.op("pe", f, reads=[T_qkvs, T_cf], writes=[pk])
                S.op("act", lambda e, p=p, ty=ty, g=g: e.activation(out=tokq[:, ty, g * 512:(g + 1) * 512], in_=p[0:16, :],
                                                                    func=AF.Copy), reads=[pk], writes=[T_tokq])
        T_qscr = Tok("qscr")
        S.dma("sp", qscr, tokq[:, 0, :], reads=[T_tokq], writes=[T_qscr])
        for g, W in enumerate((128, 512, 2048)):
            S.dma("sp", kvs[g][:, W - 1, 0:512], tokq[:, 1, g * 512:(g + 1) * 512], reads=[T_tokq], owner=T_tokq, final=True)
            S.dma("sp", kvs[g][:, W - 1, 512:1024], tokq[:, 2, g * 512:(g + 1) * 512], reads=[T_tokq], owner=T_tokq, final=True)
        bs_sb = AR.alloc([16, 12, 129], F32, "bs_sb")
        T_bs = Tok("bs")
        S.dma("sp", bs_sb, bsd.rearrange("p (h c) -> p h c", c=129), writes=[T_bs])
        sT_all = AR.alloc([128, 16, 12], F32, "sT_all")
        T_sTa = Tok("sT_all")
        stok = AR.alloc([16, 12, 129], F32, "stok")
        T_stok = Tok("stok")
        sm16 = AR.alloc([16, 64], F32, "sm16")
        T_sm16 = Tok("sm16")
        prodt = AR.alloc([16, 1536], F32, "prodt")
        T_prodt = Tok("prodt")
        NKB = 4
        kh = [(AR.alloc([128, 512], F32, "kh%d" % i), Tok("kh%d" % i)) for i in range(NKB)]
        qb = [(AR.alloc([128, 512], F32, "qb%d" % i), Tok("qb%d" % i)) for i in range(NKB)]
        vall = AR.alloc([128, 48, 512], BF16, "vall")
        T_vall = Tok("vall")
        for b in range(NS):
            for g in range(3):
                d = DIL[g]
                S.dma("pool", vall[:, b * 3 + g, :], ckv[g][b, :, 512:1024].rearrange("(j x) c -> j x c", x=d)[:, 0, :],
                      writes=[T_vall])
        wT = AR.alloc([128, 12, 16], F32, "wT")
        T_wT = Tok("wT")
        wTm = AR.alloc([128, 12, 16, 16], BF16, "wTm")
        T_wTm = Tok("wTm")
        otok = AR.alloc([16, 512], F32, "otok")
        T_otok = Tok("otok")

        S.op("dve", lambda e: e.tensor_tensor(out=prodt, in0=tokq[:, 0, :], in1=tokq[:, 1, :], op=ALU.mult),
             reads=[T_tokq], writes=[T_prodt])
        S.op("dve", lambda e: e.tensor_reduce(out=stok[:, :, 128:129], in_=prodt.rearrange("p (h x) -> p h x", x=128),
                                              axis=AX.X, op=ALU.add), reads=[T_prodt], writes=[T_stok])
        it = 0
        for b in range(NS):
            for g in range(3):
                d = DIL[g]
                kt, ktk = kh[it % NKB]
                qt, qtk = qb[it % NKB]
                it += 1
                S.dma("sp", kt, ckv[g][b, :, 0:512].rearrange("(j x) c -> j x c", x=d)[:, 0, :], writes=[ktk])
                S.dma("sp", qt, qscr[b:b + 1, g * 512:(g + 1) * 512].partition_broadcast(128), reads=[T_qscr], writes=[qtk],
                      owner=qtk)

                S.op("dve", lambda e, kt=kt, qt=qt: e.tensor_tensor(out=kt, in0=kt, in1=qt, op=ALU.mult),
                     reads=[qtk], writes=[ktk])
                S.op("dve", lambda e, kt=kt, b=b, g=g: e.tensor_reduce(
                    out=sT_all[:, b, g * 4:(g + 1) * 4], in_=kt.rearrange("p (h x) -> p h x", x=128), axis=AX.X, op=ALU.add),
                    reads=[ktk], writes=[T_sTa])
        for g3 in range(3):
            p, pk = psn()

            def f(e, p=p, g3=g3):
                ins = None
                for hh in range(4):
                    ins = e.matmul(p[0:16, hh * 128:(hh + 1) * 128], sT_all[:, :, g3 * 4 + hh], ident, start=True, stop=True)
                return ins
            S.op("pe", f, reads=[T_sTa, T_cf], writes=[pk])
            S.op("dve", lambda e, p=p, g3=g3: e.tensor_copy(
                out=stok[:, g3 * 4:(g3 + 1) * 4, 0:128], in_=p[0:16, :].rearrange("p (h x) -> p h x", x=128)),
                reads=[pk], writes=[T_stok])
        mx = sm16[:, 0:12]
        den = sm16[:, 12:24]
        Mx = sm16[:, 24:28]
        fco = sm16[:, 28:40]
        dtot = sm16[:, 40:44]

        def dv(fn, reads=(), writes=()):
            S.op("dve", fn, reads=list(reads), writes=list(writes))

        g3v = lambda ap: ap.rearrange("p (g s) -> p g s", s=4)
        s3v = lambda ap: ap.rearrange("p (g s) -> p s g", s=4)
        dv(lambda e: e.scalar_tensor_tensor(out=stok, in0=stok, scalar=SCALE, in1=bs_sb, op0=ALU.mult, op1=ALU.add),
           [T_bs], [T_stok])
        dv(lambda e: e.tensor_reduce(out=mx, in_=stok, axis=AX.X, op=ALU.max), [T_stok], [T_sm16])
        dv(lambda e: e.tensor_tensor(out=stok, in0=stok, in1=mx.unsqueeze(2).to_broadcast([16, 12, 129]), op=ALU.subtract),
           [T_sm16], [T_stok])
        S.op("act", lambda e: e.activation(out=stok, in_=stok, func=AF.Exp), writes=[T_stok])
        dv(lambda e: e.tensor_reduce(out=den, in_=stok, axis=AX.X, op=ALU.add), [T_stok], [T_sm16])
        dv(lambda e: e.tensor_reduce(out=Mx, in_=s3v(mx), axis=AX.X, op=ALU.max), [], [T_sm16])
        dv(lambda e: e.tensor_tensor(out=g3v(fco), in0=g3v(mx), in1=Mx.unsqueeze(1).to_broadcast([16, 3, 4]), op=ALU.subtract),
           [], [T_sm16])
        S.op("act", lambda e: e.activation(out=fco, in_=fco, func=AF.Exp), writes=[T_sm16])
        dv(lambda e: e.tensor_tensor(out=den, in0=den, in1=fco, op=ALU.mult), [], [T_sm16])
        dv(lambda e: e.tensor_reduce(out=dtot, in_=s3v(den), axis=AX.X, op=ALU.add), [], [T_sm16])
        dv(lambda e: e.reciprocal(out=dtot, in_=dtot), [], [T_sm16])
        dv(lambda e: e.tensor_tensor(out=g3v(fco), in0=g3v(fco), in1=dtot.unsqueeze(1).to_broadcast([16, 3, 4]), op=ALU.mult),
           [], [T_sm16])
        dv(lambda e: e.tensor_tensor(out=stok, in0=stok, in1=fco.unsqueeze(2).to_broadcast([16, 12, 129]), op=ALU.mult),
           [T_sm16], [T_stok])
        p, pk = psn()

        def f(e, p=p):
            ins = None
            for hh in range(12):
                ins = e.matmul(p[:, hh * 16:(hh + 1) * 16], stok[:, hh, 0:128], ident[0:16, 0:16], start=True, stop=True)
            return ins
        S.op("pe", f, reads=[T_stok, T_cf], writes=[pk])
        S.op("dve", lambda e, p=p: e.tensor_copy(out=wT, in_=p[:, 0:192].rearrange("p (h b) -> p h b", b=16)),
             reads=[pk], writes=[T_wT])

        def f(e):
            ins = None
            for bp in range(16):
                ins = e.tensor_tensor(out=wTm[:, :, bp, :], in0=wT,
                                      in1=i16[:, bp * 16:(bp + 1) * 16].unsqueeze(1).to_broadcast([128, 12, 16]), op=ALU.mult)
            return ins
        S.op("dve", f, reads=[T_wT, T_cf], writes=[T_wTm])
        pO, pOk = PSD
        def f(e):
            ins = None
            for hh in range(4):
                for b in range(NS):
                    for g in range(3):
                        first = (b == 0 and g == 0)
                        last = (b == NS - 1 and g == 2)
                        ins = e.matmul(pO[0:16, hh * 128:(hh + 1) * 128], wTm[:, g * 4 + hh, b, :],
                                       vall[:, b * 3 + g, hh * 128:(hh + 1) * 128], start=first, stop=last)
            return ins
        S.op("pe", f, reads=[T_vall, T_wTm], writes=[pOk])
        dv(lambda e: e.tensor_tensor(out=prodt.rearrange("p (h x) -> p h x", x=128),
                                     in0=tokq[:, 2, :].rearrange("p (h x) -> p h x", x=128),
                                     in1=stok[:, :, 128:129].to_broadcast([16, 12, 128]), op=ALU.mult),
           [T_stok, T_tokq], [T_prodt])
        dv(lambda e: e.tensor_tensor(out=otok, in0=pO[0:16, :], in1=prodt[:, 0:512], op=ALU.add), [pOk, T_prodt], [T_otok])
        dv(lambda e: e.tensor_tensor(out=otok, in0=otok, in1=prodt[:, 512:1024], op=ALU.add), [T_prodt], [T_otok])
        dv(lambda e: e.tensor_tensor(out=otok, in0=otok, in1=prodt[:, 1024:1536], op=ALU.add), [T_prodt], [T_otok])
        p, pk = psn()

        def f(e, p=p):
            ins = None
            for hh in range(4):
                ins = e.matmul(p[:, hh * 16:(hh + 1) * 16], otok[:, hh * 128:(hh + 1) * 128], ident[0:16, 0:16], start=True, stop=True)
            return ins
        S.op("pe", f, reads=[T_otok, T_cf], writes=[pk])
        S.op("dve", lambda e, p=p: e.tensor_copy(out=oaTs, in_=p[:, 0:64].rearrange("p (h b) -> p h b", b=16)),
             reads=[pk], writes=[T_oas])
        S.barrier(exclude=(T_cp,))
        AR.release(msa)
        if stage <= 5:
            raise _Stop()

        class Chunk:
            pass

        drip_on[0] = 1

        def resid_update(ck, p, pk, fc, GT):
            n = ck.n
            if not ck.sample:
                S.op("dve", lambda e: e.scalar_tensor_tensor(out=ck.x1(fc), in0=p[:, 0:n], scalar=modT[:, GT + fc, 16:17],
                                                             in1=ck.x1(fc), op0=ALU.mult, op1=ALU.add),
                     reads=[pk, T_mod], writes=[ck.T_x1])
            else:
                tt, ttk = tmpn()

                S.op("dve", lambda e: e.tensor_tensor(out=tt[:, 0:n], in0=p[:, 0:n], in1=modT[:, GT + fc, 0:16], op=ALU.mult),
                     reads=[pk, T_mod], writes=[ttk])
                S.op("dve", lambda e: e.tensor_tensor(out=ck.x1(fc), in0=ck.x1(fc), in1=tt[:, 0:n], op=ALU.add),
                     reads=[ttk], writes=[ck.T_x1])

        for tile in range(2):
            t0 = tile * 1024
            mt = AR.mark()
            mrt = AR.rmark()
            mixT = AR.alloc([128, 16, 1024], BF16, "mixT")
            T_mix = [Tok("mixT%d_%d" % (tile, c)) for c in range(2)]
            mh = AR.mark()
            hT2 = AR.alloc([128, 16, 1024], BF16, "hT2")
            T_h2 = [Tok("hT2%d_%d" % (tile, c)) for c in range(2)]
            T_x1 = [Tok("x1T%d_%d" % (tile, c)) for c in range(2)]
            T_aT = [Tok("aT%d_%d" % (tile, i)) for i in range(2)]
            T_aTs = [Tok("aTs%d_%d" % (tile, i)) for i in range(2)]
            def mk_chunks(hbuf, x1buf, aTl, aTsl, t0=t0, tile=tile, mixT=mixT, T_mix=T_mix, T_h2=T_h2, T_x1=T_x1,
                          T_aT=T_aT, T_aTs=T_aTs):
                chunks = []
                for c in range(2):
                    ck = Chunk()
                    ck.sample = False
                    ck.n = 512
                    ck.tok0 = t0 + c * 512
                    ck.x1 = lambda kc, c=c: x1buf[:, kc, c * 512:(c + 1) * 512]
                    ck.T_x1 = T_x1[c]
                    ck.h = lambda kc, c=c: hbuf[:, kc, c * 512:(c + 1) * 512]
                    ck.T_h = T_h2[c]
                    ck.mix = lambda kc, c=c: mixT[:, kc, c * 512:(c + 1) * 512]
                    ck.T_mix = T_mix[c]
                    ck.oa = lambda kc, ck=ck: oaT[:, kc, ck.tok0:ck.tok0 + 512]
                    ck.T_oa = T_oa
                    ck.pp = lambda kc, ck=ck: pT[:, kc, ck.tok0:ck.tok0 + 512]
                    ck.T_pp = T_pT
                    ck.a = lambda i, cc, c=c: aTl[i][:, cc, c * 512:(c + 1) * 512]
                    ck.T_a = T_aT
                    chunks.append(ck)
                if tile == 0:
                    ck = Chunk()
                    ck.sample = True
                    ck.n = NS
                    ck.x1 = lambda kc: x1Ts[:, kc, :]
                    ck.T_x1 = T_x1s
                    ck.h = lambda kc: hTs[:, kc, :]
                    ck.T_h = T_hs
                    ck.mix = lambda kc: mixTs[:, kc, :]
                    ck.T_mix = T_mixs
                    ck.oa = lambda kc: oaTs[:, kc, :]
                    ck.T_oa = [T_oas] * 4
                    ck.pp = lambda kc: pTs[:, kc, :]
                    ck.T_pp = [T_pTs] * 4
                    ck.a = lambda i, cc: aTsl[i][:, cc, :]
                    ck.T_a = T_aTs
                    chunks.append(ck)
                return chunks

            chunks = mk_chunks(hT2, None, None, None)

            mx_ = AR.mark()
            xst = make_xst(4)
            for c in range(2):
                for sub in range(4):
                    tk = t0 + c * 512 + sub * 128
                    build_h(xst, xp[tk:tk + 128, :], 128,
                            h_dst=lambda kc, c=c, sub=sub: hT2[:, kc, c * 512 + sub * 128:c * 512 + (sub + 1) * 128],
                            T_h=T_h2[c])
            S.barrier(exclude=(T_cp,))
            AR.release(mx_)

            jobs = []
            for fc in range(16):
                for br in range(2):
                    def loads(sl, fc=fc, br=br):
                        return [(sl[:, 0:2048].rearrange("p (k c) -> p k c", c=128),
                                 rows_view(w_in, 0, 16, (GA0, GB0)[br] + fc * 128, 128)),
                                (sl[:, 2048:2560].rearrange("p (k c) -> p k c", c=128),
                                 rows_view((w_ua, w_up)[br], 0, 4, fc * 128, 128))]

                    def comp(sl, tok, fc=fc, br=br):
                        wg = sl[:, 0:2048].rearrange("p (k c) -> p k c", c=128)
                        wu = sl[:, 2048:2560].rearrange("p (k c) -> p k c", c=128)
                        for ck in chunks:
                            n = ck.n
                            pG, pGk = psn()
                            pU, pUk = psn()

                            def f(e, pG=pG, ck=ck, n=n):
                                ins = None
                                for kc in range(16):
                                    ins = e.matmul(pG[:, 0:n], wg[:, kc, :], ck.h(kc), start=(kc == 0), stop=(kc == 15))
                                return ins
                            S.op("pe", f, reads=[tok, ck.T_h], writes=[pGk])
                            src = ck.oa if br == 0 else ck.pp
                            srck = ck.T_oa if br == 0 else ck.T_pp

                            def f(e, pU=pU, src=src, n=n):
                                ins = None
                                for kc in range(4):
                                    ins = e.matmul(pU[:, 0:n], wu[:, kc, :], src(kc), start=(kc == 0), stop=(kc == 3))
                                return ins
                            S.op("pe", f, reads=[tok] + list(srck), writes=[pUk])
                            sg, sgk = tmpn()
                            S.op("act", lambda e, pG=pG, sg=sg, n=n: e.activation(out=sg[:, 0:n], in_=pG[:, 0:n], func=AF.Sigmoid),
                                 reads=[pGk], writes=[sgk])
                            if br == 0:
                                S.op("dve", lambda e, pU=pU, sg=sg, n=n, ck=ck: e.tensor_tensor(
                                    out=ck.mix(fc), in0=sg[:, 0:n], in1=pU[:, 0:n], op=ALU.mult),
                                    reads=[pUk, sgk], writes=[ck.T_mix])
                            else:
                                S.op("dve", lambda e, pU=pU, sg=sg, n=n: e.tensor_tensor(
                                    out=sg[:, 0:n], in0=sg[:, 0:n], in1=pU[:, 0:n], op=ALU.mult), reads=[pUk], writes=[sgk])
                                S.op("dve", lambda e, sg=sg, n=n, ck=ck: e.tensor_tensor(
                                    out=ck.mix(fc), in0=ck.mix(fc), in1=sg[:, 0:n], op=ALU.add), reads=[sgk], writes=[ck.T_mix])
                    jobs.append((loads, comp))
            run_jobs(jobs)
            S.barrier(exclude=(T_cp,))
            AR.release(mh)

            x1T = AR.alloc([128, 16, 1024], F32, "x1T", side="R")
            chunks = mk_chunks(None, x1T, None, None)
            mx_ = AR.mark()
            xst = make_xst(2)
            for c in range(2):
                for sub in range(4):
                    tk = t0 + c * 512 + sub * 128
                    build_h(xst, xp[tk:tk + 128, :], 128,
                            x_dst=lambda kq, c=c, sub=sub: x1T[:, kq * 4:(kq + 1) * 4, c * 512 + sub * 128:c * 512 + (sub + 1) * 128],
                            T_x=T_x1[c])
            S.barrier(exclude=(T_cp,))
            AR.release(mx_)

            jobs = []
            for jc in range(8):
                def loads(sl, jc=jc):
                    return [(wview(sl, 16, 256), rows_view(w_out, 0, 16, jc * 256, 256))]

                def comp(sl, tok, jc=jc):
                    wv = wview(sl, 16, 256)
                    for ff in range(2):
                        fc = jc * 2 + ff
                        for ck in chunks:
                            n = ck.n
                            p, pk = psn()

                            def f(e, p=p, ck=ck, n=n, ff=ff):
                                ins = None
                                for kc in range(16):
                                    ins = e.matmul(p[:, 0:n], wv[:, kc, ff * 128:(ff + 1) * 128], ck.mix(kc),
                                                   start=(kc == 0), stop=(kc == 15))
                                return ins
                            S.op("pe", f, reads=[tok, ck.T_mix], writes=[pk])
                            resid_update(ck, p, pk, fc, GT1)
                jobs.append((loads, comp))
            run_jobs(jobs)
            S.barrier(exclude=(T_cp,))
            AR.release(mt)

            h2T = AR.alloc([128, 16, 1024], BF16, "h2T")
            aTl = [AR.alloc([128, 2, 1024], BF16, "aT%d" % i) for i in range(2)]
            aTsl = [AR.alloc([128, 2, NS], BF16, "aTs%d" % i) for i in range(2)]
            chunks = mk_chunks(h2T, x1T, aTl, aTsl)
            rstdb = AR.alloc([128, 512], F32, "rstdb")
            T_rstdb = Tok("rstdb%d" % tile)
            yst = [(AR.alloc([128, 512], F32, "yst%d" % i), T_yst[i]) for i in range(3)]
            ysi = [0]

            for ck in chunks:
                n = ck.n
                pS_, pSk_ = psn()
                for kc in range(16):
                    sq, sqk = tmpbn()
                    S.op("act", lambda e, sq=sq, ck=ck, kc=kc, n=n: e.activation(out=sq[:, 0:n], in_=ck.x1(kc), func=AF.Square),
                         reads=[ck.T_x1], writes=[sqk])
                    S.op("pe", lambda e, sq=sq, kc=kc, n=n, pS_=pS_: e.matmul(pS_[:, 0:n], onesb, sq[:, 0:n], start=(kc == 0),
                                                                               stop=(kc == 15)), reads=[sqk, T_cb], writes=[pSk_])
                rstd_from_sum(rstdb[:, 0:n], pS_[:, 0:n], [pSk_], [T_rstdb])
                for kc in range(16):
                    tt, ttk = tmpn()
                    S.op("dve", lambda e, tt=tt, ck=ck, kc=kc, n=n: e.tensor_tensor(out=tt[:, 0:n], in0=ck.x1(kc), in1=rstdb[:, 0:n],
                                                                                    op=ALU.mult),
                         reads=[ck.T_x1, T_rstdb], writes=[ttk])
                    if not ck.sample:
                        S.op("act", lambda e, tt=tt, ck=ck, kc=kc, n=n: e.activation(
                            out=ck.h(kc), in_=tt[:, 0:n], func=AF.Identity, bias=modT[:, SH2 + kc, 16:17], scale=A2[:, kc, 16:17]),
                            reads=[ttk, T_A, T_mod], writes=[ck.T_h])
                    else:
                        S.op("dve", lambda e, tt=tt, kc=kc, n=n: e.tensor_tensor(out=tt[:, 0:n], in0=tt[:, 0:n],
                                                                                 in1=A2[:, kc, 0:16], op=ALU.mult),
                             reads=[T_A], writes=[ttk])
                        S.op("dve", lambda e, tt=tt, ck=ck, kc=kc, n=n: e.tensor_tensor(
                            out=ck.h(kc), in0=tt[:, 0:n], in1=modT[:, SH2 + kc, 0:16], op=ALU.add),
                            reads=[ttk, T_mod], writes=[ck.T_h])

            NG = DFF // 256
            jobs = []

            def up_job(g):
                def loads(sl):
                    return [(wview(sl, 16, 256), rows_view(w_mu, 0, 16, g * 256, 256))]

                def comp(sl, tok):
                    wv = wview(sl, 16, 256)
                    for cc in range(2):
                        for ck in chunks:
                            n = ck.n
                            p, pk = psn()

                            def f(e, p=p, ck=ck, n=n, cc=cc):
                                ins = None
                                for kc in range(16):
                                    ins = e.matmul(p[:, 0:n], wv[:, kc, cc * 128:(cc + 1) * 128], ck.h(kc),
                                                   start=(kc == 0), stop=(kc == 15))
                                return ins
                            S.op("pe", f, reads=[tok, ck.T_h], writes=[pk])
                            rl, rlk = tmpbn()
                            S.op("act", lambda e, p=p, rl=rl, n=n: e.activation(out=rl[:, 0:n], in_=p[:, 0:n], func=AF.Relu),
                                 reads=[pk], writes=[rlk])
                            S.op("dve", lambda e, rl=rl, ck=ck, n=n, cc=cc: e.tensor_tensor(
                                out=ck.a(g % 2, cc), in0=rl[:, 0:n], in1=rl[:, 0:n], op=ALU.mult),
                                reads=[rlk], writes=[ck.T_a[g % 2]])
                return (loads, comp)

            def down_job(g):
                def loads(sl):
                    return [(wview(sl, 2, 2048), rows_view(w_md, g * 256, 2, 0, 2048))]

                def comp(sl, tok):
                    wv = wview(sl, 2, 2048)
                    for fc in range(16):
                        for ck in chunks:
                            n = ck.n
                            p, pk = psn()

                            def f(e, p=p, ck=ck, n=n, fc=fc):
                                ins = None
                                for cc in range(2):
                                    ins = e.matmul(p[:, 0:n], wv[:, cc, fc * 128:(fc + 1) * 128], ck.a(g % 2, cc),
                                                   start=(cc == 0), stop=(cc == 1))
                                return ins
                            S.op("pe", f, reads=[tok, ck.T_a[g % 2]], writes=[pk])
                            resid_update(ck, p, pk, fc, GT2)
                return (loads, comp)

            for g in range(NG):
                jobs.append(up_job(g))
                if g > 0:
                    jobs.append(down_job(g - 1))
            jobs.append(down_job(NG - 1))
            run_jobs(jobs)

            for ck in chunks:
                n = ck.n
                for kc in range(16):
                    S.op("act", lambda e, ck=ck, kc=kc: e.activation(out=ck.h(kc), in_=ck.x1(kc), func=AF.Square),
                         reads=[ck.T_x1], writes=[ck.T_h])
                    S.op("dve", lambda e, ck=ck, kc=kc: e.tensor_scalar(out=ck.x1(kc), in0=ck.x1(kc), scalar1=gfT[:, kc:kc + 1],
                                                                        scalar2=None, op0=ALU.mult),
                         reads=[ck.T_h, T_vT], writes=[ck.T_x1])
                nsub = 1 if ck.sample else 4
                m = NS if ck.sample else 128
                for sub in range(nsub):
                    pS_, pSk_ = psn()

                    def f(e, pS_=pS_, ck=ck, sub=sub, m=m):
                        ins = None
                        for kc in range(16):
                            ins = e.matmul(pS_[0:m, 0:1], ck.h(kc)[:, sub * m:(sub + 1) * m], onesb[:, 0:1],
                                           start=(kc == 0), stop=(kc == 15))
                        return ins
                    S.op("pe", f, reads=[ck.T_h, T_cb], writes=[pSk_])
                    rs, rsk = bhsm[bhstate[0] % 4]
                    bhstate[0] += 1
                    rstd_from_sum(rs[0:m, 0:1], pS_[0:m, 0:1], [pSk_], [rsk])
                    for nq in range(4):
                        p, pk = psn()

                        def f(e, p=p, ck=ck, sub=sub, m=m, nq=nq):
                            ins = None
                            for jj in range(4):
                                kc = nq * 4 + jj
                                ins = e.matmul(p[0:m, jj * 128:(jj + 1) * 128], ck.x1(kc)[:, sub * m:(sub + 1) * m], ident,
                                               start=True, stop=True)
                            return ins
                        S.op("pe", f, reads=[ck.T_x1, T_cf], writes=[pk])
                        st, stk = yst[ysi[0] % 3]
                        ysi[0] += 1
                        S.op("act", lambda e, p=p, st=st, rs=rs, m=m: e.activation(out=st[0:m, :], in_=p[0:m, :], func=AF.Identity,
                                                                                   scale=rs[0:m, 0:1]),
                             reads=[pk, rsk], writes=[stk])
                        if ck.sample:
                            S.dma("sp", ys[:, nq * 512:(nq + 1) * 512], st[0:m, :], reads=[stk], owner=stk, final=True)
                        else:
                            r0 = ck.tok0 + sub * 128
                            S.dma("sp", yp[r0:r0 + 128, nq * 512:(nq + 1) * 512], st[:, :], reads=[stk], owner=stk, final=True)
            S.barrier(exclude=(T_cp,))
            AR.release(mt)
            AR.rrelease(mrt)

    except _Stop:
        pass
    drip(10000)
    with nc.Block() as block:
        S.emit(block)
    es.close()
    return nc


_CACHE = {}


def kernel(**inp):
    f32 = lambda a: np.ascontiguousarray(np.asarray(a, dtype=np.float32))
    CF, MW, BS = host_consts()
    vecs = np.concatenate([f32(inp["norm_mix_g"]).reshape(16, 128), f32(inp["norm_mlp_g"]).reshape(16, 128),
                           f32(inp["norm_final_g"]).reshape(16, 128), f32(inp["pool_scale"]).reshape(4, 128)], axis=0)
    shared = {
        "w_ada": f32(inp["w_ada"])[0], "b_ada": f32(inp["b_ada"]).reshape(96, 128), "w_in": f32(inp["w_in"])[0],
        "w_ua": f32(inp["w_up_attn"])[0], "w_pool": f32(inp["w_pool"])[0], "vecs": f32(vecs),
        "w_up": f32(inp["w_up_pool"])[0], "w_out": f32(inp["w_out"])[0], "w_mu": f32(inp["w_mlp_up"])[0],
        "w_md": f32(inp["w_mlp_down"])[0], "cf": CF, "mw": MW, "bs": BS,
    }
    xp = f32(inp["x_prompt"])
    xs = f32(inp["x_sample"])[:, 0, :]
    cp = f32(inp["c_prompt"])
    cs = f32(inp["c_sample"])
    c0 = f32(inp["cache_kv_w128"])[0]
    c1 = f32(inp["cache_kv_w512"])[0]
    c2 = f32(inp["cache_kv_w2048"])[0]
    sp = f32(inp["state_pool"])[0]
    in_maps = []
    for c in range(NCORES):
        sl = slice(c * NS, (c + 1) * NS)
        m = dict(shared)
        m["xp"] = xp[c]
        m["xs"] = xs[sl]
        m["call"] = np.concatenate([cs[sl], cp[c:c + 1]], axis=0)
        m["ckv0"] = c0[sl].reshape(NS, 128, 1024)
        m["ckv1"] = c1[sl].reshape(NS, 512, 1024)
        m["ckv2"] = c2[sl].reshape(NS, 2048, 1024)
        m["spool"] = sp[sl].reshape(NS * 15, 512)
        in_maps.append(m)
    if "nc" not in _CACHE:
        _CACHE["nc"] = build_program()
    res = run_bass_kernel_spmd(_CACHE["nc"], in_maps, core_ids=list(range(NCORES)))
    R = res.results
    cat = lambda k: np.stack([np.asarray(R[c][k], dtype=np.float32) for c in range(NCORES)], axis=0)
    y_p = cat("yp")
    y_s = np.concatenate([np.asarray(R[c]["ys"], np.float32) for c in range(NCORES)], axis=0).reshape(128, 1, D)
    kvp0 = cat("kvp0").reshape(1, 8, 128, 2, 4, 128)
    kvp1 = cat("kvp1").reshape(1, 8, 512, 2, 4, 128)
    kvp2 = cat("kvp2").reshape(1, 8, 2048, 2, 4, 128)
    poolp = cat("poolp").reshape(1, 8, 15, 512)
    ccat = lambda k: np.concatenate([np.asarray(R[c][k], np.float32) for c in range(NCORES)], axis=0)
    kvs0 = ccat("kvs0").reshape(1, 128, 128, 2, 4, 128)
    kvs1 = ccat("kvs1").reshape(1, 128, 512, 2, 4, 128)
    kvs2 = ccat("kvs2").reshape(1, 128, 2048, 2, 4, 128)
    pools = ccat("pools").reshape(1, 128, 15, 512)
    return (y_p, y_s, kvp0, kvp1, kvp2, poolp, kvs0, kvs1, kvs2, pools)
```
